# Optimizing a Trainium2 kernel written in Bass

```python
import jax, jax.numpy as jnp
from jax import lax
import numpy as np

D_MODEL = 1024
BATCH = 4
SEQ = 4096
DEPTH = 1

CHUNK = 64
N_PREV_CHUNKS = 8
BAND_CHUNKS = N_PREV_CHUNKS + 1
N_HEADS = 16
HEAD_DIM = 64
D_ATTN = N_HEADS * HEAD_DIM
D_CONV = D_MODEL
CONV_WIDTH = 3
MAX_REL = 256
D_FF = 4 * D_MODEL
N_BRANCHES = 2
EPS = 1e-6
NEG_INF = -1e30

kernel_name = "chunk_causal_hybrid_attn_shortconv_block"


def rms_norm(x, g):
    xf = x.astype(jnp.float32)
    y = xf * lax.rsqrt(jnp.mean(xf * xf, axis=-1, keepdims=True) + EPS)
    return (y * g.astype(jnp.float32)).astype(x.dtype)


def chunk_band(t):
    b, nc, c, h, dh = t.shape
    tp = jnp.pad(t, ((0, 0), (N_PREV_CHUNKS, 0), (0, 0), (0, 0), (0, 0)))
    band = jnp.stack([tp[:, o:o + nc] for o in range(BAND_CHUNKS)], axis=2)
    return band.reshape(b, nc, BAND_CHUNKS * c, h, dh)


def chunked_relpos_attention(q, k, v, q_norm_g, k_norm_g, rel_bias):
    b, s, _ = q.shape
    nc = s // CHUNK
    kw = BAND_CHUNKS * CHUNK
    q = rms_norm(q.reshape(b, nc, CHUNK, N_HEADS, HEAD_DIM), q_norm_g)
    k = rms_norm(k.reshape(b, nc, CHUNK, N_HEADS, HEAD_DIM), k_norm_g)
    v = v.reshape(b, nc, CHUNK, N_HEADS, HEAD_DIM)
    kb = chunk_band(k)
    vb = chunk_band(v)

    q_idx = jnp.arange(CHUNK)[:, None]
    k_idx = jnp.arange(kw)[None, :]
    dist = q_idx - k_idx + N_PREV_CHUNKS * CHUNK
    rel_idx = jnp.clip(dist, -MAX_REL, MAX_REL) + MAX_REL
    bias = rel_bias[:, rel_idx].astype(jnp.float32)

    key_chunk = jnp.arange(nc)[:, None] + (jnp.arange(kw) // CHUNK)[None, :] - N_PREV_CHUNKS
    valid = key_chunk >= 0

    scale = HEAD_DIM ** -0.5
    scores = jnp.einsum('bnqhd,bnkhd->bnhqk', q, kb).astype(jnp.float32) * scale
    scores = scores + bias[None, None]
    scores = jnp.where(valid[None, :, None, None, :], scores, NEG_INF)
    probs = jax.nn.softmax(scores, axis=-1).astype(vb.dtype)
    out = jnp.einsum('bnhqk,bnkhd->bnqhd', probs, vb)
    return out.reshape(b, s, D_ATTN)


def gated_short_conv(bg, cg, xc, conv_w, conv_b):
    s = xc.shape[1]
    u = cg * xc
    up = jnp.pad(u, ((0, 0), (CONV_WIDTH - 1, 0), (0, 0)))
    conv = conv_b + sum(conv_w[j] * up[:, j:j + s] for j in range(CONV_WIDTH))
    return bg * conv


def setup_inputs(seed: int = 0) -> dict:
    key = jax.random.key(seed)
    ks = jax.random.split(key, 20)
    f32 = jnp.float32
    d_in = 3 * D_ATTN + 3 * D_CONV
    return {
        "x": jax.random.normal(ks[0], (BATCH, SEQ, D_MODEL), f32),
        "norm1_g": 1.0 + 0.05 * jax.random.normal(ks[1], (D_MODEL,), f32),
        "w_in": jax.random.normal(ks[2], (D_MODEL, d_in), f32) * D_MODEL ** -0.5,
        "q_norm_g": 1.0 + 0.05 * jax.random.normal(ks[3], (HEAD_DIM,), f32),
        "k_norm_g": 1.0 + 0.05 * jax.random.normal(ks[4], (HEAD_DIM,), f32),
        "rel_bias": 0.5 * jax.random.normal(ks[5], (N_HEADS, 2 * MAX_REL + 1), f32),
        "conv_w": jax.random.normal(ks[6], (CONV_WIDTH, D_CONV), f32) * CONV_WIDTH ** -0.5,
        "conv_b": 0.02 * jax.random.normal(ks[7], (D_CONV,), f32),
        "w_attn_proj": jax.random.normal(ks[8], (D_ATTN, D_MODEL), f32) * D_ATTN ** -0.5,
        "w_conv_proj": jax.random.normal(ks[9], (D_CONV, D_MODEL), f32) * D_CONV ** -0.5,
        "w_gate": jax.random.normal(ks[10], (D_MODEL, N_BRANCHES * D_MODEL), f32) * D_MODEL ** -0.5,
        "b_gate": 0.02 * jax.random.normal(ks[11], (N_BRANCHES * D_MODEL,), f32),
        "w_out": jax.random.normal(ks[12], (D_MODEL, D_MODEL), f32) * D_MODEL ** -0.5,
        "norm2_g": 1.0 + 0.05 * jax.random.normal(ks[13], (D_MODEL,), f32),
        "w_up": jax.random.normal(ks[14], (D_MODEL, D_FF), f32) * D_MODEL ** -0.5,
        "w_down": jax.random.normal(ks[15], (D_FF, D_MODEL), f32) * D_FF ** -0.5,
    }


def reference(x, norm1_g, w_in, q_norm_g, k_norm_g, rel_bias, conv_w, conv_b,
              w_attn_proj, w_conv_proj, w_gate, b_gate, w_out, norm2_g, w_up, w_down):
    for _ in range(DEPTH):
        h = rms_norm(x, norm1_g)
        proj = jnp.einsum('bsd,de->bse', h, w_in)
        q, k, v, bg, cg, xc = jnp.split(
            proj,
            [D_ATTN, 2 * D_ATTN, 3 * D_ATTN,
             3 * D_ATTN + D_CONV, 3 * D_ATTN + 2 * D_CONV],
            axis=-1)

        y_attn = chunked_relpos_attention(q, k, v, q_norm_g, k_norm_g, rel_bias)
        y_conv = gated_short_conv(bg, cg, xc, conv_w, conv_b)

        y_attn = jnp.einsum('bse,ed->bsd', y_attn, w_attn_proj)
        y_conv = jnp.einsum('bse,ed->bsd', y_conv, w_conv_proj)

        gates = jax.nn.sigmoid(jnp.einsum('bsd,de->bse', h, w_gate) + b_gate)
        g_attn, g_conv = jnp.split(gates, 2, axis=-1)
        merged = g_attn * y_attn + g_conv * y_conv
        x = x + jnp.einsum('bsd,de->bse', merged, w_out)

        h2 = rms_norm(x, norm2_g)
        u = jnp.square(jax.nn.relu(jnp.einsum('bsd,df->bsf', h2, w_up)))
        x = x + jnp.einsum('bsf,fd->bsd', u, w_down)
    return x
```

```python
import numpy as np
import concourse.bass as bass
import concourse.mybir as mybir
from concourse.bass_utils import run_bass_kernel_spmd

F32 = mybir.dt.float32
BF16 = mybir.dt.bfloat16
U8 = mybir.dt.uint8
AF = mybir.ActivationFunctionType
ALU = mybir.AluOpType

D = 1024
NTOK = 2048
HALO = 512
NLOC = NTOK + HALO
NTT = NLOC // 128
EPS = 1e-6
KB = 1024

DEBUG_TAPS = False
N_RUN = 8
STOP_AFTER = None


class _Op:
    __slots__ = ("eng", "fn", "raw", "oth", "idx", "sig", "sigidx", "slot", "dmawait", "barrier")


class Prog:
    ENGS = ("pe", "act", "dve", "pool", "sp")

    def __init__(self):
        self.ops = []
        self.last_writer = {}
        self.readers = {}
        self.slot_count = {}
        self.last_on_eng = {}

    def add(self, eng, fn, reads=(), writes=(), slot=None):
        op = _Op()
        op.eng, op.fn, op.idx, op.sig, op.slot, op.barrier = eng, fn, len(self.ops), False, slot, False
        op.raw, op.oth, op.dmawait = set(), set(), {}
        for r in reads:
            w = self.last_writer.get(r)
            if w is not None:
                op.raw.add(w)
        for r in writes:
            w = self.last_writer.get(r)
            if w is not None:
                op.oth.add(w)
            for rd in self.readers.get(r, ()):
                op.oth.add(rd)
        for r in reads:
            self.readers.setdefault(r, []).append(op.idx)
        for r in writes:
            self.last_writer[r] = op.idx
            self.readers[r] = []
        for d in list(op.raw | op.oth):
            dop = self.ops[d]
            if dop.slot is not None:
                op.dmawait[dop.slot] = self.slot_count[dop.slot]
        if slot is not None:
            self.slot_count[slot] = self.slot_count.get(slot, 0) + 1
        self.ops.append(op)
        self.last_on_eng[eng] = op.idx
        return op

    def barrier(self):
        lasts = dict(self.last_on_eng)
        slots = dict(self.slot_count)
        for eng in self.ENGS:
            op = _Op()
            op.eng, op.fn, op.idx, op.sig, op.slot, op.barrier = eng, None, len(self.ops), False, None, True
            op.raw = set(v for k, v in lasts.items())
            op.oth = set()
            op.dmawait = dict(slots)
            self.ops.append(op)
        self.last_writer = {}
        self.readers = {}

    def emit(self, nc, engobjs, sems, slot_sems):
        ops = self.ops
        for op in ops:
            for d in op.raw | op.oth:
                dop = ops[d]
                if dop.slot is not None:
                    continue
                if dop.eng == op.eng and not op.barrier:
                    if op.eng == "pe" or op.eng == "sp":
                        continue
                    if d not in op.raw:
                        continue
                dop.sig = True
        cnt = {e: 0 for e in self.ENGS}
        for op in ops:
            if op.sig:
                cnt[op.eng] += 1
                op.sigidx = cnt[op.eng]
        per_eng = {e: [] for e in self.ENGS}
        for op in ops:
            per_eng[op.eng].append(op)

        def run(eng):
            e = engobjs[eng]
            waited = {}
            for op in per_eng[eng]:
                need = {}
                for d in op.raw | op.oth:
                    dop = ops[d]
                    if dop.slot is not None or not dop.sig:
                        continue
                    if dop.eng == eng and not op.barrier:
                        if eng == "pe" or eng == "sp" or d not in op.raw:
                            continue
                    if dop.eng == eng and op.barrier:
                        continue
                    key = ("e", dop.eng)
                    need[key] = max(need.get(key, 0), dop.sigidx)
                for s, c in op.dmawait.items():
                    key = ("s", s)
                    need[key] = max(need.get(key, 0), 16 * c)
                for key, v in need.items():
                    if waited.get(key, 0) >= v:
                        continue
                    waited[key] = v
                    sem = sems[key[1]] if key[0] == "e" else slot_sems[key[1]]
                    e.wait_ge(sem, v)
                if op.fn is None:
                    continue
                ins = op.fn(e)
                if op.slot is not None:
                    ins.then_inc(slot_sems[op.slot], 16)
                elif op.sig:
                    ins.then_inc(sems[eng], 1)
        return run


def build_program():
    nc = bass.Bass("TRN2", target_bir_lowering=False)

    def din(name, shape):
        return nc.dram_tensor(name, list(shape), F32, kind="ExternalInput").ap()

    xh = din("xh", [NLOC, D])
    w_in = din("w_in", [D, 6 * D])
    w_gate = din("w_gate", [D, 2 * D])
    w_ap = din("w_ap", [D, D])
    w_cp = din("w_cp", [D, D])
    w_out = din("w_out", [D, D])
    w_up = din("w_up", [D, 4 * D])
    w_down = din("w_down", [4 * D, D])
    g1b_d = din("g1b", [128, D])
    g2b_d = din("g2b", [128, D])
    cst_d = din("cst", [128, 72])
    cmat_d = din("cmat", [128, 256])
    bias_d = din("biasT", [128, 16 * 640])
    y = nc.dram_tensor("y", [NTOK, D], F32, kind="ExternalOutput").ap()
    taps = {}
    if DEBUG_TAPS:
        def dout(name, shape, dt):
            taps[name] = nc.dram_tensor(name, list(shape), dt, kind="ExternalOutput").ap()
        dout("t_hT", [128, 8 * NLOC], BF16)
        dout("t_ya", [128, 8 * NTOK], BF16)
        dout("t_yc", [128, 8 * NTOK], BF16)
        dout("t_m", [128, 8 * NTOK], BF16)
        dout("t_x1", [128, 16 * D], F32)
        dout("t_h2", [128, 8 * NTOK], BF16)

    w_in_v = w_in.rearrange("(kt p) (s n) -> p s kt n", p=128, s=6)
    w_gate_v = w_gate.rearrange("(kt p) (s n) -> p s kt n", p=128, s=2)
    w_ap_v = w_ap.rearrange("(kt p) n -> p kt n", p=128)
    w_cp_v = w_cp.rearrange("(kt p) n -> p kt n", p=128)
    w_out_v = w_out.rearrange("(kt p) n -> p kt n", p=128)
    w_up_v = w_up.rearrange("(kt p) n -> p kt n", p=128)
    w_down_v = w_down.rearrange("(fc p) n -> p fc n", p=128)

    ARENA = 207 * KB
    P = Prog()

    from contextlib import ExitStack
    with ExitStack() as es:
        arena = es.enter_context(nc.sbuf_tensor("arena", [128, ARENA], U8))
        banks = [es.enter_context(nc.psum_tensor(f"bank{i}", [128, 512], F32)) for i in range(8)]
        sems = {e: es.enter_context(nc.semaphore(f"sem_{e}")) for e in Prog.ENGS}
        slot_names = ["const", "constp", "xt0", "xt1", "xt2", "xt3", "xt4", "xt5", "w0", "w1", "wd0", "wd1", "bias0", "bias1", "out", "tap"]
        slot_sems = {s: es.enter_context(nc.semaphore(f"slot_{s}")) for s in slot_names}
        block = es.enter_context(nc.Block())

        def view(off, shape, dt):
            esz = 4 if dt == F32 else 2
            n = int(np.prod(shape[1:]))
            assert off % 4 == 0 and off + n * esz <= ARENA, (off, shape)
            v = arena[:, off:off + n * esz].bitcast(dt)
            if len(shape) == 3:
                v = v.rearrange("p (a b) -> p a b", a=shape[1])
            elif len(shape) == 4:
                v = v.rearrange("p (a b c) -> p a b c", a=shape[1], b=shape[2])
            return v

        class Bump:
            def __init__(self, base, limit):
                self.o, self.limit = base, limit

            def __call__(self, shape, dt):
                esz = 4 if dt == F32 else 2
                n = int(np.prod(shape[1:])) * esz
                n4 = (n + 63) // 64 * 64
                off = self.o
                self.o += n4
                assert self.o <= self.limit, ("sbuf overflow", self.o, self.limit)
                return view(off, shape, dt)

        def psb(b):
            return banks[b][:, :]

        gb = Bump(0, 12 * KB)
        cst = gb([128, 72], F32)
        cmat = gb([128, 256], BF16)
        gbb = gb([128, D], F32)
        ms = gb([128, NTT], F32)
        rstd = gb([128, NTT], F32)
        ms2 = gb([128, 16], F32)
        rstd2 = gb([128, 16], F32)
        lnt = gb([128, NTT], F32)
        epsc = gb([128, 1], F32)
        ginv = gb([128, 2], F32)
        gsgn = gb([128, 2], F32)
        glog = gb([128, 2], F32)
        ident = cmat[:, 0:128]
        blk = cmat[:, 128:256]
        gq = cst[:, 0:1]
        gk = cst[:, 1:2]
        cw = cst[:, 2:26]
        cb = cst[:, 26:34]
        bgt = cst[:, 34:50]
        vmask = cst[:, 50:70]
        WB = 12 * KB
        PH = 44 * KB
        hT = view(PH, [128, 8, NLOC], BF16)
        yaT = view(PH + 40 * KB, [128, 8, NTOK], BF16)
        ycT = view(PH + 72 * KB, [128, 8, NTOK], BF16)
        mT = view(PH + 104 * KB, [128, 8, NTOK], BF16)
        x1 = view(PH, [128, 16, D], F32)
        h2T = view(PH + 64 * KB, [128, 8, NTOK], BF16)

        P.add("sp", lambda e: e.dma_start(out=cst, in_=cst_d[:, :]), writes=["cst"], slot="const")
        P.add("sp", lambda e: e.dma_start(out=gbb, in_=g1b_d[:, :]), writes=["gbb"], slot="const")
        P.add("pool", lambda e: e.dma_start(out=cmat, in_=cmat_d[:, :]), writes=["cmat"], slot="constp")
        P.add("dve", lambda e: e.memset(ms, 0.0), writes=["ms_init"])
        P.add("dve", lambda e: e.memset(ms2, 0.0), writes=["ms_init"])

        P.add("dve", lambda e: e.memset(epsc, EPS), writes=["epsc"])
        P.add("act", lambda e: e.activation(out=gsgn, in_=cst[:, 0:2], func=AF.Sign), reads=["cst"], writes=["gsgn"])
        P.add("dve", lambda e: e.tensor_tensor(out=ginv, in0=cst[:, 0:2], in1=gsgn, op=ALU.mult),
              reads=["cst", "gsgn"], writes=["gabs"])
        P.add("dve", lambda e: e.tensor_scalar_max(out=ginv, in0=ginv, scalar1=1e-30), reads=["gabs"], writes=["gabs2"])
        P.add("act", lambda e: e.activation(out=glog, in_=ginv, func=AF.Ln), reads=["gabs2"], writes=["glog"])

        def rsqrt_act(out, in_, tmp, reads, writes, tmptok):
            P.add("act", lambda e: e.activation(out=tmp, in_=in_, func=AF.Ln, bias=epsc[0:tmp.shape[0], :]),
                  reads=list(reads) + ["epsc"], writes=[tmptok])
            P.add("act", lambda e: e.activation(out=out, in_=tmp, func=AF.Exp, scale=-0.5),
                  reads=[tmptok], writes=list(writes))

        def wslot(i, S):
            return view(WB + i * 16 * KB, [128, S, 8, 128], BF16)

        def wtok(i, k):
            return ("w", i) if k == 0 else ("w", i, k)

        def load_qkv_like(i, sbase, cchunk):
            wv = wslot(i, 3)
            for k in range(3):
                src = w_in_v[:, sbase + k, :, cchunk * 128:(cchunk + 1) * 128]
                P.add("pool", lambda e, k=k, src=src: e.dma_start(out=wv[:, k, :, :], in_=src),
                      writes=[wtok(i, k)], slot=f"w{i}")

        p1 = Bump(PH + 72 * KB, ARENA)
        Etab = p1([128, 16, 5, 128], BF16)
        PT = [p1([128, 2, 5, 128], BF16) for _ in range(3)]
        rcp = [p1([128, 2, 128], F32) for _ in range(2)]
        lnd = [p1([128, 2, 128], F32) for _ in range(2)]
        qA, qB, kT, Vb = [None, None], [None, None], [None, None], [None, None]
        qA[1] = p1([128, NTOK], BF16)
        qB[1] = p1([128, NTOK], BF16)
        kT[1] = p1([128, NLOC], BF16)
        Vb[1] = p1([128, NTT, 192], BF16)
        assert p1.o <= PH + 136 * KB
        qA[0] = p1([128, NTOK], BF16)
        qB[0] = p1([128, NTOK], BF16)
        kT[0] = p1([128, NLOC], BF16)
        Vb[0] = p1([128, NTT, 192], BF16)
        assert PH + 124 * KB <= PH + 136 * KB and p1.o >= PH + 136 * KB
        qsb = [p1([128, 512], F32) for _ in range(3)]
        sqb = [p1([128, 512], BF16) for _ in range(3)]
        rsb = [p1([128, 512], F32) for _ in range(3)]
        obank = [None]
        bstage = [view(WB + 8 * KB + i * 16 * KB, [128, 640], F32) for i in range(2)]

        pb = Bump(PH + 40 * KB, PH + 72 * KB)
        xt = [pb([128, D], F32) for _ in range(6)]
        xsb = [pb([128, D], BF16) for _ in range(2)]
        junk = pb([128, D], BF16)
        load_qkv_like(0, 0, 0)
        load_qkv_like(1, 0, 1)
        def pro_A(tt):
            s = tt % 6
            P.add("sp", lambda e: e.dma_start(out=xt[s], in_=xh[tt * 128:(tt + 1) * 128, :]),
                  writes=[("xt", s)], slot=f"xt{s}")
            P.add("act", lambda e: e.activation(out=junk, in_=xt[s], func=AF.Square, scale=1.0 / 32.0,
                                                accum_out=ms[:, tt:tt + 1]),
                  reads=[("xt", s), "ms_init", "junk"], writes=[("ms", tt), "junk"])

        def pro_A2(tt):
            rsqrt_act(rstd[:, tt:tt + 1], ms[:, tt:tt + 1], lnt[:, tt:tt + 1], [("ms", tt)], [("rstd", tt)], ("lnt", tt))

        def pro_B(tt):
            s, s2, b = tt % 6, tt % 2, tt % 2
            P.add("dve", lambda e: e.scalar_tensor_tensor(
                out=xsb[s2], in0=xt[s], scalar=rstd[:, tt:tt + 1], in1=gbb, op0=ALU.mult, op1=ALU.mult),
                reads=[("xt", s), ("rstd", tt), "gbb"], writes=[("xsb", s2)])

            def tr(e):
                pbf = psb(b).bitcast(BF16)
                ins = None
                for kt in range(8):
                    ins = e.transpose(out=pbf[:, kt * 128:(kt + 1) * 128], in_=xsb[s2][:, kt * 128:(kt + 1) * 128],
                                      identity=ident)
                return ins
            P.add("pe", tr, reads=[("xsb", s2), "cmat"], writes=[("ps", b)])

        def pro_C(tt):
            b = tt % 2
            P.add("dve", lambda e: e.tensor_copy(
                out=hT[:, :, tt * 128:(tt + 1) * 128],
                in_=psb(b).bitcast(BF16).rearrange("p (k t) -> p k t", k=8)),
                reads=[("ps", b)], writes=[("hT", tt // 4)])
        def etab_head(h):
            s = h % 2
            P.add("sp", lambda e: e.dma_start(out=bstage[s], in_=bias_d[:, h * 640:(h + 1) * 640]),
                  writes=[("bst", s)], slot=f"bias{s}")
            P.add("act", lambda e: e.activation(
                out=Etab[:, h, :, :], in_=bstage[s].rearrange("p (j q) -> p j q", j=5), func=AF.Exp),
                reads=[("bst", s)], writes=["Etab"])
        for i in range(NTT + 3):
            if i < NTT:
                pro_A(i)
            if 0 <= i - 1 < NTT:
                pro_A2(i - 1)
            if 0 <= i - 2 < NTT:
                pro_B(i - 2)
            if 0 <= i - 3 < NTT:
                pro_C(i - 3)
        if DEBUG_TAPS:
            P.add("sp", lambda e: e.dma_start(out=taps["t_hT"][:, :], in_=hT.rearrange("p k t -> p (k t)")),
                  reads=[("hT", i) for i in range(5)], slot="tap")

        def finish():
            P.barrier()
            engobjs = {}
            run = P.emit(nc, engobjs, sems, slot_sems)

            def mk(eng):
                def f(e):
                    engobjs[eng] = e
                    run(eng)
                return f
            block.tensor(mk("pe"))
            block.scalar(mk("act"))
            block.vector(mk("dve"))
            block.gpsimd(mk("pool"))
            block.sync(mk("sp"))

        if STOP_AFTER == "pro":
            finish()
            return nc

        for s in range(2):
            P.add("pool", lambda e, s=s: e.memset(qA[s][64:128, :], 0.0), writes=[("qA", s)])
            P.add("pool", lambda e, s=s: e.memset(qB[s][0:64, :], 0.0), writes=[("qB", s)])
            P.add("dve", lambda e, s=s: e.tensor_copy(
                out=Vb[s][:, :, 64:128], in_=vmask.unsqueeze(2).to_broadcast([128, NTT, 64])),
                reads=["cst"], writes=[("Vones", s)])

        ring = {"proj": [0, 1, 2], "S": [3, 4, 5], "O": [6, 7]}
        rpos = {"proj": 0, "S": 0, "O": 0}

        def nb(kind):
            b = ring[kind][rpos[kind] % len(ring[kind])]
            rpos[kind] += 1
            return b

        tcnt = [0]

        def proj_units(hp):
            ws = hp % 2
            wv = wslot(ws, 3)
            units = []

            def qk_unit(which, st):
                st_ = {}

                def part1a():
                    b = nb("proj")
                    st_["b"] = b
                    loc = (st + 1) if which == 0 else st

                    def mm(e):
                        ins = None
                        for kt in range(8):
                            ins = e.matmul(psb(b), wv[:, which, kt, :], hT[:, kt, loc * 512:(loc + 1) * 512],
                                           start=(kt == 0), stop=(kt == 7))
                        return ins
                    P.add("pe", mm, reads=[wtok(ws, which), ("hT", loc)], writes=[("ps", b)])
                    t = tcnt[0] % 3
                    tcnt[0] += 1
                    st_["t"] = t
                    P.add("dve", lambda e: e.tensor_scalar_mul(out=qsb[t], in0=psb(b), scalar1=gsgn[:, which:which + 1]),
                          reads=[("ps", b), "gsgn"], writes=[("qsb", t)])
                    P.add("dve", lambda e: e.tensor_tensor(out=sqb[t], in0=qsb[t], in1=qsb[t], op=ALU.mult),
                          reads=[("qsb", t)], writes=[("sqb", t)])

                def part1b():
                    pass

                def part2():
                    t = st_["t"]
                    b2 = nb("proj")
                    P.add("pe", lambda e: e.matmul(psb(b2), blk, sqb[t], start=True, stop=True),
                          reads=[("sqb", t), "cmat"], writes=[("ps", b2)])
                    P.add("act", lambda e: e.activation(out=rsb[t], in_=psb(b2), func=AF.Ln, bias=epsc),
                          reads=[("ps", b2), "epsc"], writes=[("rsb", t)])
                    P.add("act", lambda e: e.activation(out=rsb[t], in_=rsb[t], func=AF.Exp, scale=-0.5,
                                                        bias=glog[:, which:which + 1]),
                          reads=[("rsb", t), "glog"], writes=[("rsb", t)])
                    c0 = st * 512
                    if which == 0:
                        P.add("pool", lambda e: e.tensor_tensor(
                            out=qA[ws][0:64, c0:c0 + 512], in0=qsb[t][0:64, :], in1=rsb[t][0:64, :], op=ALU.mult),
                            reads=[("qsb", t), ("rsb", t)], writes=[("qA", ws)])
                        P.add("pool", lambda e: e.tensor_tensor(
                            out=qB[ws][64:128, c0:c0 + 512], in0=qsb[t][64:128, :], in1=rsb[t][64:128, :], op=ALU.mult),
                            reads=[("qsb", t), ("rsb", t)], writes=[("qB", ws)])
                    else:
                        P.add("pool", lambda e: e.tensor_tensor(
                            out=kT[ws][:, c0:c0 + 512], in0=qsb[t], in1=rsb[t], op=ALU.mult),
                            reads=[("qsb", t), ("rsb", t)], writes=[("kT", ws)])
                return part1a, part1b, part2

            def v_unit(t4):
                def part1():
                    b = nb("proj")

                    def mm(e):
                        ins = None
                        for q in range(4):
                            tt = t4 * 4 + q
                            for kt in range(8):
                                ins = e.matmul(psb(b)[:, q * 128:(q + 1) * 128], hT[:, kt, tt * 128:(tt + 1) * 128],
                                               wv[:, 2, kt, :], start=(kt == 0), stop=(kt == 7))
                        return ins
                    P.add("pe", mm, reads=[wtok(ws, 2), ("hT", t4)], writes=[("ps", b)])
                    src = psb(b).rearrange("p (q c) -> p q c", q=4)
                    P.add("dve", lambda e: e.tensor_copy(out=Vb[ws][:, t4 * 4:(t4 + 1) * 4, 0:64], in_=src[:, :, 0:64]),
                          reads=[("ps", b)], writes=[("V", ws)])
                    P.add("dve", lambda e: e.tensor_copy(out=Vb[ws][:, t4 * 4:(t4 + 1) * 4, 128:192],
                                                         in_=src[:, :, 64:128]),
                          reads=[("ps", b)], writes=[("V", ws)])
                return part1, None, None
            for st in range(5):
                units.append(qk_unit(1, st))
                units.append(v_unit(st))
                if st >= 1:
                    units.append(qk_unit(0, st - 1))
            return units

        q1b, q2 = [], []

        def drain_stages():
            if q2 and q2[0][0] <= 0:
                q2.pop(0)[1]()
            if q1b and q1b[0][0] <= 0:
                f1b, f2 = q1b.pop(0)[1:]
                f1b()
                q2.append([1, f2])
            for q in (q1b, q2):
                for ent in q:
                    ent[0] -= 1

        def push_unit(u3):
            p1a, p1b, p2 = u3
            p1a()
            if p1b is not None:
                q1b.append([0, p1b, p2])
        for n_, u3 in enumerate(proj_units(0)):
            drain_stages()
            push_unit(u3)
            for h in range(16):
                if h * 14 // 16 == n_:
                    etab_head(h)
        while q1b or q2:
            drain_stages()

        def attn_step(hp, g):
            ws = hp % 2
            pt = (hp * 16 + g) % 3
            st_ = {}

            def e_qk():
                bA, bB, bZ = nb("S"), nb("S"), nb("S")
                st_["b"] = (bA, bB, bZ)
                for hd, qq, bb in ((0, qA[ws], bA), (1, qB[ws], bB)):
                    def qk4(e, qq=qq, bb=bb):
                        ins = None
                        for j in range(4):
                            ins = e.matmul(psb(bb)[:, j * 128:(j + 1) * 128], kT[ws][:, (g + j) * 128:(g + j + 1) * 128],
                                           qq[:, g * 128:(g + 1) * 128], start=True, stop=True)
                        return ins
                    P.add("pe", qk4, reads=[("kT", ws), ("qA", ws), ("qB", ws)], writes=[("ps", bb)])

                def qkz(e):
                    ins = None
                    for hd, qq in ((0, qA[ws]), (1, qB[ws])):
                        ins = e.matmul(psb(bZ)[:, hd * 128:(hd + 1) * 128], kT[ws][:, (g + 4) * 128:(g + 5) * 128],
                                       qq[:, g * 128:(g + 1) * 128], start=True, stop=True)
                    return ins
                P.add("pe", qkz, reads=[("kT", ws), ("qA", ws), ("qB", ws)], writes=[("ps", bZ)])

            def e_softmax():
                bA, bB, bZ = st_["b"]
                P.add("act", lambda e: e.activation(
                    out=PT[pt][:, 0, 0:4, :], in_=psb(bA).rearrange("p (j q) -> p j q", j=4), func=AF.Exp, scale=0.125),
                    reads=[("ps", bA)], writes=[("PT", pt, 0)])
                P.add("act", lambda e: e.activation(
                    out=PT[pt][:, 1, 0:4, :], in_=psb(bB).rearrange("p (j q) -> p j q", j=4), func=AF.Exp, scale=0.125),
                    reads=[("ps", bB)], writes=[("PT", pt, 1)])
                P.add("act", lambda e: e.activation(
                    out=PT[pt][:, :, 4, :], in_=psb(bZ)[:, 0:256].rearrange("p (h q) -> p h q", h=2), func=AF.Exp,
                    scale=0.125),
                    reads=[("ps", bZ)], writes=[("PT", pt, 2)])
                P.add("dve", lambda e: e.tensor_tensor(
                    out=PT[pt], in0=PT[pt], in1=Etab[:, 2 * hp:2 * hp + 2, :, :], op=ALU.mult),
                    reads=[("PT", pt, 0), ("PT", pt, 1), ("PT", pt, 2), "Etab"], writes=[("PTm", pt)])

            def e_pv():
                if g % 2 == 0:
                    obank[0] = nb("O")
                bO = obank[0]
                st_["bO"] = bO
                c0 = (g % 2) * 256

                def pv(e):
                    ins = None
                    for hd in range(2):
                        for j in range(5):
                            ins = e.matmul(psb(bO)[:, c0 + hd * 128:c0 + (hd + 1) * 128],
                                           Vb[ws][:, g + j, hd * 64:hd * 64 + 128], PT[pt][:, hd, j, :],
                                           start=(j == 0), stop=(j == 4))
                    return ins
                P.add("pe", pv, reads=[("PTm", pt), ("V", ws), ("Vones", ws)],
                      writes=[("ps", bO), ("PT", pt, 0), ("PT", pt, 1), ("PT", pt, 2)])

            def e_norm():
                if g % 2 == 0:
                    return
                bO = st_["bO"]
                rc = (hp * 8 + g // 2) % 2
                O4 = psb(bO).rearrange("p (g h q) -> p g h q", g=2, h=2)
                g0 = g - 1
                ydst = yaT[:, hp, g0 * 128:(g0 + 2) * 128].rearrange("p (g q) -> p g q", g=2)
                P.add("act", lambda e: e.activation(out=lnd[rc][0:64, :, :], in_=O4[64:128, :, 0, :], func=AF.Ln),
                      reads=[("ps", bO)], writes=[("lndA", rc)])
                P.add("act", lambda e: e.activation(out=lnd[rc][64:128, :, :], in_=O4[0:64, :, 1, :], func=AF.Ln),
                      reads=[("ps", bO)], writes=[("lndB", rc)])
                P.add("act", lambda e: e.activation(out=rcp[rc], in_=lnd[rc], func=AF.Exp, scale=-1.0),
                      reads=[("lndA", rc), ("lndB", rc)], writes=[("rcp", rc)])
                P.add("dve", lambda e: e.tensor_tensor(
                    out=ydst[0:64], in0=O4[0:64, :, 0, :], in1=rcp[rc][0:64, :, :], op=ALU.mult),
                    reads=[("ps", bO), ("rcp", rc)], writes=[("yaT", g // 4, hp, 0)])
                P.add("dve", lambda e: e.tensor_tensor(
                    out=ydst[64:128], in0=O4[64:128, :, 1, :], in1=rcp[rc][64:128, :, :], op=ALU.mult),
                    reads=[("ps", bO), ("rcp", rc)], writes=[("yaT", g // 4, hp, 1)])
            return e_qk, e_softmax, e_pv, e_norm

        steps = [attn_step(hp, g) for hp in range(8) for g in range(16)]
        NS = len(steps)
        for i in range(NS + 3):
            hp, g = divmod(i, 16)
            if i < NS:
                if g == 0:
                    nxt = proj_units(hp + 1) if hp + 1 < 8 else []
                    if hp + 2 < 8:
                        load_qkv_like(hp % 2, 0, hp + 2)
                steps[i][0]()
                steps[i][1]()
            if 0 <= i - 2 < NS:
                steps[i - 2][3]()
            drain_stages()
            if i < NS and nxt:
                push_unit(nxt.pop(0))
            if 0 <= i - 1 < NS:
                steps[i - 1][2]()
        while q1b or q2:
            drain_stages()
        ya_res = [("yaT", st, hp, hd) for st in range(4) for hp in range(8) for hd in range(2)]
        if DEBUG_TAPS:
            P.add("sp", lambda e: e.dma_start(out=taps["t_ya"][:, :], in_=yaT.rearrange("p k t -> p (k t)")),
                  reads=ya_res, slot="tap")
        if STOP_AFTER == "p1":
            finish()
            return nc

        load_qkv_like(0, 3, 0)
        load_qkv_like(1, 3, 1)
        p2 = Bump(PH + 136 * KB, ARENA)
        ubuf = [p2([128, 2 + NTOK], F32) for _ in range(2)]
        xcs = [p2([128, 512], F32) for _ in range(2)]
        at = [p2([128, 512], F32) for _ in range(2)]
        xch = p2([128, 64], F32)
        ring = {"all": list(range(8))}
        rpos = {"all": 0}
        k2 = [0]
        p2_tokens = ["xch"] + [("xcs", t) for t in range(2)] + [("at", t) for t in range(2)] + \
                    [("ub", p, st) for p in range(2) for st in range(-1, 4)]
        for cc in range(8):
            ws = cc % 2
            wv = wslot(ws, 3)
            ub = ubuf[cc % 2]
            w0c, w1c, w2c = (cw[:, cc * 3 + j:cc * 3 + j + 1] for j in range(3))
            cbc = cb[:, cc:cc + 1]
            b = nb("all")

            def mmh(e, wv=wv, b=b):
                ins = None
                for which, c0 in ((2, 0), (1, 64)):
                    for kt in range(8):
                        ins = e.matmul(psb(b)[:, c0:c0 + 64], wv[:, which, kt, :], hT[:, kt, 448:512],
                                       start=(kt == 0), stop=(kt == 7))
                return ins
            P.add("pe", mmh, reads=[wtok(ws, 1), wtok(ws, 2), ("hT", 0)], writes=[("ps", b)])
            P.add("act", lambda e, b=b: e.activation(out=xch, in_=psb(b)[:, 0:64], func=AF.Copy),
                  reads=[("ps", b)], writes=["xch"])
            P.add("dve", lambda e, b=b, ub=ub: e.tensor_tensor(out=ub[:, 0:2], in0=psb(b)[:, 126:128],
                                                              in1=xch[:, 62:64], op=ALU.mult),
                  reads=[("ps", b), "xch"], writes=[("ub", cc % 2, -1)])
            for st in range(4):
                loc = st + 1
                bx, bc, bbk = nb("all"), nb("all"), nb("all")

                def mm3(e, wv=wv, loc=loc, bx=bx, bc=bc, bbk=bbk):
                    ins = None
                    for which, bb in ((2, bx), (1, bc), (0, bbk)):
                        for kt in range(8):
                            ins = e.matmul(psb(bb), wv[:, which, kt, :], hT[:, kt, loc * 512:(loc + 1) * 512],
                                           start=(kt == 0), stop=(kt == 7))
                    return ins
                P.add("pe", mm3, reads=[wtok(ws, 0), wtok(ws, 1), wtok(ws, 2), ("hT", loc)],
                      writes=[("ps", bx), ("ps", bc), ("ps", bbk)])
                t = k2[0] % 2
                k2[0] += 1
                o = 2 + st * 512
                P.add("act", lambda e, t=t, bx=bx: e.activation(out=xcs[t], in_=psb(bx), func=AF.Copy),
                      reads=[("ps", bx)], writes=[("xcs", t)])
                P.add("dve", lambda e, t=t, bc=bc, ub=ub, o=o: e.tensor_tensor(
                    out=ub[:, o:o + 512], in0=psb(bc), in1=xcs[t], op=ALU.mult),
                    reads=[("ps", bc), ("xcs", t)], writes=[("ub", cc % 2, st)])
                P.add("act", lambda e, t=t, ub=ub, o=o, w2c=w2c, cbc=cbc: e.activation(
                    out=at[t], in_=ub[:, o:o + 512], func=AF.Identity, scale=w2c, bias=cbc),
                    reads=[("ub", cc % 2, st), "cst"], writes=[("at", t)])
                P.add("dve", lambda e, t=t, ub=ub, o=o, w1c=w1c: e.scalar_tensor_tensor(
                    out=at[t], in0=ub[:, o - 1:o + 511], scalar=w1c, in1=at[t], op0=ALU.mult, op1=ALU.add),
                    reads=[("ub", cc % 2, st), ("ub", cc % 2, st - 1), ("at", t)], writes=[("at", t)])
                P.add("dve", lambda e, t=t, ub=ub, o=o, w0c=w0c: e.scalar_tensor_tensor(
                    out=at[t], in0=ub[:, o - 2:o + 510], scalar=w0c, in1=at[t], op0=ALU.mult, op1=ALU.add),
                    reads=[("ub", cc % 2, st), ("ub", cc % 2, st - 1), ("at", t)], writes=[("at", t)])
                P.add("dve", lambda e, t=t, bbk=bbk, cc=cc, st=st: e.tensor_tensor(
                    out=ycT[:, cc, st * 512:(st + 1) * 512], in0=psb(bbk), in1=at[t], op=ALU.mult),
                    reads=[("ps", bbk), ("at", t)], writes=[("ycT", st)])
            if cc + 2 < 8:
                load_qkv_like(cc % 2, 3, cc + 2)
        if DEBUG_TAPS:
            P.add("sp", lambda e: e.dma_start(out=taps["t_yc"][:, :], in_=ycT.rearrange("p k t -> p (k t)")),
                  reads=[("ycT", st) for st in range(4)], slot="tap")
        if STOP_AFTER == "p2":
            finish()
            return nc

        def load_p3(i, oc):
            wv = wslot(i, 4)
            cs = slice(oc * 128, (oc + 1) * 128)
            P.add("pool", lambda e: e.dma_start(out=wv[:, 0, :, :], in_=w_ap_v[:, :, cs]), writes=[("w", i)],
                  slot=f"w{i}")
            P.add("pool", lambda e: e.dma_start(out=wv[:, 1, :, :], in_=w_cp_v[:, :, cs]), writes=[("w", i, 1)],
                  slot=f"w{i}")
            P.add("pool", lambda e: e.dma_start(out=wv[:, 2, :, :], in_=w_gate_v[:, 0, :, cs]), writes=[("w", i, 2)],
                  slot=f"w{i}")
            P.add("pool", lambda e: e.dma_start(out=wv[:, 3, :, :], in_=w_gate_v[:, 1, :, cs]), writes=[("w", i, 3)],
                  slot=f"w{i}")
        load_p3(0, 0)
        load_p3(1, 1)
        sga = [view(WB + 8 * KB + i * 16 * KB, [128, 512], F32) for i in range(2)]
        sgc = [view(WB + 10 * KB + i * 16 * KB, [128, 512], F32) for i in range(2)]
        m1 = [view(WB + 12 * KB + i * 16 * KB, [128, 512], F32) for i in range(2)]
        m2 = [view(WB + 14 * KB + i * 16 * KB, [128, 512], F32) for i in range(2)]

        k3 = [0]
        for oc in range(8):
            ws = oc % 2
            wv = wslot(ws, 4)
            for st in range(4):
                loc = st + 1
                b1, b2, b3, b4 = nb("all"), nb("all"), nb("all"), nb("all")

                for which, bb, src, c0, rd in ((2, b1, hT, loc * 512, [("w", ws, 2), ("hT", loc)]),
                                               (3, b2, hT, loc * 512, [("w", ws, 3), ("hT", loc)]),
                                               (0, b3, yaT, st * 512, [("w", ws)] + ya_res),
                                               (1, b4, ycT, st * 512, [("w", ws, 1), ("ycT", st)])):
                    def mm1(e, wv=wv, which=which, bb=bb, src=src, c0=c0):
                        ins = None
                        for kt in range(8):
                            ins = e.matmul(psb(bb), wv[:, which, kt, :], src[:, kt, c0:c0 + 512],
                                           start=(kt == 0), stop=(kt == 7))
                        return ins
                    P.add("pe", mm1, reads=rd, writes=[("ps", bb)])
                t = k3[0] % 2
                k3[0] += 1
                P.add("act", lambda e, t=t, b1=b1, oc=oc: e.activation(out=sga[t], in_=psb(b1), func=AF.Sigmoid,
                                                                       bias=bgt[:, oc:oc + 1]),
                      reads=[("ps", b1), "cst"], writes=[("sga", t)])
                P.add("act", lambda e, t=t, b2=b2, oc=oc: e.activation(out=sgc[t], in_=psb(b2), func=AF.Sigmoid,
                                                                       bias=bgt[:, 8 + oc:9 + oc]),
                      reads=[("ps", b2), "cst"], writes=[("sgc", t)])
                P.add("dve", lambda e, t=t, b3=b3: e.tensor_tensor(out=m1[t], in0=psb(b3), in1=sga[t], op=ALU.mult),
                      reads=[("ps", b3), ("sga", t)], writes=[("m1", t)])
                P.add("dve", lambda e, t=t, b4=b4: e.tensor_tensor(out=m2[t], in0=psb(b4), in1=sgc[t], op=ALU.mult),
                      reads=[("ps", b4), ("sgc", t)], writes=[("m2", t)])
                P.add("pool", lambda e, t=t, oc=oc, st=st: e.tensor_tensor(
                    out=mT[:, oc, st * 512:(st + 1) * 512], in0=m1[t], in1=m2[t], op=ALU.add),
                    reads=[("m1", t), ("m2", t)], writes=[("mT", st)])
            if oc + 2 < 8:
                load_p3(oc % 2, oc + 2)
        if DEBUG_TAPS:
            P.add("sp", lambda e: e.dma_start(out=taps["t_m"][:, :], in_=mT.rearrange("p k t -> p (k t)")),
                  reads=[("mT", st) for st in range(4)], slot="tap")
        if STOP_AFTER == "p3a":
            finish()
            return nc

        WoA = view(WB, [128, 4, D], BF16)
        WoB = view(PH + 152 * KB, [128, 4, D], BF16)
        P.add("pool", lambda e: e.dma_start(out=WoA, in_=w_out_v[:, 0:4, :]),
              writes=[("w", 0), ("w", 0, 1), ("w", 0, 2), ("w", 0, 3)], slot="w0")
        P.add("pool", lambda e: e.dma_start(out=WoB, in_=w_out_v[:, 4:8, :]),
              writes=["wo2"] + p2_tokens, slot="wd0")
        P.barrier()
        P.add("sp", lambda e: e.dma_start(out=gbb, in_=g2b_d[:, :]), writes=["gbb"], slot="const")
        def wu_slot(i):
            return view(WB + i * 16 * KB, [128, 8, 512], BF16)

        def wd_slot(i):
            return view(WB + i * 16 * KB + 8 * KB, [128, 4, D], BF16)

        def load_wu(i, grp):
            wu = wu_slot(i)
            P.add("pool", lambda e: e.dma_start(out=wu, in_=w_up_v[:, :, grp * 512:(grp + 1) * 512]),
                  writes=[("w", i)], slot=f"w{i}")

        def load_wd(i, grp):
            wd = wd_slot(i)
            P.add("pool", lambda e: e.dma_start(out=wd, in_=w_down_v[:, grp * 4:(grp + 1) * 4, :]),
                  writes=[("w", i, 1)], slot=f"wd{i}")

        def load_p4(i, grp):
            load_wu(i, grp)
            load_wd(i, grp)
        load_p4(1, 0)
        p3b = Bump(PH + 96 * KB, PH + 104 * KB)
        xr = [p3b([128, D], F32) for _ in range(2)]
        p3c = Bump(PH + 136 * KB, ARENA)
        h2b = [p3c([128, D], BF16) for _ in range(2)]
        junk2 = p3c([128, D], BF16)
        def p3b_A(tt):
            s = tt % 2
            P.add("sp", lambda e: e.dma_start(out=xr[s], in_=xh[HALO + tt * 128:HALO + (tt + 1) * 128, :]),
                  writes=[("xr", s)], slot=f"xt{s}")
            for half in range(2):
                b = nb("all")

                def mmo(e, half=half, b=b):
                    ins = None
                    for kt in range(8):
                        wsrc = WoA[:, kt, :] if kt < 4 else WoB[:, kt - 4, :]
                        ins = e.matmul(psb(b), mT[:, kt, tt * 128:(tt + 1) * 128], wsrc[:, half * 512:(half + 1) * 512],
                                       start=(kt == 0), stop=(kt == 7))
                    return ins
                P.add("pe", mmo, reads=[("w", 0), "wo2", ("mT", tt // 4)], writes=[("ps", b)])
                P.add("dve", lambda e, half=half, b=b: e.tensor_tensor(
                    out=x1[:, tt, half * 512:(half + 1) * 512], in0=psb(b), in1=xr[s][:, half * 512:(half + 1) * 512],
                    op=ALU.add), reads=[("ps", b), ("xr", s)], writes=[("x1", tt, half)])
            P.add("act", lambda e: e.activation(out=junk2, in_=x1[:, tt, :], func=AF.Square, scale=1.0 / 32.0,
                                                accum_out=ms2[:, tt:tt + 1]),
                  reads=[("x1", tt, 0), ("x1", tt, 1), "ms_init", "junk2"], writes=[("ms2", tt), "junk2"])

        def p3b_A2(tt):
            rsqrt_act(rstd2[:, tt:tt + 1], ms2[:, tt:tt + 1], lnt[:, tt:tt + 1], [("ms2", tt)], [("rstd2", tt)],
                      ("lnt2", tt))

        trb = {}

        def p3b_B1(tt):
            s = tt % 2
            P.add("dve", lambda e: e.scalar_tensor_tensor(
                out=h2b[s], in0=x1[:, tt, :], scalar=rstd2[:, tt:tt + 1], in1=gbb, op0=ALU.mult, op1=ALU.mult),
                reads=[("x1", tt, 0), ("x1", tt, 1), ("rstd2", tt), "gbb"], writes=[("h2b", s)])

        def p3b_B2(tt):
            s = tt % 2
            b = nb("all")
            trb[tt] = b

            def tr2(e):
                pbf = psb(b).bitcast(BF16)
                ins = None
                for kt in range(8):
                    ins = e.transpose(out=pbf[:, kt * 128:(kt + 1) * 128], in_=h2b[s][:, kt * 128:(kt + 1) * 128],
                                      identity=ident)
                return ins
            P.add("pe", tr2, reads=[("h2b", s), "cmat"], writes=[("ps", b)])

        def p3b_C(tt):
            b = trb[tt]
            P.add("dve", lambda e: e.tensor_copy(
                out=h2T[:, :, tt * 128:(tt + 1) * 128],
                in_=psb(b).bitcast(BF16).rearrange("p (k t) -> p k t", k=8)),
                reads=[("ps", b)], writes=[("h2T", tt // 4)])
        for i in range(16 + 4):
            if 0 <= i - 3 < 16:
                p3b_B1(i - 3)
            if i < 16:
                p3b_A(i)
            if 0 <= i - 1 < 16:
                p3b_A2(i - 1)
            if 0 <= i - 3 < 16:
                p3b_B2(i - 3)
            if 0 <= i - 4 < 16:
                p3b_C(i - 4)
        load_p4(0, 1)
        x1_res = [("x1", tt, half) for tt in range(16) for half in range(2)]
        if DEBUG_TAPS:
            P.add("sp", lambda e: e.dma_start(out=taps["t_x1"][:, :], in_=x1.rearrange("p k t -> p (k t)")),
                  reads=x1_res, slot="tap")
            P.add("sp", lambda e: e.dma_start(out=taps["t_h2"][:, :], in_=h2T.rearrange("p k t -> p (k t)")),
                  reads=[("h2T", st) for st in range(4)], slot="tap")
        if STOP_AFTER == "p3b":
            finish()
            return nc

        P.barrier()
        p4 = Bump(PH + 104 * KB, ARENA)
        uT = [p4([128, 4, NTOK], BF16) for _ in range(2)]
        rl = [p4([128, 512], F32) for _ in range(3)]

        k4 = [0]

        def up_phase(grp):
            ws = grp % 2
            wl = (grp + 1) % 2
            wu = wu_slot(wl)
            for fc in range(4):
                for st in range(4):
                    b = nb("all")

                    def mmu(e, wu=wu, fc=fc, st=st, b=b):
                        ins = None
                        for kt in range(8):
                            ins = e.matmul(psb(b), wu[:, kt, fc * 128:(fc + 1) * 128], h2T[:, kt, st * 512:(st + 1) * 512],
                                           start=(kt == 0), stop=(kt == 7))
                        return ins
                    P.add("pe", mmu, reads=[("w", wl), ("h2T", st)], writes=[("ps", b)])
                    t = k4[0] % 3
                    k4[0] += 1
                    P.add("act", lambda e, t=t, b=b: e.activation(out=rl[t], in_=psb(b), func=AF.Relu),
                          reads=[("ps", b)], writes=[("rl", t)])
                    P.add("pool", lambda e, t=t, ws=ws, fc=fc, st=st: e.tensor_tensor(
                        out=uT[ws][:, fc, st * 512:(st + 1) * 512], in0=rl[t], in1=rl[t], op=ALU.mult),
                        reads=[("rl", t)], writes=[("uT", ws, st)])

        def down_phase(grp):
            ws = grp % 2
            wl = (grp + 1) % 2
            wd = wd_slot(wl)
            for tt in range(16):
                for half in range(2):
                    b = nb("all")

                    def mmd(e, wd=wd, ws=ws, tt=tt, half=half, b=b):
                        ins = None
                        for fc in range(4):
                            ins = e.matmul(psb(b), uT[ws][:, fc, tt * 128:(tt + 1) * 128],
                                           wd[:, fc, half * 512:(half + 1) * 512], start=(fc == 0), stop=(fc == 3))
                        return ins
                    P.add("pe", mmd, reads=[("w", wl, 1), ("uT", ws, tt // 4)], writes=[("ps", b)])
                    P.add("dve", lambda e, tt=tt, half=half, b=b: e.tensor_tensor(
                        out=x1[:, tt, half * 512:(half + 1) * 512], in0=psb(b), in1=x1[:, tt, half * 512:(half + 1) * 512],
                        op=ALU.add), reads=[("ps", b), ("x1", tt, half)], writes=[("x1", tt, half)])
                if grp == 7:
                    P.add("sp", lambda e, tt=tt: e.dma_start(out=y[tt * 128:(tt + 1) * 128, :], in_=x1[:, tt, :]),
                          reads=[("x1", tt, 0), ("x1", tt, 1)], slot="out")
            if grp + 2 < 8:
                load_wd(wl, grp + 2)

        up_phase(0)
        for grp in range(8):
            if grp + 1 < 8:
                up_phase(grp + 1)
            if grp + 2 < 8:
                load_wu((grp + 1) % 2, grp + 2)
            down_phase(grp)
        finish()
    return nc


_CACHE = {}


def _host_consts(q_norm_g, k_norm_g, rel_bias, conv_w, conv_b, b_gate, norm1_g, norm2_g):
    cst = np.zeros((8, 128, 72), np.float32)
    p = np.arange(128)
    cst[:, :, 0] = q_norm_g[p % 64]
    cst[:, :, 1] = k_norm_g[p % 64]
    cwv = conv_w.reshape(3, 8, 128)
    cst[:, :, 2:26] = np.transpose(cwv, (2, 1, 0)).reshape(128, 24)
    cst[:, :, 26:34] = conv_b.reshape(8, 128).T
    cst[:, :, 34:50] = b_gate.reshape(16, 128).T
    cst[:, :, 50:70] = 1.0
    for c in range(8):
        if c % 2 == 0:
            cst[c, :, 50:54] = 0.0
    cmat = np.zeros((128, 256), np.float32)
    cmat[:, 0:128] = np.eye(128, dtype=np.float32)
    cmat[:, 128:256] = (p[:, None] // 64 == p[None, :] // 64).astype(np.float32) / 64.0
    kk = np.arange(128)[:, None, None]
    j = np.arange(5)[None, :, None]
    qq = np.arange(128)[None, None, :]
    dist = 512 - 128 * j + qq - kk
    idx = np.clip(dist, -256, 256) + 256
    bias = rel_bias[:, idx]
    cdiff = (8 - 2 * j + qq // 64) - (kk // 64)
    valid = (cdiff >= 0) & (cdiff <= 8)
    bias = np.where(valid[None], bias, np.float32(-1e30)).astype(np.float32)
    biasT = np.ascontiguousarray(np.transpose(bias, (1, 0, 2, 3))).reshape(128, 16 * 640)
    g1b = np.ascontiguousarray(np.broadcast_to(norm1_g[None, :], (128, D))).astype(np.float32)
    g2b = np.ascontiguousarray(np.broadcast_to(norm2_g[None, :], (128, D))).astype(np.float32)
    return cst, cmat, biasT, g1b, g2b


def kernel(x, norm1_g, w_in, q_norm_g, k_norm_g, rel_bias, conv_w, conv_b, w_attn_proj, w_conv_proj,
           w_gate, b_gate, w_out, norm2_g, w_up, w_down):
    f = lambda a: np.ascontiguousarray(np.asarray(a, dtype=np.float32))
    x = f(x)
    B, S, _ = x.shape
    cst, cmat, biasT, g1b, g2b = _host_consts(f(q_norm_g), f(k_norm_g), f(rel_bias), f(conv_w), f(conv_b),
                                              f(b_gate), f(norm1_g), f(norm2_g))
    if "nc" not in _CACHE:
        _CACHE["nc"] = build_program()
    nc = _CACHE["nc"]
    shared = {"w_in": f(w_in), "w_gate": f(w_gate), "w_ap": f(w_attn_proj), "w_cp": f(w_conv_proj),
              "w_out": f(w_out), "w_up": f(w_up), "w_down": f(w_down), "g1b": g1b, "g2b": g2b, "cmat": cmat,
              "biasT": biasT}
    in_maps = []
    for c in range(N_RUN):
        b, half = c // 2, c % 2
        xh = np.zeros((NLOC, D), np.float32)
        if half == 0:
            xh[HALO:] = x[b, 0:NTOK]
        else:
            xh[:] = x[b, NTOK - HALO:2 * NTOK]
        m = dict(shared)
        m["xh"] = xh
        m["cst"] = cst[c]
        in_maps.append(m)
    res = run_bass_kernel_spmd(nc, in_maps, core_ids=list(range(N_RUN)))
    _CACHE["last"] = res
    out = np.zeros((B, S, D), np.float32)
    for c in range(N_RUN):
        b, half = c // 2, c % 2
        out[b, half * NTOK:(half + 1) * NTOK] = res.results[c]["y"]
    return out
```

```python
import numpy as np
import concourse.bass as bass
import concourse.mybir as mybir
from concourse.bass_utils import run_bass_kernel_spmd

F32 = mybir.dt.float32
BF16 = mybir.dt.bfloat16
U8 = mybir.dt.uint8
AF = mybir.ActivationFunctionType
ALU = mybir.AluOpType

D = 1024
NTOK = 2048
HALO = 512
NLOC = NTOK + HALO
NTT = NLOC // 128
EPS = 1e-6
KB = 1024

DEBUG_TAPS = False
N_RUN = 8
STOP_AFTER = None


class _Op:
    __slots__ = ("eng", "fn", "raw", "oth", "idx", "sig", "sigidx", "slot", "dmawait", "barrier")


class Prog:
    ENGS = ("pe", "act", "dve", "pool", "sp")

    def __init__(self):
        self.ops = []
        self.last_writer = {}
        self.readers = {}
        self.slot_count = {}
        self.last_on_eng = {}

    def add(self, eng, fn, reads=(), writes=(), slot=None):
        op = _Op()
        op.eng, op.fn, op.idx, op.sig, op.slot, op.barrier = eng, fn, len(self.ops), False, slot, False
        op.raw, op.oth, op.dmawait = set(), set(), {}
        for r in reads:
            w = self.last_writer.get(r)
            if w is not None:
                op.raw.add(w)
        for r in writes:
            w = self.last_writer.get(r)
            if w is not None:
                op.oth.add(w)
            for rd in self.readers.get(r, ()):
                op.oth.add(rd)
        for r in reads:
            self.readers.setdefault(r, []).append(op.idx)
        for r in writes:
            self.last_writer[r] = op.idx
            self.readers[r] = []
        for d in list(op.raw | op.oth):
            dop = self.ops[d]
            if dop.slot is not None:
                op.dmawait[dop.slot] = self.slot_count[dop.slot]
        if slot is not None:
            self.slot_count[slot] = self.slot_count.get(slot, 0) + 1
        self.ops.append(op)
        self.last_on_eng[eng] = op.idx
        return op

    def barrier(self):
        lasts = dict(self.last_on_eng)
        slots = dict(self.slot_count)
        for eng in self.ENGS:
            op = _Op()
            op.eng, op.fn, op.idx, op.sig, op.slot, op.barrier = eng, None, len(self.ops), False, None, True
            op.raw = set(v for k, v in lasts.items())
            op.oth = set()
            op.dmawait = dict(slots)
            self.ops.append(op)
        self.last_writer = {}
        self.readers = {}

    def emit(self, nc, engobjs, sems, slot_sems):
        ops = self.ops
        for op in ops:
            for d in op.raw | op.oth:
                dop = ops[d]
                if dop.slot is not None:
                    continue
                if dop.eng == op.eng and not op.barrier:
                    if op.eng == "pe" or op.eng == "sp":
                        continue
                    if d not in op.raw:
                        continue
                dop.sig = True
        cnt = {e: 0 for e in self.ENGS}
        for op in ops:
            if op.sig:
                cnt[op.eng] += 1
                op.sigidx = cnt[op.eng]
        per_eng = {e: [] for e in self.ENGS}
        for op in ops:
            per_eng[op.eng].append(op)

        def run(eng):
            e = engobjs[eng]
            waited = {}
            for op in per_eng[eng]:
                need = {}
                for d in op.raw | op.oth:
                    dop = ops[d]
                    if dop.slot is not None or not dop.sig:
                        continue
                    if dop.eng == eng and not op.barrier:
                        if eng == "pe" or eng == "sp" or d not in op.raw:
                            continue
                    if dop.eng == eng and op.barrier:
                        continue
                    key = ("e", dop.eng)
                    need[key] = max(need.get(key, 0), dop.sigidx)
                for s, c in op.dmawait.items():
                    key = ("s", s)
                    need[key] = max(need.get(key, 0), 16 * c)
                for key, v in need.items():
                    if waited.get(key, 0) >= v:
                        continue
                    waited[key] = v
                    sem = sems[key[1]] if key[0] == "e" else slot_sems[key[1]]
                    e.wait_ge(sem, v)
                if op.fn is None:
                    continue
                ins = op.fn(e)
                if op.slot is not None:
                    ins.then_inc(slot_sems[op.slot], 16)
                elif op.sig:
                    ins.then_inc(sems[eng], 1)
        return run


def build_program():
    nc = bass.Bass("TRN2", target_bir_lowering=False)

    def din(name, shape):
        return nc.dram_tensor(name, list(shape), F32, kind="ExternalInput").ap()

    xh = din("xh", [NLOC, D])
    w_in = din("w_in", [D, 6 * D])
    w_gate = din("w_gate", [D, 2 * D])
    w_ap = din("w_ap", [D, D])
    w_cp = din("w_cp", [D, D])
    w_out = din("w_out", [D, D])
    w_up = din("w_up", [D, 4 * D])
    w_down = din("w_down", [4 * D, D])
    g1b_d = din("g1b", [128, D])
    g2b_d = din("g2b", [128, D])
    cst_d = din("cst", [128, 72])
    cmat_d = din("cmat", [128, 256])
    bias_d = din("biasT", [128, 16 * 640])
    y = nc.dram_tensor("y", [NTOK, D], F32, kind="ExternalOutput").ap()
    taps = {}
    if DEBUG_TAPS:
        def dout(name, shape, dt):
            taps[name] = nc.dram_tensor(name, list(shape), dt, kind="ExternalOutput").ap()
        dout("t_hT", [128, 8 * NLOC], BF16)
        dout("t_ya", [128, 8 * NTOK], BF16)
        dout("t_yc", [128, 8 * NTOK], BF16)
        dout("t_m", [128, 8 * NTOK], BF16)
        dout("t_x1", [128, 16 * D], F32)
        dout("t_h2", [128, 8 * NTOK], BF16)

    w_in_v = w_in.rearrange("(kt p) (s n) -> p s kt n", p=128, s=6)
    w_gate_v = w_gate.rearrange("(kt p) (s n) -> p s kt n", p=128, s=2)
    w_ap_v = w_ap.rearrange("(kt p) n -> p kt n", p=128)
    w_cp_v = w_cp.rearrange("(kt p) n -> p kt n", p=128)
    w_out_v = w_out.rearrange("(kt p) n -> p kt n", p=128)
    w_up_v = w_up.rearrange("(kt p) n -> p kt n", p=128)
    w_down_v = w_down.rearrange("(fc p) n -> p fc n", p=128)

    ARENA = 207 * KB
    P = Prog()

    from contextlib import ExitStack
    with ExitStack() as es:
        arena = es.enter_context(nc.sbuf_tensor("arena", [128, ARENA], U8))
        banks = [es.enter_context(nc.psum_tensor(f"bank{i}", [128, 512], F32)) for i in range(8)]
        sems = {e: es.enter_context(nc.semaphore(f"sem_{e}")) for e in Prog.ENGS}
        slot_names = ["const", "constp", "xt0", "xt1", "xt2", "xt3", "xt4", "xt5", "w0", "w1", "wd0", "wd1", "bias0", "bias1", "out", "tap"]
        slot_sems = {s: es.enter_context(nc.semaphore(f"slot_{s}")) for s in slot_names}
        block = es.enter_context(nc.Block())

        def view(off, shape, dt):
            esz = 4 if dt == F32 else 2
            n = int(np.prod(shape[1:]))
            assert off % 4 == 0 and off + n * esz <= ARENA, (off, shape)
            v = arena[:, off:off + n * esz].bitcast(dt)
            if len(shape) == 3:
                v = v.rearrange("p (a b) -> p a b", a=shape[1])
            elif len(shape) == 4:
                v = v.rearrange("p (a b c) -> p a b c", a=shape[1], b=shape[2])
            return v

        class Bump:
            def __init__(self, base, limit):
                self.o, self.limit = base, limit

            def __call__(self, shape, dt):
                esz = 4 if dt == F32 else 2
                n = int(np.prod(shape[1:])) * esz
                n4 = (n + 63) // 64 * 64
                off = self.o
                self.o += n4
                assert self.o <= self.limit, ("sbuf overflow", self.o, self.limit)
                return view(off, shape, dt)

        def psb(b):
            return banks[b][:, :]

        gb = Bump(0, 12 * KB)
        cst = gb([128, 72], F32)
        cmat = gb([128, 256], BF16)
        gbb = gb([128, D], F32)
        ms = gb([128, NTT], F32)
        rstd = gb([128, NTT], F32)
        ms2 = gb([128, 16], F32)
        rstd2 = gb([128, 16], F32)
        lnt = gb([128, NTT], F32)
        epsc = gb([128, 1], F32)
        ginv = gb([128, 2], F32)
        gsgn = gb([128, 2], F32)
        glog = gb([128, 2], F32)
        ident = cmat[:, 0:128]
        blk = cmat[:, 128:256]
        gq = cst[:, 0:1]
        gk = cst[:, 1:2]
        cw = cst[:, 2:26]
        cb = cst[:, 26:34]
        bgt = cst[:, 34:50]
        vmask = cst[:, 50:70]
        WB = 12 * KB
        PH = 44 * KB
        hT = view(PH, [128, 8, NLOC], BF16)
        yaT = view(PH + 40 * KB, [128, 8, NTOK], BF16)
        ycT = view(PH + 72 * KB, [128, 8, NTOK], BF16)
        mT = view(PH + 104 * KB, [128, 8, NTOK], BF16)
        x1 = view(PH, [128, 16, D], F32)
        h2T = view(PH + 64 * KB, [128, 8, NTOK], BF16)

        P.add("sp", lambda e: e.dma_start(out=cst, in_=cst_d[:, :]), writes=["cst"], slot="const")
        P.add("sp", lambda e: e.dma_start(out=gbb, in_=g1b_d[:, :]), writes=["gbb"], slot="const")
        P.add("pool", lambda e: e.dma_start(out=cmat, in_=cmat_d[:, :]), writes=["cmat"], slot="constp")
        P.add("dve", lambda e: e.memset(ms, 0.0), writes=["ms_init"])
        P.add("dve", lambda e: e.memset(ms2, 0.0), writes=["ms_init"])

        P.add("dve", lambda e: e.memset(epsc, EPS), writes=["epsc"])
        P.add("act", lambda e: e.activation(out=gsgn, in_=cst[:, 0:2], func=AF.Sign), reads=["cst"], writes=["gsgn"])
        P.add("dve", lambda e: e.tensor_tensor(out=ginv, in0=cst[:, 0:2], in1=gsgn, op=ALU.mult),
              reads=["cst", "gsgn"], writes=["gabs"])
        P.add("dve", lambda e: e.tensor_scalar_max(out=ginv, in0=ginv, scalar1=1e-30), reads=["gabs"], writes=["gabs2"])
        P.add("act", lambda e: e.activation(out=glog, in_=ginv, func=AF.Ln), reads=["gabs2"], writes=["glog"])

        def rsqrt_act(out, in_, tmp, reads, writes, tmptok):
            P.add("act", lambda e: e.activation(out=tmp, in_=in_, func=AF.Ln, bias=epsc[0:tmp.shape[0], :]),
                  reads=list(reads) + ["epsc"], writes=[tmptok])
            P.add("act", lambda e: e.activation(out=out, in_=tmp, func=AF.Exp, scale=-0.5),
                  reads=[tmptok], writes=list(writes))

        def wslot(i, S):
            return view(WB + i * 16 * KB, [128, S, 8, 128], BF16)

        def wtok(i, k):
            return ("w", i) if k == 0 else ("w", i, k)

        def load_qkv_like(i, sbase, cchunk):
            wv = wslot(i, 3)
            for k in range(3):
                src = w_in_v[:, sbase + k, :, cchunk * 128:(cchunk + 1) * 128]
                P.add("pool", lambda e, k=k, src=src: e.dma_start(out=wv[:, k, :, :], in_=src),
                      writes=[wtok(i, k)], slot=f"w{i}")

        p1 = Bump(PH + 72 * KB, ARENA)
        Etab = p1([128, 16, 5, 128], BF16)
        PT = [p1([128, 2, 5, 128], BF16) for _ in range(3)]
        rcp = [p1([128, 2, 128], F32) for _ in range(2)]
        lnd = [p1([128, 2, 128], F32) for _ in range(2)]
        qA, qB, kT, Vb = [None, None], [None, None], [None, None], [None, None]
        qA[1] = p1([128, NTOK], BF16)
        qB[1] = p1([128, NTOK], BF16)
        kT[1] = p1([128, NLOC], BF16)
        Vb[1] = p1([128, NTT, 192], BF16)
        assert p1.o <= PH + 136 * KB
        qA[0] = p1([128, NTOK], BF16)
        qB[0] = p1([128, NTOK], BF16)
        kT[0] = p1([128, NLOC], BF16)
        Vb[0] = p1([128, NTT, 192], BF16)
        assert PH + 124 * KB <= PH + 136 * KB and p1.o >= PH + 136 * KB
        qsb = [p1([128, 512], F32) for _ in range(3)]
        sqb = [p1([128, 512], BF16) for _ in range(3)]
        rsb = [p1([128, 512], F32) for _ in range(3)]
        obank = [None]
        bstage = [view(WB + 8 * KB + i * 16 * KB, [128, 640], F32) for i in range(2)]

        pb = Bump(PH + 40 * KB, PH + 72 * KB)
        xt = [pb([128, D], F32) for _ in range(6)]
        xsb = [pb([128, D], BF16) for _ in range(2)]
        junk = pb([128, D], BF16)
        load_qkv_like(0, 0, 0)
        load_qkv_like(1, 0, 1)
        def pro_A(tt):
            s = tt % 6
            P.add("sp", lambda e: e.dma_start(out=xt[s], in_=xh[tt * 128:(tt + 1) * 128, :]),
                  writes=[("xt", s)], slot=f"xt{s}")
            P.add("act", lambda e: e.activation(out=junk, in_=xt[s], func=AF.Square, scale=1.0 / 32.0,
                                                accum_out=ms[:, tt:tt + 1]),
                  reads=[("xt", s), "ms_init", "junk"], writes=[("ms", tt), "junk"])

        def pro_A2(tt):
            rsqrt_act(rstd[:, tt:tt + 1], ms[:, tt:tt + 1], lnt[:, tt:tt + 1], [("ms", tt)], [("rstd", tt)], ("lnt", tt))

        def pro_B(tt):
            s, s2, b = tt % 6, tt % 2, tt % 2
            P.add("dve", lambda e: e.scalar_tensor_tensor(
                out=xsb[s2], in0=xt[s], scalar=rstd[:, tt:tt + 1], in1=gbb, op0=ALU.mult, op1=ALU.mult),
                reads=[("xt", s), ("rstd", tt), "gbb"], writes=[("xsb", s2)])

            def tr(e):
                pbf = psb(b).bitcast(BF16)
                ins = None
                for kt in range(8):
                    ins = e.transpose(out=pbf[:, kt * 128:(kt + 1) * 128], in_=xsb[s2][:, kt * 128:(kt + 1) * 128],
                                      identity=ident)
                return ins
            P.add("pe", tr, reads=[("xsb", s2), "cmat"], writes=[("ps", b)])

        def pro_C(tt):
            b = tt % 2
            P.add("dve", lambda e: e.tensor_copy(
                out=hT[:, :, tt * 128:(tt + 1) * 128],
                in_=psb(b).bitcast(BF16).rearrange("p (k t) -> p k t", k=8)),
                reads=[("ps", b)], writes=[("hT", tt // 4)])
        def etab_head(h):
            s = h % 2
            P.add("sp", lambda e: e.dma_start(out=bstage[s], in_=bias_d[:, h * 640:(h + 1) * 640]),
                  writes=[("bst", s)], slot=f"bias{s}")
            P.add("act", lambda e: e.activation(
                out=Etab[:, h, :, :], in_=bstage[s].rearrange("p (j q) -> p j q", j=5), func=AF.Exp),
                reads=[("bst", s)], writes=["Etab"])
        for i in range(NTT + 3):
            if i < NTT:
                pro_A(i)
            if 0 <= i - 1 < NTT:
                pro_A2(i - 1)
            if 0 <= i - 2 < NTT:
                pro_B(i - 2)
            if 0 <= i - 3 < NTT:
                pro_C(i - 3)
        if DEBUG_TAPS:
            P.add("sp", lambda e: e.dma_start(out=taps["t_hT"][:, :], in_=hT.rearrange("p k t -> p (k t)")),
                  reads=[("hT", i) for i in range(5)], slot="tap")

        def finish():
            P.barrier()
            engobjs = {}
            run = P.emit(nc, engobjs, sems, slot_sems)

            def mk(eng):
                def f(e):
                    engobjs[eng] = e
                    run(eng)
                return f
            block.tensor(mk("pe"))
            block.scalar(mk("act"))
            block.vector(mk("dve"))
            block.gpsimd(mk("pool"))
            block.sync(mk("sp"))

        if STOP_AFTER == "pro":
            finish()
            return nc

        for s in range(2):
            P.add("pool", lambda e, s=s: e.memset(qA[s][64:128, :], 0.0), writes=[("qA", s)])
            P.add("pool", lambda e, s=s: e.memset(qB[s][0:64, :], 0.0), writes=[("qB", s)])
            P.add("dve", lambda e, s=s: e.tensor_copy(
                out=Vb[s][:, :, 64:128], in_=vmask.unsqueeze(2).to_broadcast([128, NTT, 64])),
                reads=["cst"], writes=[("Vones", s)])

        ring = {"proj": [0, 1, 2], "S": [3, 4, 5], "O": [6, 7]}
        rpos = {"proj": 0, "S": 0, "O": 0}

        def nb(kind):
            b = ring[kind][rpos[kind] % len(ring[kind])]
            rpos[kind] += 1
            return b

        tcnt = [0]

        def proj_units(hp):
            ws = hp % 2
            wv = wslot(ws, 3)
            units = []

            def qk_unit(which, st):
                st_ = {}

                def part1a():
                    b = nb("proj")
                    st_["b"] = b
                    loc = (st + 1) if which == 0 else st

                    def mm(e):
                        ins = None
                        for kt in range(8):
                            ins = e.matmul(psb(b), wv[:, which, kt, :], hT[:, kt, loc * 512:(loc + 1) * 512],
                                           start=(kt == 0), stop=(kt == 7))
                        return ins
                    P.add("pe", mm, reads=[wtok(ws, which), ("hT", loc)], writes=[("ps", b)])
                    t = tcnt[0] % 3
                    tcnt[0] += 1
                    st_["t"] = t
                    P.add("dve", lambda e: e.tensor_scalar_mul(out=qsb[t], in0=psb(b), scalar1=gsgn[:, which:which + 1]),
                          reads=[("ps", b), "gsgn"], writes=[("qsb", t)])
                    P.add("dve", lambda e: e.tensor_tensor(out=sqb[t], in0=qsb[t], in1=qsb[t], op=ALU.mult),
                          reads=[("qsb", t)], writes=[("sqb", t)])

                def part1b():
                    pass

                def part2():
                    t = st_["t"]
                    b2 = nb("proj")
                    P.add("pe", lambda e: e.matmul(psb(b2), blk, sqb[t], start=True, stop=True),
                          reads=[("sqb", t), "cmat"], writes=[("ps", b2)])
                    P.add("act", lambda e: e.activation(out=rsb[t], in_=psb(b2), func=AF.Ln, bias=epsc),
                          reads=[("ps", b2), "epsc"], writes=[("rsb", t)])
                    P.add("act", lambda e: e.activation(out=rsb[t], in_=rsb[t], func=AF.Exp, scale=-0.5,
                                                        bias=glog[:, which:which + 1]),
                          reads=[("rsb", t), "glog"], writes=[("rsb", t)])
                    c0 = st * 512
                    if which == 0:
                        P.add("pool", lambda e: e.tensor_tensor(
                            out=qA[ws][0:64, c0:c0 + 512], in0=qsb[t][0:64, :], in1=rsb[t][0:64, :], op=ALU.mult),
                            reads=[("qsb", t), ("rsb", t)], writes=[("qA", ws)])
                        P.add("pool", lambda e: e.tensor_tensor(
                            out=qB[ws][64:128, c0:c0 + 512], in0=qsb[t][64:128, :], in1=rsb[t][64:128, :], op=ALU.mult),
                            reads=[("qsb", t), ("rsb", t)], writes=[("qB", ws)])
                    else:
                        P.add("pool", lambda e: e.tensor_tensor(
                            out=kT[ws][:, c0:c0 + 512], in0=qsb[t], in1=rsb[t], op=ALU.mult),
                            reads=[("qsb", t), ("rsb", t)], writes=[("kT", ws)])
                return part1a, part1b, part2

            def v_unit(t4):
                def part1():
                    b = nb("proj")

                    def mm(e):
                        ins = None
                        for q in range(4):
                            tt = t4 * 4 + q
                            for kt in range(8):
                                ins = e.matmul(psb(b)[:, q * 128:(q + 1) * 128], hT[:, kt, tt * 128:(tt + 1) * 128],
                                               wv[:, 2, kt, :], start=(kt == 0), stop=(kt == 7))
                        return ins
                    P.add("pe", mm, reads=[wtok(ws, 2), ("hT", t4)], writes=[("ps", b)])
                    src = psb(b).rearrange("p (q c) -> p q c", q=4)
                    P.add("dve", lambda e: e.tensor_copy(out=Vb[ws][:, t4 * 4:(t4 + 1) * 4, 0:64], in_=src[:, :, 0:64]),
                          reads=[("ps", b)], writes=[("V", ws)])
                    P.add("dve", lambda e: e.tensor_copy(out=Vb[ws][:, t4 * 4:(t4 + 1) * 4, 128:192],
                                                         in_=src[:, :, 64:128]),
                          reads=[("ps", b)], writes=[("V", ws)])
                return part1, None, None
            if hp == 0:
                for st in range(5):
                    units.append(qk_unit(1, st))
                    units.append(v_unit(st))
                    if st >= 1:
                        units.append(qk_unit(0, st - 1))
                return units
            for st in range(5):
                units.append(qk_unit(1, st))
            for st in range(4):
                units.append(qk_unit(0, st))
            for t4 in range(5):
                units.append(v_unit(t4))
            return units

        q1b, q2 = [], []

        def drain_stages():
            if q2 and q2[0][0] <= 0:
                q2.pop(0)[1]()
            if q1b and q1b[0][0] <= 0:
                f1b, f2 = q1b.pop(0)[1:]
                f1b()
                q2.append([1, f2])
            for q in (q1b, q2):
                for ent in q:
                    ent[0] -= 1

        def push_unit(u3):
            p1a, p1b, p2 = u3
            p1a()
            if p1b is not None:
                q1b.append([0, p1b, p2])
        for n_, u3 in enumerate(proj_units(0)):
            drain_stages()
            push_unit(u3)
            for h in range(16):
                if h * 14 // 16 == n_:
                    etab_head(h)
        while q1b or q2:
            drain_stages()

        def attn_step(hp, g):
            ws = hp % 2
            pt = (hp * 16 + g) % 3
            st_ = {}

            def e_qk():
                bA, bB, bZ = nb("S"), nb("S"), nb("S")
                st_["b"] = (bA, bB, bZ)
                for hd, qq, bb in ((0, qA[ws], bA), (1, qB[ws], bB)):
                    def qk4(e, qq=qq, bb=bb):
                        ins = None
                        for j in range(4):
                            ins = e.matmul(psb(bb)[:, j * 128:(j + 1) * 128], kT[ws][:, (g + j) * 128:(g + j + 1) * 128],
                                           qq[:, g * 128:(g + 1) * 128], start=True, stop=True)
                        return ins
                    P.add("pe", qk4, reads=[("kT", ws), ("qA", ws), ("qB", ws)], writes=[("ps", bb)])

                def qkz(e):
                    ins = None
                    for hd, qq in ((0, qA[ws]), (1, qB[ws])):
                        ins = e.matmul(psb(bZ)[:, hd * 128:(hd + 1) * 128], kT[ws][:, (g + 4) * 128:(g + 5) * 128],
                                       qq[:, g * 128:(g + 1) * 128], start=True, stop=True)
                    return ins
                P.add("pe", qkz, reads=[("kT", ws), ("qA", ws), ("qB", ws)], writes=[("ps", bZ)])

            def e_softmax():
                bA, bB, bZ = st_["b"]
                P.add("act", lambda e: e.activation(
                    out=PT[pt][:, 0, 0:4, :], in_=psb(bA).rearrange("p (j q) -> p j q", j=4), func=AF.Exp, scale=0.125),
                    reads=[("ps", bA)], writes=[("PT", pt, 0)])
                P.add("act", lambda e: e.activation(
                    out=PT[pt][:, 1, 0:4, :], in_=psb(bB).rearrange("p (j q) -> p j q", j=4), func=AF.Exp, scale=0.125),
                    reads=[("ps", bB)], writes=[("PT", pt, 1)])
                P.add("act", lambda e: e.activation(
                    out=PT[pt][:, :, 4, :], in_=psb(bZ)[:, 0:256].rearrange("p (h q) -> p h q", h=2), func=AF.Exp,
                    scale=0.125),
                    reads=[("ps", bZ)], writes=[("PT", pt, 2)])
                P.add("dve", lambda e: e.tensor_tensor(
                    out=PT[pt], in0=PT[pt], in1=Etab[:, 2 * hp:2 * hp + 2, :, :], op=ALU.mult),
                    reads=[("PT", pt, 0), ("PT", pt, 1), ("PT", pt, 2), "Etab"], writes=[("PTm", pt)])

            def e_pv():
                if g % 2 == 0:
                    obank[0] = nb("O")
                bO = obank[0]
                st_["bO"] = bO
                c0 = (g % 2) * 256

                def pv(e):
                    ins = None
                    for hd in range(2):
                        for j in range(5):
                            ins = e.matmul(psb(bO)[:, c0 + hd * 128:c0 + (hd + 1) * 128],
                                           Vb[ws][:, g + j, hd * 64:hd * 64 + 128], PT[pt][:, hd, j, :],
                                           start=(j == 0), stop=(j == 4))
                    return ins
                P.add("pe", pv, reads=[("PTm", pt), ("V", ws), ("Vones", ws)],
                      writes=[("ps", bO), ("PT", pt, 0), ("PT", pt, 1), ("PT", pt, 2)])

            def e_norm():
                if g % 2 == 0:
                    return
                bO = st_["bO"]
                rc = (hp * 8 + g // 2) % 2
                O4 = psb(bO).rearrange("p (g h q) -> p g h q", g=2, h=2)
                g0 = g - 1
                ydst = yaT[:, hp, g0 * 128:(g0 + 2) * 128].rearrange("p (g q) -> p g q", g=2)
                P.add("act", lambda e: e.activation(out=lnd[rc][0:64, :, :], in_=O4[64:128, :, 0, :], func=AF.Ln),
                      reads=[("ps", bO)], writes=[("lndA", rc)])
                P.add("act", lambda e: e.activation(out=lnd[rc][64:128, :, :], in_=O4[0:64, :, 1, :], func=AF.Ln),
                      reads=[("ps", bO)], writes=[("lndB", rc)])
                P.add("act", lambda e: e.activation(out=rcp[rc], in_=lnd[rc], func=AF.Exp, scale=-1.0),
                      reads=[("lndA", rc), ("lndB", rc)], writes=[("rcp", rc)])
                P.add("dve", lambda e: e.tensor_tensor(
                    out=ydst[0:64], in0=O4[0:64, :, 0, :], in1=rcp[rc][0:64, :, :], op=ALU.mult),
                    reads=[("ps", bO), ("rcp", rc)], writes=[("yaT", g // 4, hp, 0)])
                P.add("dve", lambda e: e.tensor_tensor(
                    out=ydst[64:128], in0=O4[64:128, :, 1, :], in1=rcp[rc][64:128, :, :], op=ALU.mult),
                    reads=[("ps", bO), ("rcp", rc)], writes=[("yaT", g // 4, hp, 1)])
            return e_qk, e_softmax, e_pv, e_norm

        steps = [attn_step(hp, g) for hp in range(8) for g in range(16)]
        NS = len(steps)
        for i in range(NS + 3):
            hp, g = divmod(i, 16)
            if i < NS:
                if g == 0:
                    nxt = proj_units(hp + 1) if hp + 1 < 8 else []
                    if hp + 2 < 8:
                        load_qkv_like(hp % 2, 0, hp + 2)
                steps[i][0]()
                steps[i][1]()
            if 0 <= i - 2 < NS:
                steps[i - 2][3]()
            drain_stages()
            if i < NS and nxt:
                push_unit(nxt.pop(0))
            if 0 <= i - 1 < NS:
                steps[i - 1][2]()
        while q1b or q2:
            drain_stages()
        ya_res = [("yaT", st, hp, hd) for st in range(4) for hp in range(8) for hd in range(2)]
        if DEBUG_TAPS:
            P.add("sp", lambda e: e.dma_start(out=taps["t_ya"][:, :], in_=yaT.rearrange("p k t -> p (k t)")),
                  reads=ya_res, slot="tap")
        if STOP_AFTER == "p1":
            finish()
            return nc

        load_qkv_like(0, 3, 0)
        load_qkv_like(1, 3, 1)
        p2 = Bump(PH + 136 * KB, ARENA)
        ubuf = [p2([128, 2 + NTOK], F32) for _ in range(2)]
        xcs = [p2([128, 512], F32) for _ in range(2)]
        at = [p2([128, 512], F32) for _ in range(2)]
        xch = p2([128, 64], F32)
        ring = {"all": list(range(8))}
        rpos = {"all": 0}
        k2 = [0]
        p2_tokens = ["xch"] + [("xcs", t) for t in range(2)] + [("at", t) for t in range(2)] + \
                    [("ub", p, st) for p in range(2) for st in range(-1, 4)]
        for cc in range(8):
            ws = cc % 2
            wv = wslot(ws, 3)
            ub = ubuf[cc % 2]
            w0c, w1c, w2c = (cw[:, cc * 3 + j:cc * 3 + j + 1] for j in range(3))
            cbc = cb[:, cc:cc + 1]
            b = nb("all")

            def mmh(e, wv=wv, b=b):
                ins = None
                for which, c0 in ((2, 0), (1, 64)):
                    for kt in range(8):
                        ins = e.matmul(psb(b)[:, c0:c0 + 64], wv[:, which, kt, :], hT[:, kt, 448:512],
                                       start=(kt == 0), stop=(kt == 7))
                return ins
            P.add("pe", mmh, reads=[wtok(ws, 1), wtok(ws, 2), ("hT", 0)], writes=[("ps", b)])
            P.add("act", lambda e, b=b: e.activation(out=xch, in_=psb(b)[:, 0:64], func=AF.Copy),
                  reads=[("ps", b)], writes=["xch"])
            P.add("dve", lambda e, b=b, ub=ub: e.tensor_tensor(out=ub[:, 0:2], in0=psb(b)[:, 126:128],
                                                              in1=xch[:, 62:64], op=ALU.mult),
                  reads=[("ps", b), "xch"], writes=[("ub", cc % 2, -1)])
            for st in range(4):
                loc = st + 1
                bx, bc, bbk = nb("all"), nb("all"), nb("all")

                def mm3(e, wv=wv, loc=loc, bx=bx, bc=bc, bbk=bbk):
                    ins = None
                    for which, bb in ((2, bx), (1, bc), (0, bbk)):
                        for kt in range(8):
                            ins = e.matmul(psb(bb), wv[:, which, kt, :], hT[:, kt, loc * 512:(loc + 1) * 512],
                                           start=(kt == 0), stop=(kt == 7))
                    return ins
                P.add("pe", mm3, reads=[wtok(ws, 0), wtok(ws, 1), wtok(ws, 2), ("hT", loc)],
                      writes=[("ps", bx), ("ps", bc), ("ps", bbk)])
                t = k2[0] % 2
                k2[0] += 1
                o = 2 + st * 512
                P.add("act", lambda e, t=t, bx=bx: e.activation(out=xcs[t], in_=psb(bx), func=AF.Copy),
                      reads=[("ps", bx)], writes=[("xcs", t)])
                P.add("dve", lambda e, t=t, bc=bc, ub=ub, o=o: e.tensor_tensor(
                    out=ub[:, o:o + 512], in0=psb(bc), in1=xcs[t], op=ALU.mult),
                    reads=[("ps", bc), ("xcs", t)], writes=[("ub", cc % 2, st)])
                P.add("act", lambda e, t=t, ub=ub, o=o, w2c=w2c, cbc=cbc: e.activation(
                    out=at[t], in_=ub[:, o:o + 512], func=AF.Identity, scale=w2c, bias=cbc),
                    reads=[("ub", cc % 2, st), "cst"], writes=[("at", t)])
                P.add("dve", lambda e, t=t, ub=ub, o=o, w1c=w1c: e.scalar_tensor_tensor(
                    out=at[t], in0=ub[:, o - 1:o + 511], scalar=w1c, in1=at[t], op0=ALU.mult, op1=ALU.add),
                    reads=[("ub", cc % 2, st), ("ub", cc % 2, st - 1), ("at", t)], writes=[("at", t)])
                P.add("dve", lambda e, t=t, ub=ub, o=o, w0c=w0c: e.scalar_tensor_tensor(
                    out=at[t], in0=ub[:, o - 2:o + 510], scalar=w0c, in1=at[t], op0=ALU.mult, op1=ALU.add),
                    reads=[("ub", cc % 2, st), ("ub", cc % 2, st - 1), ("at", t)], writes=[("at", t)])
                P.add("dve", lambda e, t=t, bbk=bbk, cc=cc, st=st: e.tensor_tensor(
                    out=ycT[:, cc, st * 512:(st + 1) * 512], in0=psb(bbk), in1=at[t], op=ALU.mult),
                    reads=[("ps", bbk), ("at", t)], writes=[("ycT", st)])
            if cc + 2 < 8:
                load_qkv_like(cc % 2, 3, cc + 2)
        if DEBUG_TAPS:
            P.add("sp", lambda e: e.dma_start(out=taps["t_yc"][:, :], in_=ycT.rearrange("p k t -> p (k t)")),
                  reads=[("ycT", st) for st in range(4)], slot="tap")
        if STOP_AFTER == "p2":
            finish()
            return nc

        def load_p3(i, oc):
            wv = wslot(i, 4)
            cs = slice(oc * 128, (oc + 1) * 128)
            P.add("pool", lambda e: e.dma_start(out=wv[:, 0, :, :], in_=w_ap_v[:, :, cs]), writes=[("w", i)],
                  slot=f"w{i}")
            P.add("pool", lambda e: e.dma_start(out=wv[:, 1, :, :], in_=w_cp_v[:, :, cs]), writes=[("w", i, 1)],
                  slot=f"w{i}")
            P.add("pool", lambda e: e.dma_start(out=wv[:, 2, :, :], in_=w_gate_v[:, 0, :, cs]), writes=[("w", i, 2)],
                  slot=f"w{i}")
            P.add("pool", lambda e: e.dma_start(out=wv[:, 3, :, :], in_=w_gate_v[:, 1, :, cs]), writes=[("w", i, 3)],
                  slot=f"w{i}")
        load_p3(0, 0)
        load_p3(1, 1)
        sga = [view(WB + 8 * KB + i * 16 * KB, [128, 512], F32) for i in range(2)]
        sgc = [view(WB + 10 * KB + i * 16 * KB, [128, 512], F32) for i in range(2)]
        m1 = [view(WB + 12 * KB + i * 16 * KB, [128, 512], F32) for i in range(2)]
        m2 = [view(WB + 14 * KB + i * 16 * KB, [128, 512], F32) for i in range(2)]

        k3 = [0]
        for oc in range(8):
            ws = oc % 2
            wv = wslot(ws, 4)
            for st in range(4):
                loc = st + 1
                b1, b2, b3, b4 = nb("all"), nb("all"), nb("all"), nb("all")

                for which, bb, src, c0, rd in ((2, b1, hT, loc * 512, [("w", ws, 2), ("hT", loc)]),
                                               (3, b2, hT, loc * 512, [("w", ws, 3), ("hT", loc)]),
                                               (0, b3, yaT, st * 512, [("w", ws)] + ya_res),
                                               (1, b4, ycT, st * 512, [("w", ws, 1), ("ycT", st)])):
                    def mm1(e, wv=wv, which=which, bb=bb, src=src, c0=c0):
                        ins = None
                        for kt in range(8):
                            ins = e.matmul(psb(bb), wv[:, which, kt, :], src[:, kt, c0:c0 + 512],
                                           start=(kt == 0), stop=(kt == 7))
                        return ins
                    P.add("pe", mm1, reads=rd, writes=[("ps", bb)])
                t = k3[0] % 2
                k3[0] += 1
                P.add("act", lambda e, t=t, b1=b1, oc=oc: e.activation(out=sga[t], in_=psb(b1), func=AF.Sigmoid,
                                                                       bias=bgt[:, oc:oc + 1]),
                      reads=[("ps", b1), "cst"], writes=[("sga", t)])
                P.add("act", lambda e, t=t, b2=b2, oc=oc: e.activation(out=sgc[t], in_=psb(b2), func=AF.Sigmoid,
                                                                       bias=bgt[:, 8 + oc:9 + oc]),
                      reads=[("ps", b2), "cst"], writes=[("sgc", t)])
                P.add("dve", lambda e, t=t, b3=b3: e.tensor_tensor(out=m1[t], in0=psb(b3), in1=sga[t], op=ALU.mult),
                      reads=[("ps", b3), ("sga", t)], writes=[("m1", t)])
                P.add("dve", lambda e, t=t, b4=b4: e.tensor_tensor(out=m2[t], in0=psb(b4), in1=sgc[t], op=ALU.mult),
                      reads=[("ps", b4), ("sgc", t)], writes=[("m2", t)])
                P.add("pool", lambda e, t=t, oc=oc, st=st: e.tensor_tensor(
                    out=mT[:, oc, st * 512:(st + 1) * 512], in0=m1[t], in1=m2[t], op=ALU.add),
                    reads=[("m1", t), ("m2", t)], writes=[("mT", st)])
            if oc + 2 < 8:
                load_p3(oc % 2, oc + 2)
        if DEBUG_TAPS:
            P.add("sp", lambda e: e.dma_start(out=taps["t_m"][:, :], in_=mT.rearrange("p k t -> p (k t)")),
                  reads=[("mT", st) for st in range(4)], slot="tap")
        if STOP_AFTER == "p3a":
            finish()
            return nc

        WoA = view(WB, [128, 4, D], BF16)
        WoB = view(PH + 152 * KB, [128, 4, D], BF16)
        P.add("pool", lambda e: e.dma_start(out=WoA, in_=w_out_v[:, 0:4, :]),
              writes=[("w", 0), ("w", 0, 1), ("w", 0, 2), ("w", 0, 3)], slot="w0")
        P.add("pool", lambda e: e.dma_start(out=WoB, in_=w_out_v[:, 4:8, :]),
              writes=["wo2"] + p2_tokens, slot="wd0")
        P.barrier()
        P.add("sp", lambda e: e.dma_start(out=gbb, in_=g2b_d[:, :]), writes=["gbb"], slot="const")
        def wu_slot(i):
            return view(WB + i * 16 * KB, [128, 8, 512], BF16)

        def wd_slot(i):
            return view(WB + i * 16 * KB + 8 * KB, [128, 4, D], BF16)

        def load_wu(i, grp):
            wu = wu_slot(i)
            P.add("pool", lambda e: e.dma_start(out=wu, in_=w_up_v[:, :, grp * 512:(grp + 1) * 512]),
                  writes=[("w", i)], slot=f"w{i}")

        def load_wd(i, grp):
            wd = wd_slot(i)
            P.add("pool", lambda e: e.dma_start(out=wd, in_=w_down_v[:, grp * 4:(grp + 1) * 4, :]),
                  writes=[("w", i, 1)], slot=f"wd{i}")

        def load_p4(i, grp):
            load_wu(i, grp)
            load_wd(i, grp)
        load_p4(1, 0)
        p3b = Bump(PH + 96 * KB, PH + 104 * KB)
        xr = [p3b([128, D], F32) for _ in range(2)]
        p3c = Bump(PH + 136 * KB, ARENA)
        h2b = [p3c([128, D], BF16) for _ in range(2)]
        junk2 = p3c([128, D], BF16)
        def p3b_A(tt):
            s = tt % 2
            P.add("sp", lambda e: e.dma_start(out=xr[s], in_=xh[HALO + tt * 128:HALO + (tt + 1) * 128, :]),
                  writes=[("xr", s)], slot=f"xt{s}")
            for half in range(2):
                b = nb("all")

                def mmo(e, half=half, b=b):
                    ins = None
                    for kt in range(8):
                        wsrc = WoA[:, kt, :] if kt < 4 else WoB[:, kt - 4, :]
                        ins = e.matmul(psb(b), mT[:, kt, tt * 128:(tt + 1) * 128], wsrc[:, half * 512:(half + 1) * 512],
                                       start=(kt == 0), stop=(kt == 7))
                    return ins
                P.add("pe", mmo, reads=[("w", 0), "wo2", ("mT", tt // 4)], writes=[("ps", b)])
                P.add("dve", lambda e, half=half, b=b: e.tensor_tensor(
                    out=x1[:, tt, half * 512:(half + 1) * 512], in0=psb(b), in1=xr[s][:, half * 512:(half + 1) * 512],
                    op=ALU.add), reads=[("ps", b), ("xr", s)], writes=[("x1", tt, half)])
            P.add("act", lambda e: e.activation(out=junk2, in_=x1[:, tt, :], func=AF.Square, scale=1.0 / 32.0,
                                                accum_out=ms2[:, tt:tt + 1]),
                  reads=[("x1", tt, 0), ("x1", tt, 1), "ms_init", "junk2"], writes=[("ms2", tt), "junk2"])

        def p3b_A2(tt):
            rsqrt_act(rstd2[:, tt:tt + 1], ms2[:, tt:tt + 1], lnt[:, tt:tt + 1], [("ms2", tt)], [("rstd2", tt)],
                      ("lnt2", tt))

        trb = {}

        def p3b_B1(tt):
            s = tt % 2
            P.add("dve", lambda e: e.scalar_tensor_tensor(
                out=h2b[s], in0=x1[:, tt, :], scalar=rstd2[:, tt:tt + 1], in1=gbb, op0=ALU.mult, op1=ALU.mult),
                reads=[("x1", tt, 0), ("x1", tt, 1), ("rstd2", tt), "gbb"], writes=[("h2b", s)])

        def p3b_B2(tt):
            s = tt % 2
            b = nb("all")
            trb[tt] = b

            def tr2(e):
                pbf = psb(b).bitcast(BF16)
                ins = None
                for kt in range(8):
                    ins = e.transpose(out=pbf[:, kt * 128:(kt + 1) * 128], in_=h2b[s][:, kt * 128:(kt + 1) * 128],
                                      identity=ident)
                return ins
            P.add("pe", tr2, reads=[("h2b", s), "cmat"], writes=[("ps", b)])

        def p3b_C(tt):
            b = trb[tt]
            P.add("dve", lambda e: e.tensor_copy(
                out=h2T[:, :, tt * 128:(tt + 1) * 128],
                in_=psb(b).bitcast(BF16).rearrange("p (k t) -> p k t", k=8)),
                reads=[("ps", b)], writes=[("h2T", tt // 4)])
        for i in range(16 + 4):
            if 0 <= i - 3 < 16:
                p3b_B1(i - 3)
            if i < 16:
                p3b_A(i)
            if 0 <= i - 1 < 16:
                p3b_A2(i - 1)
            if 0 <= i - 3 < 16:
                p3b_B2(i - 3)
            if 0 <= i - 4 < 16:
                p3b_C(i - 4)
        load_p4(0, 1)
        x1_res = [("x1", tt, half) for tt in range(16) for half in range(2)]
        if DEBUG_TAPS:
            P.add("sp", lambda e: e.dma_start(out=taps["t_x1"][:, :], in_=x1.rearrange("p k t -> p (k t)")),
                  reads=x1_res, slot="tap")
            P.add("sp", lambda e: e.dma_start(out=taps["t_h2"][:, :], in_=h2T.rearrange("p k t -> p (k t)")),
                  reads=[("h2T", st) for st in range(4)], slot="tap")
        if STOP_AFTER == "p3b":
            finish()
            return nc

        P.barrier()
        p4 = Bump(PH + 104 * KB, ARENA)
        uT = [p4([128, 4, NTOK], BF16) for _ in range(2)]
        rl = [p4([128, 512], F32) for _ in range(3)]

        k4 = [0]

        def up_phase(grp):
            ws = grp % 2
            wl = (grp + 1) % 2
            wu = wu_slot(wl)
            for fc in range(4):
                for st in range(4):
                    b = nb("all")

                    def mmu(e, wu=wu, fc=fc, st=st, b=b):
                        ins = None
                        for kt in range(8):
                            ins = e.matmul(psb(b), wu[:, kt, fc * 128:(fc + 1) * 128], h2T[:, kt, st * 512:(st + 1) * 512],
                                           start=(kt == 0), stop=(kt == 7))
                        return ins
                    P.add("pe", mmu, reads=[("w", wl), ("h2T", st)], writes=[("ps", b)])
                    t = k4[0] % 3
                    k4[0] += 1
                    P.add("act", lambda e, t=t, b=b: e.activation(out=rl[t], in_=psb(b), func=AF.Relu),
                          reads=[("ps", b)], writes=[("rl", t)])
                    P.add("pool", lambda e, t=t, ws=ws, fc=fc, st=st: e.tensor_tensor(
                        out=uT[ws][:, fc, st * 512:(st + 1) * 512], in0=rl[t], in1=rl[t], op=ALU.mult),
                        reads=[("rl", t)], writes=[("uT", ws, st)])

        def down_phase(grp):
            ws = grp % 2
            wl = (grp + 1) % 2
            wd = wd_slot(wl)
            for tt in range(16):
                for half in range(2):
                    b = nb("all")

                    def mmd(e, wd=wd, ws=ws, tt=tt, half=half, b=b):
                        ins = None
                        for fc in range(4):
                            ins = e.matmul(psb(b), uT[ws][:, fc, tt * 128:(tt + 1) * 128],
                                           wd[:, fc, half * 512:(half + 1) * 512], start=(fc == 0), stop=(fc == 3))
                        return ins
                    P.add("pe", mmd, reads=[("w", wl, 1), ("uT", ws, tt // 4)], writes=[("ps", b)])
                    P.add("dve", lambda e, tt=tt, half=half, b=b: e.tensor_tensor(
                        out=x1[:, tt, half * 512:(half + 1) * 512], in0=psb(b), in1=x1[:, tt, half * 512:(half + 1) * 512],
                        op=ALU.add), reads=[("ps", b), ("x1", tt, half)], writes=[("x1", tt, half)])
                if grp == 7:
                    P.add("sp", lambda e, tt=tt: e.dma_start(out=y[tt * 128:(tt + 1) * 128, :], in_=x1[:, tt, :]),
                          reads=[("x1", tt, 0), ("x1", tt, 1)], slot="out")
            if grp + 2 < 8:
                load_wd(wl, grp + 2)

        up_phase(0)
        for grp in range(8):
            if grp + 1 < 8:
                up_phase(grp + 1)
            if grp + 2 < 8:
                load_wu((grp + 1) % 2, grp + 2)
            down_phase(grp)
        finish()
    return nc


_CACHE = {}


def _host_consts(q_norm_g, k_norm_g, rel_bias, conv_w, conv_b, b_gate, norm1_g, norm2_g):
    cst = np.zeros((8, 128, 72), np.float32)
    p = np.arange(128)
    cst[:, :, 0] = q_norm_g[p % 64]
    cst[:, :, 1] = k_norm_g[p % 64]
    cwv = conv_w.reshape(3, 8, 128)
    cst[:, :, 2:26] = np.transpose(cwv, (2, 1, 0)).reshape(128, 24)
    cst[:, :, 26:34] = conv_b.reshape(8, 128).T
    cst[:, :, 34:50] = b_gate.reshape(16, 128).T
    cst[:, :, 50:70] = 1.0
    for c in range(8):
        if c % 2 == 0:
            cst[c, :, 50:54] = 0.0
    cmat = np.zeros((128, 256), np.float32)
    cmat[:, 0:128] = np.eye(128, dtype=np.float32)
    cmat[:, 128:256] = (p[:, None] // 64 == p[None, :] // 64).astype(np.float32) / 64.0
    kk = np.arange(128)[:, None, None]
    j = np.arange(5)[None, :, None]
    qq = np.arange(128)[None, None, :]
    dist = 512 - 128 * j + qq - kk
    idx = np.clip(dist, -256, 256) + 256
    bias = rel_bias[:, idx]
    cdiff = (8 - 2 * j + qq // 64) - (kk // 64)
    valid = (cdiff >= 0) & (cdiff <= 8)
    bias = np.where(valid[None], bias, np.float32(-1e30)).astype(np.float32)
    biasT = np.ascontiguousarray(np.transpose(bias, (1, 0, 2, 3))).reshape(128, 16 * 640)
    g1b = np.ascontiguousarray(np.broadcast_to(norm1_g[None, :], (128, D))).astype(np.float32)
    g2b = np.ascontiguousarray(np.broadcast_to(norm2_g[None, :], (128, D))).astype(np.float32)
    return cst, cmat, biasT, g1b, g2b


def kernel(x, norm1_g, w_in, q_norm_g, k_norm_g, rel_bias, conv_w, conv_b, w_attn_proj, w_conv_proj,
           w_gate, b_gate, w_out, norm2_g, w_up, w_down):
    f = lambda a: np.ascontiguousarray(np.asarray(a, dtype=np.float32))
    x = f(x)
    B, S, _ = x.shape
    cst, cmat, biasT, g1b, g2b = _host_consts(f(q_norm_g), f(k_norm_g), f(rel_bias), f(conv_w), f(conv_b),
                                              f(b_gate), f(norm1_g), f(norm2_g))
    if "nc" not in _CACHE:
        _CACHE["nc"] = build_program()
    nc = _CACHE["nc"]
    shared = {"w_in": f(w_in), "w_gate": f(w_gate), "w_ap": f(w_attn_proj), "w_cp": f(w_conv_proj),
              "w_out": f(w_out), "w_up": f(w_up), "w_down": f(w_down), "g1b": g1b, "g2b": g2b, "cmat": cmat,
              "biasT": biasT}
    in_maps = []
    for c in range(N_RUN):
        b, half = c // 2, c % 2
        xh = np.zeros((NLOC, D), np.float32)
        if half == 0:
            xh[HALO:] = x[b, 0:NTOK]
        else:
            xh[:] = x[b, NTOK - HALO:2 * NTOK]
        m = dict(shared)
        m["xh"] = xh
        m["cst"] = cst[c]
        in_maps.append(m)
    res = run_bass_kernel_spmd(nc, in_maps, core_ids=list(range(N_RUN)))
    _CACHE["last"] = res
    out = np.zeros((B, S, D), np.float32)
    for c in range(N_RUN):
        b, half = c // 2, c % 2
        out[b, half * NTOK:(half + 1) * NTOK] = res.results[c]["y"]
    return out
```

```python
import numpy as np
import concourse.bass as bass
import concourse.mybir as mybir
from concourse.bass_utils import run_bass_kernel_spmd

F32 = mybir.dt.float32
BF16 = mybir.dt.bfloat16
U8 = mybir.dt.uint8
AF = mybir.ActivationFunctionType
ALU = mybir.AluOpType

D = 1024
NTOK = 2048
HALO = 512
NLOC = NTOK + HALO
NTT = NLOC // 128
EPS = 1e-6
KB = 1024

DEBUG_TAPS = False
N_RUN = 8
STOP_AFTER = None


class _Op:
    __slots__ = ("eng", "fn", "raw", "oth", "idx", "sig", "sigidx", "slot", "dmawait", "barrier")


class Prog:
    ENGS = ("pe", "act", "dve", "pool", "sp")

    def __init__(self):
        self.ops = []
        self.last_writer = {}
        self.readers = {}
        self.slot_count = {}
        self.last_on_eng = {}

    def add(self, eng, fn, reads=(), writes=(), slot=None):
        op = _Op()
        op.eng, op.fn, op.idx, op.sig, op.slot, op.barrier = eng, fn, len(self.ops), False, slot, False
        op.raw, op.oth, op.dmawait = set(), set(), {}
        for r in reads:
            w = self.last_writer.get(r)
            if w is not None:
                op.raw.add(w)
        for r in writes:
            w = self.last_writer.get(r)
            if w is not None:
                op.oth.add(w)
            for rd in self.readers.get(r, ()):
                op.oth.add(rd)
        for r in reads:
            self.readers.setdefault(r, []).append(op.idx)
        for r in writes:
            self.last_writer[r] = op.idx
            self.readers[r] = []
        for d in list(op.raw | op.oth):
            dop = self.ops[d]
            if dop.slot is not None:
                op.dmawait[dop.slot] = self.slot_count[dop.slot]
        if slot is not None:
            self.slot_count[slot] = self.slot_count.get(slot, 0) + 1
        self.ops.append(op)
        self.last_on_eng[eng] = op.idx
        return op

    def barrier(self):
        lasts = dict(self.last_on_eng)
        slots = dict(self.slot_count)
        for eng in self.ENGS:
            op = _Op()
            op.eng, op.fn, op.idx, op.sig, op.slot, op.barrier = eng, None, len(self.ops), False, None, True
            op.raw = set(v for k, v in lasts.items())
            op.oth = set()
            op.dmawait = dict(slots)
            self.ops.append(op)
        self.last_writer = {}
        self.readers = {}

    def emit(self, nc, engobjs, sems, slot_sems):
        ops = self.ops
        for op in ops:
            for d in op.raw | op.oth:
                dop = ops[d]
                if dop.slot is not None:
                    continue
                if dop.eng == op.eng and not op.barrier:
                    if op.eng == "pe" or op.eng == "sp":
                        continue
                    if d not in op.raw:
                        continue
                dop.sig = True
        cnt = {e: 0 for e in self.ENGS}
        for op in ops:
            if op.sig:
                cnt[op.eng] += 1
                op.sigidx = cnt[op.eng]
        per_eng = {e: [] for e in self.ENGS}
        for op in ops:
            per_eng[op.eng].append(op)

        def run(eng):
            e = engobjs[eng]
            waited = {}
            for op in per_eng[eng]:
                need = {}
                for d in op.raw | op.oth:
                    dop = ops[d]
                    if dop.slot is not None or not dop.sig:
                        continue
                    if dop.eng == eng and not op.barrier:
                        if eng == "pe" or eng == "sp" or d not in op.raw:
                            continue
                    if dop.eng == eng and op.barrier:
                        continue
                    key = ("e", dop.eng)
                    need[key] = max(need.get(key, 0), dop.sigidx)
                for s, c in op.dmawait.items():
                    key = ("s", s)
                    need[key] = max(need.get(key, 0), 16 * c)
                for key, v in need.items():
                    if waited.get(key, 0) >= v:
                        continue
                    waited[key] = v
                    sem = sems[key[1]] if key[0] == "e" else slot_sems[key[1]]
                    e.wait_ge(sem, v)
                if op.fn is None:
                    continue
                ins = op.fn(e)
                if op.slot is not None:
                    ins.then_inc(slot_sems[op.slot], 16)
                elif op.sig:
                    ins.then_inc(sems[eng], 1)
        return run


def build_program():
    nc = bass.Bass("TRN2", target_bir_lowering=False)

    def din(name, shape):
        return nc.dram_tensor(name, list(shape), F32, kind="ExternalInput").ap()

    xh = din("xh", [NLOC, D])
    w_in = din("w_in", [D, 6 * D])
    w_gate = din("w_gate", [D, 2 * D])
    w_ap = din("w_ap", [D, D])
    w_cp = din("w_cp", [D, D])
    w_out = din("w_out", [D, D])
    w_up = din("w_up", [D, 4 * D])
    w_down = din("w_down", [4 * D, D])
    g1b_d = din("g1b", [128, D])
    g2b_d = din("g2b", [128, D])
    cst_d = din("cst", [128, 72])
    cmat_d = din("cmat", [128, 256])
    bias_d = din("biasT", [128, 16 * 640])
    y = nc.dram_tensor("y", [NTOK, D], F32, kind="ExternalOutput").ap()
    taps = {}
    if DEBUG_TAPS:
        def dout(name, shape, dt):
            taps[name] = nc.dram_tensor(name, list(shape), dt, kind="ExternalOutput").ap()
        dout("t_hT", [128, 8 * NLOC], BF16)
        dout("t_ya", [128, 8 * NTOK], BF16)
        dout("t_yc", [128, 8 * NTOK], BF16)
        dout("t_m", [128, 8 * NTOK], BF16)
        dout("t_x1", [128, 16 * D], F32)
        dout("t_h2", [128, 8 * NTOK], BF16)

    w_in_v = w_in.rearrange("(kt p) (s n) -> p s kt n", p=128, s=6)
    w_gate_v = w_gate.rearrange("(kt p) (s n) -> p s kt n", p=128, s=2)
    w_ap_v = w_ap.rearrange("(kt p) n -> p kt n", p=128)
    w_cp_v = w_cp.rearrange("(kt p) n -> p kt n", p=128)
    w_out_v = w_out.rearrange("(kt p) n -> p kt n", p=128)
    w_up_v = w_up.rearrange("(kt p) n -> p kt n", p=128)
    w_down_v = w_down.rearrange("(fc p) n -> p fc n", p=128)

    ARENA = 207 * KB
    P = Prog()

    from contextlib import ExitStack
    with ExitStack() as es:
        arena = es.enter_context(nc.sbuf_tensor("arena", [128, ARENA], U8))
        banks = [es.enter_context(nc.psum_tensor(f"bank{i}", [128, 512], F32)) for i in range(8)]
        sems = {e: es.enter_context(nc.semaphore(f"sem_{e}")) for e in Prog.ENGS}
        slot_names = ["const", "constp", "xt0", "xt1", "xt2", "xt3", "xt4", "xt5", "w0", "w1", "wd0", "wd1", "bias0", "bias1", "out", "tap"]
        slot_sems = {s: es.enter_context(nc.semaphore(f"slot_{s}")) for s in slot_names}
        block = es.enter_context(nc.Block())

        def view(off, shape, dt):
            esz = 4 if dt == F32 else 2
            n = int(np.prod(shape[1:]))
            assert off % 4 == 0 and off + n * esz <= ARENA, (off, shape)
            v = arena[:, off:off + n * esz].bitcast(dt)
            if len(shape) == 3:
                v = v.rearrange("p (a b) -> p a b", a=shape[1])
            elif len(shape) == 4:
                v = v.rearrange("p (a b c) -> p a b c", a=shape[1], b=shape[2])
            return v

        class Bump:
            def __init__(self, base, limit):
                self.o, self.limit = base, limit

            def __call__(self, shape, dt):
                esz = 4 if dt == F32 else 2
                n = int(np.prod(shape[1:])) * esz
                n4 = (n + 63) // 64 * 64
                off = self.o
                self.o += n4
                assert self.o <= self.limit, ("sbuf overflow", self.o, self.limit)
                return view(off, shape, dt)

        def psb(b):
            return banks[b][:, :]

        gb = Bump(0, 12 * KB)
        cst = gb([128, 72], F32)
        cmat = gb([128, 256], BF16)
        gbb = gb([128, D], F32)
        ms = gb([128, NTT], F32)
        rstd = gb([128, NTT], F32)
        ms2 = gb([128, 16], F32)
        rstd2 = gb([128, 16], F32)
        lnt = gb([128, NTT], F32)
        epsc = gb([128, 1], F32)
        ginv = gb([128, 2], F32)
        gsgn = gb([128, 2], F32)
        glog = gb([128, 2], F32)
        ident = cmat[:, 0:128]
        blk = cmat[:, 128:256]
        gq = cst[:, 0:1]
        gk = cst[:, 1:2]
        cw = cst[:, 2:26]
        cb = cst[:, 26:34]
        bgt = cst[:, 34:50]
        vmask = cst[:, 50:70]
        WB = 12 * KB
        PH = 44 * KB
        hT = view(PH, [128, 8, NLOC], BF16)
        yaT = view(PH + 40 * KB, [128, 8, NTOK], BF16)
        ycT = view(PH + 72 * KB, [128, 8, NTOK], BF16)
        mT = view(PH + 104 * KB, [128, 8, NTOK], BF16)
        x1 = view(PH, [128, 16, D], F32)
        h2T = view(PH + 64 * KB, [128, 8, NTOK], BF16)

        P.add("sp", lambda e: e.dma_start(out=cst, in_=cst_d[:, :]), writes=["cst"], slot="const")
        P.add("sp", lambda e: e.dma_start(out=gbb, in_=g1b_d[:, :]), writes=["gbb"], slot="const")
        P.add("pool", lambda e: e.dma_start(out=cmat, in_=cmat_d[:, :]), writes=["cmat"], slot="constp")
        P.add("dve", lambda e: e.memset(ms, 0.0), writes=["ms_init"])
        P.add("dve", lambda e: e.memset(ms2, 0.0), writes=["ms_init"])

        P.add("dve", lambda e: e.memset(epsc, EPS), writes=["epsc"])
        P.add("act", lambda e: e.activation(out=gsgn, in_=cst[:, 0:2], func=AF.Sign), reads=["cst"], writes=["gsgn"])
        P.add("dve", lambda e: e.tensor_tensor(out=ginv, in0=cst[:, 0:2], in1=gsgn, op=ALU.mult),
              reads=["cst", "gsgn"], writes=["gabs"])
        P.add("dve", lambda e: e.tensor_scalar_max(out=ginv, in0=ginv, scalar1=1e-30), reads=["gabs"], writes=["gabs2"])
        P.add("act", lambda e: e.activation(out=glog, in_=ginv, func=AF.Ln), reads=["gabs2"], writes=["glog"])

        def rsqrt_act(out, in_, tmp, reads, writes, tmptok):
            P.add("act", lambda e: e.activation(out=tmp, in_=in_, func=AF.Ln, bias=epsc[0:tmp.shape[0], :]),
                  reads=list(reads) + ["epsc"], writes=[tmptok])
            P.add("act", lambda e: e.activation(out=out, in_=tmp, func=AF.Exp, scale=-0.5),
                  reads=[tmptok], writes=list(writes))

        def wslot(i, S):
            return view(WB + i * 16 * KB, [128, S, 8, 128], BF16)

        def wtok(i, k):
            return ("w", i) if k == 0 else ("w", i, k)

        def load_qkv_like(i, sbase, cchunk):
            wv = wslot(i, 3)
            for k in range(3):
                src = w_in_v[:, sbase + k, :, cchunk * 128:(cchunk + 1) * 128]
                P.add("pool", lambda e, k=k, src=src: e.dma_start(out=wv[:, k, :, :], in_=src),
                      writes=[wtok(i, k)], slot=f"w{i}")

        p1 = Bump(PH + 72 * KB, ARENA)
        Etab = p1([128, 16, 5, 128], BF16)
        PT = [p1([128, 2, 5, 128], BF16) for _ in range(3)]
        rcp = [p1([128, 2, 128], F32) for _ in range(2)]
        lnd = [p1([128, 2, 128], F32) for _ in range(2)]
        qA, qB, kT, Vb = [None, None], [None, None], [None, None], [None, None]
        qA[1] = p1([128, NTOK], BF16)
        qB[1] = p1([128, NTOK], BF16)
        kT[1] = p1([128, NLOC], BF16)
        Vb[1] = p1([128, NTT, 192], BF16)
        assert p1.o <= PH + 136 * KB
        qA[0] = p1([128, NTOK], BF16)
        qB[0] = p1([128, NTOK], BF16)
        kT[0] = p1([128, NLOC], BF16)
        Vb[0] = p1([128, NTT, 192], BF16)
        assert PH + 124 * KB <= PH + 136 * KB and p1.o >= PH + 136 * KB
        qsb = [p1([128, 512], F32) for _ in range(3)]
        sqb = [p1([128, 512], BF16) for _ in range(3)]
        rsb = [p1([128, 512], F32) for _ in range(3)]
        obank = [None]
        bstage = [view(WB + 8 * KB + i * 16 * KB, [128, 640], F32) for i in range(2)]

        pb = Bump(PH + 40 * KB, PH + 72 * KB)
        xt = [pb([128, D], F32) for _ in range(6)]
        xsb = [pb([128, D], BF16) for _ in range(2)]
        junk = pb([128, D], BF16)
        load_qkv_like(0, 0, 0)
        load_qkv_like(1, 0, 1)
        def pro_A(tt):
            s = tt % 6
            P.add("sp", lambda e: e.dma_start(out=xt[s], in_=xh[tt * 128:(tt + 1) * 128, :]),
                  writes=[("xt", s)], slot=f"xt{s}")
            P.add("act", lambda e: e.activation(out=junk, in_=xt[s], func=AF.Square, scale=1.0 / 32.0,
                                                accum_out=ms[:, tt:tt + 1]),
                  reads=[("xt", s), "ms_init", "junk"], writes=[("ms", tt), "junk"])

        def pro_A2(tt):
            rsqrt_act(rstd[:, tt:tt + 1], ms[:, tt:tt + 1], lnt[:, tt:tt + 1], [("ms", tt)], [("rstd", tt)], ("lnt", tt))

        def pro_B(tt):
            s, s2, b = tt % 6, tt % 2, tt % 2
            P.add("dve", lambda e: e.scalar_tensor_tensor(
                out=xsb[s2], in0=xt[s], scalar=rstd[:, tt:tt + 1], in1=gbb, op0=ALU.mult, op1=ALU.mult),
                reads=[("xt", s), ("rstd", tt), "gbb"], writes=[("xsb", s2)])

            def tr(e):
                pbf = psb(b).bitcast(BF16)
                ins = None
                for kt in range(8):
                    ins = e.transpose(out=pbf[:, kt * 128:(kt + 1) * 128], in_=xsb[s2][:, kt * 128:(kt + 1) * 128],
                                      identity=ident)
                return ins
            P.add("pe", tr, reads=[("xsb", s2), "cmat"], writes=[("ps", b)])

        def pro_C(tt):
            b = tt % 2
            P.add("dve", lambda e: e.tensor_copy(
                out=hT[:, :, tt * 128:(tt + 1) * 128],
                in_=psb(b).bitcast(BF16).rearrange("p (k t) -> p k t", k=8)),
                reads=[("ps", b)], writes=[("hT", tt // 4)])
        def etab_head(h):
            s = h % 2
            P.add("sp", lambda e: e.dma_start(out=bstage[s], in_=bias_d[:, h * 640:(h + 1) * 640]),
                  writes=[("bst", s)], slot=f"bias{s}")
            P.add("act", lambda e: e.activation(
                out=Etab[:, h, :, :], in_=bstage[s].rearrange("p (j q) -> p j q", j=5), func=AF.Exp),
                reads=[("bst", s)], writes=["Etab"])
        for i in range(NTT + 3):
            if i < NTT:
                pro_A(i)
            if 0 <= i - 1 < NTT:
                pro_A2(i - 1)
            if 0 <= i - 2 < NTT:
                pro_B(i - 2)
            if 0 <= i - 3 < NTT:
                pro_C(i - 3)
        if DEBUG_TAPS:
            P.add("sp", lambda e: e.dma_start(out=taps["t_hT"][:, :], in_=hT.rearrange("p k t -> p (k t)")),
                  reads=[("hT", i) for i in range(5)], slot="tap")

        def finish():
            P.barrier()
            engobjs = {}
            run = P.emit(nc, engobjs, sems, slot_sems)

            def mk(eng):
                def f(e):
                    engobjs[eng] = e
                    run(eng)
                return f
            block.tensor(mk("pe"))
            block.scalar(mk("act"))
            block.vector(mk("dve"))
            block.gpsimd(mk("pool"))
            block.sync(mk("sp"))

        if STOP_AFTER == "pro":
            finish()
            return nc

        for s in range(2):
            P.add("pool", lambda e, s=s: e.memset(qA[s][64:128, :], 0.0), writes=[("qA", s)])
            P.add("pool", lambda e, s=s: e.memset(qB[s][0:64, :], 0.0), writes=[("qB", s)])
            P.add("dve", lambda e, s=s: e.tensor_copy(
                out=Vb[s][:, :, 64:128], in_=vmask.unsqueeze(2).to_broadcast([128, NTT, 64])),
                reads=["cst"], writes=[("Vones", s)])

        ring = {"proj": [0, 1, 2], "S": [3, 4, 5], "O": [6, 7]}
        rpos = {"proj": 0, "S": 0, "O": 0}

        def nb(kind):
            b = ring[kind][rpos[kind] % len(ring[kind])]
            rpos[kind] += 1
            return b

        tcnt = [0]

        def proj_units(hp):
            ws = hp % 2
            wv = wslot(ws, 3)
            units = []

            def qk_unit(which, st):
                st_ = {}

                def part1a():
                    b = nb("proj")
                    st_["b"] = b
                    loc = (st + 1) if which == 0 else st

                    def mm(e):
                        ins = None
                        for kt in range(8):
                            ins = e.matmul(psb(b), wv[:, which, kt, :], hT[:, kt, loc * 512:(loc + 1) * 512],
                                           start=(kt == 0), stop=(kt == 7))
                        return ins
                    P.add("pe", mm, reads=[wtok(ws, which), ("hT", loc)], writes=[("ps", b)])
                    t = tcnt[0] % 3
                    tcnt[0] += 1
                    st_["t"] = t
                    P.add("dve", lambda e: e.tensor_scalar_mul(out=qsb[t], in0=psb(b), scalar1=gsgn[:, which:which + 1]),
                          reads=[("ps", b), "gsgn"], writes=[("qsb", t)])
                    P.add("dve", lambda e: e.tensor_tensor(out=sqb[t], in0=qsb[t], in1=qsb[t], op=ALU.mult),
                          reads=[("qsb", t)], writes=[("sqb", t)])

                def part1b():
                    pass

                def part2():
                    t = st_["t"]
                    b2 = nb("proj")
                    P.add("pe", lambda e: e.matmul(psb(b2), blk, sqb[t], start=True, stop=True),
                          reads=[("sqb", t), "cmat"], writes=[("ps", b2)])
                    P.add("act", lambda e: e.activation(out=rsb[t], in_=psb(b2), func=AF.Ln, bias=epsc),
                          reads=[("ps", b2), "epsc"], writes=[("rsb", t)])
                    P.add("act", lambda e: e.activation(out=rsb[t], in_=rsb[t], func=AF.Exp, scale=-0.5,
                                                        bias=glog[:, which:which + 1]),
                          reads=[("rsb", t), "glog"], writes=[("rsb", t)])
                    c0 = st * 512
                    if which == 0:
                        P.add("pool", lambda e: e.tensor_tensor(
                            out=qA[ws][0:64, c0:c0 + 512], in0=qsb[t][0:64, :], in1=rsb[t][0:64, :], op=ALU.mult),
                            reads=[("qsb", t), ("rsb", t)], writes=[("qA", ws)])
                        P.add("pool", lambda e: e.tensor_tensor(
                            out=qB[ws][64:128, c0:c0 + 512], in0=qsb[t][64:128, :], in1=rsb[t][64:128, :], op=ALU.mult),
                            reads=[("qsb", t), ("rsb", t)], writes=[("qB", ws)])
                    else:
                        P.add("pool", lambda e: e.tensor_tensor(
                            out=kT[ws][:, c0:c0 + 512], in0=qsb[t], in1=rsb[t], op=ALU.mult),
                            reads=[("qsb", t), ("rsb", t)], writes=[("kT", ws)])
                return part1a, part1b, part2

            def v_unit(t4):
                def part1():
                    b = nb("proj")

                    def mm(e):
                        ins = None
                        for q in range(4):
                            tt = t4 * 4 + q
                            for kt in range(8):
                                ins = e.matmul(psb(b)[:, q * 128:(q + 1) * 128], hT[:, kt, tt * 128:(tt + 1) * 128],
                                               wv[:, 2, kt, :], start=(kt == 0), stop=(kt == 7))
                        return ins
                    P.add("pe", mm, reads=[wtok(ws, 2), ("hT", t4)], writes=[("ps", b)])
                    src = psb(b).rearrange("p (q c) -> p q c", q=4)
                    P.add("dve", lambda e: e.tensor_copy(out=Vb[ws][:, t4 * 4:(t4 + 1) * 4, 0:64], in_=src[:, :, 0:64]),
                          reads=[("ps", b)], writes=[("V", ws)])
                    P.add("dve", lambda e: e.tensor_copy(out=Vb[ws][:, t4 * 4:(t4 + 1) * 4, 128:192],
                                                         in_=src[:, :, 64:128]),
                          reads=[("ps", b)], writes=[("V", ws)])
                return part1, None, None
            for st in range(5):
                units.append(qk_unit(1, st))
            for st in range(4):
                units.append(qk_unit(0, st))
            for t4 in range(5):
                units.append(v_unit(t4))
            return units

        q1b, q2 = [], []

        def drain_stages():
            if q2 and q2[0][0] <= 0:
                q2.pop(0)[1]()
            if q1b and q1b[0][0] <= 0:
                f1b, f2 = q1b.pop(0)[1:]
                f1b()
                q2.append([1, f2])
            for q in (q1b, q2):
                for ent in q:
                    ent[0] -= 1

        def push_unit(u3):
            p1a, p1b, p2 = u3
            p1a()
            if p1b is not None:
                q1b.append([0, p1b, p2])
        for n_, u3 in enumerate(proj_units(0)):
            drain_stages()
            push_unit(u3)
            for h in range(16):
                if h * 14 // 16 == n_:
                    etab_head(h)
        while q1b or q2:
            drain_stages()

        def attn_step(hp, g):
            ws = hp % 2
            pt = (hp * 16 + g) % 3
            st_ = {}

            def e_qk():
                bA, bB, bZ = nb("S"), nb("S"), nb("S")
                st_["b"] = (bA, bB, bZ)
                for hd, qq, bb in ((0, qA[ws], bA), (1, qB[ws], bB)):
                    def qk4(e, qq=qq, bb=bb):
                        ins = None
                        for j in range(4):
                            ins = e.matmul(psb(bb)[:, j * 128:(j + 1) * 128], kT[ws][:, (g + j) * 128:(g + j + 1) * 128],
                                           qq[:, g * 128:(g + 1) * 128], start=True, stop=True)
                        return ins
                    P.add("pe", qk4, reads=[("kT", ws), ("qA", ws), ("qB", ws)], writes=[("ps", bb)])

                def qkz(e):
                    ins = None
                    for hd, qq in ((0, qA[ws]), (1, qB[ws])):
                        ins = e.matmul(psb(bZ)[:, hd * 128:(hd + 1) * 128], kT[ws][:, (g + 4) * 128:(g + 5) * 128],
                                       qq[:, g * 128:(g + 1) * 128], start=True, stop=True)
                    return ins
                P.add("pe", qkz, reads=[("kT", ws), ("qA", ws), ("qB", ws)], writes=[("ps", bZ)])

            def e_softmax():
                bA, bB, bZ = st_["b"]
                P.add("act", lambda e: e.activation(
                    out=PT[pt][:, 0, 0:4, :], in_=psb(bA).rearrange("p (j q) -> p j q", j=4), func=AF.Exp, scale=0.125),
                    reads=[("ps", bA)], writes=[("PT", pt, 0)])
                P.add("act", lambda e: e.activation(
                    out=PT[pt][:, 1, 0:4, :], in_=psb(bB).rearrange("p (j q) -> p j q", j=4), func=AF.Exp, scale=0.125),
                    reads=[("ps", bB)], writes=[("PT", pt, 1)])
                P.add("act", lambda e: e.activation(
                    out=PT[pt][:, :, 4, :], in_=psb(bZ)[:, 0:256].rearrange("p (h q) -> p h q", h=2), func=AF.Exp,
                    scale=0.125),
                    reads=[("ps", bZ)], writes=[("PT", pt, 2)])
                P.add("dve", lambda e: e.tensor_tensor(
                    out=PT[pt], in0=PT[pt], in1=Etab[:, 2 * hp:2 * hp + 2, :, :], op=ALU.mult),
                    reads=[("PT", pt, 0), ("PT", pt, 1), ("PT", pt, 2), "Etab"], writes=[("PTm", pt)])

            def e_pv():
                if g % 2 == 0:
                    obank[0] = nb("O")
                bO = obank[0]
                st_["bO"] = bO
                c0 = (g % 2) * 256

                def pv(e):
                    ins = None
                    for hd in range(2):
                        for j in range(5):
                            ins = e.matmul(psb(bO)[:, c0 + hd * 128:c0 + (hd + 1) * 128],
                                           Vb[ws][:, g + j, hd * 64:hd * 64 + 128], PT[pt][:, hd, j, :],
                                           start=(j == 0), stop=(j == 4))
                    return ins
                P.add("pe", pv, reads=[("PTm", pt), ("V", ws), ("Vones", ws)],
                      writes=[("ps", bO), ("PT", pt, 0), ("PT", pt, 1), ("PT", pt, 2)])

            def e_norm():
                if g % 2 == 0:
                    return
                bO = st_["bO"]
                rc = (hp * 8 + g // 2) % 2
                O4 = psb(bO).rearrange("p (g h q) -> p g h q", g=2, h=2)
                g0 = g - 1
                ydst = yaT[:, hp, g0 * 128:(g0 + 2) * 128].rearrange("p (g q) -> p g q", g=2)
                P.add("act", lambda e: e.activation(out=lnd[rc][0:64, :, :], in_=O4[64:128, :, 0, :], func=AF.Ln),
                      reads=[("ps", bO)], writes=[("lndA", rc)])
                P.add("act", lambda e: e.activation(out=lnd[rc][64:128, :, :], in_=O4[0:64, :, 1, :], func=AF.Ln),
                      reads=[("ps", bO)], writes=[("lndB", rc)])
                P.add("act", lambda e: e.activation(out=rcp[rc], in_=lnd[rc], func=AF.Exp, scale=-1.0),
                      reads=[("lndA", rc), ("lndB", rc)], writes=[("rcp", rc)])
                P.add("dve", lambda e: e.tensor_tensor(
                    out=ydst[0:64], in0=O4[0:64, :, 0, :], in1=rcp[rc][0:64, :, :], op=ALU.mult),
                    reads=[("ps", bO), ("rcp", rc)], writes=[("yaT", g // 4, hp, 0)])
                P.add("dve", lambda e: e.tensor_tensor(
                    out=ydst[64:128], in0=O4[64:128, :, 1, :], in1=rcp[rc][64:128, :, :], op=ALU.mult),
                    reads=[("ps", bO), ("rcp", rc)], writes=[("yaT", g // 4, hp, 1)])
            return e_qk, e_softmax, e_pv, e_norm

        steps = [attn_step(hp, g) for hp in range(8) for g in range(16)]
        NS = len(steps)
        for i in range(NS + 3):
            hp, g = divmod(i, 16)
            if i < NS:
                if g == 0:
                    nxt = proj_units(hp + 1) if hp + 1 < 8 else []
                    if hp + 2 < 8:
                        load_qkv_like(hp % 2, 0, hp + 2)
                steps[i][0]()
                steps[i][1]()
            if 0 <= i - 2 < NS:
                steps[i - 2][3]()
            drain_stages()
            if i < NS and nxt:
                push_unit(nxt.pop(0))
            if 0 <= i - 1 < NS:
                steps[i - 1][2]()
        while q1b or q2:
            drain_stages()
        ya_res = [("yaT", st, hp, hd) for st in range(4) for hp in range(8) for hd in range(2)]
        if DEBUG_TAPS:
            P.add("sp", lambda e: e.dma_start(out=taps["t_ya"][:, :], in_=yaT.rearrange("p k t -> p (k t)")),
                  reads=ya_res, slot="tap")
        if STOP_AFTER == "p1":
            finish()
            return nc

        load_qkv_like(0, 3, 0)
        load_qkv_like(1, 3, 1)
        p2 = Bump(PH + 136 * KB, ARENA)
        ubuf = [p2([128, 2 + NTOK], F32) for _ in range(2)]
        xcs = [p2([128, 512], F32) for _ in range(2)]
        at = [p2([128, 512], F32) for _ in range(2)]
        xch = p2([128, 64], F32)
        ring = {"all": list(range(8))}
        rpos = {"all": 0}
        k2 = [0]
        p2_tokens = ["xch"] + [("xcs", t) for t in range(2)] + [("at", t) for t in range(2)] + \
                    [("ub", p, st) for p in range(2) for st in range(-1, 4)]
        for cc in range(8):
            ws = cc % 2
            wv = wslot(ws, 3)
            ub = ubuf[cc % 2]
            w0c, w1c, w2c = (cw[:, cc * 3 + j:cc * 3 + j + 1] for j in range(3))
            cbc = cb[:, cc:cc + 1]
            b = nb("all")

            def mmh(e, wv=wv, b=b):
                ins = None
                for which, c0 in ((2, 0), (1, 64)):
                    for kt in range(8):
                        ins = e.matmul(psb(b)[:, c0:c0 + 64], wv[:, which, kt, :], hT[:, kt, 448:512],
                                       start=(kt == 0), stop=(kt == 7))
                return ins
            P.add("pe", mmh, reads=[wtok(ws, 1), wtok(ws, 2), ("hT", 0)], writes=[("ps", b)])
            P.add("act", lambda e, b=b: e.activation(out=xch, in_=psb(b)[:, 0:64], func=AF.Copy),
                  reads=[("ps", b)], writes=["xch"])
            P.add("dve", lambda e, b=b, ub=ub: e.tensor_tensor(out=ub[:, 0:2], in0=psb(b)[:, 126:128],
                                                              in1=xch[:, 62:64], op=ALU.mult),
                  reads=[("ps", b), "xch"], writes=[("ub", cc % 2, -1)])
            for st in range(4):
                loc = st + 1
                bx, bc, bbk = nb("all"), nb("all"), nb("all")

                def mm3(e, wv=wv, loc=loc, bx=bx, bc=bc, bbk=bbk):
                    ins = None
                    for which, bb in ((2, bx), (1, bc), (0, bbk)):
                        for kt in range(8):
                            ins = e.matmul(psb(bb), wv[:, which, kt, :], hT[:, kt, loc * 512:(loc + 1) * 512],
                                           start=(kt == 0), stop=(kt == 7))
                    return ins
                P.add("pe", mm3, reads=[wtok(ws, 0), wtok(ws, 1), wtok(ws, 2), ("hT", loc)],
                      writes=[("ps", bx), ("ps", bc), ("ps", bbk)])
                t = k2[0] % 2
                k2[0] += 1
                o = 2 + st * 512
                P.add("act", lambda e, t=t, bx=bx: e.activation(out=xcs[t], in_=psb(bx), func=AF.Copy),
                      reads=[("ps", bx)], writes=[("xcs", t)])
                P.add("dve", lambda e, t=t, bc=bc, ub=ub, o=o: e.tensor_tensor(
                    out=ub[:, o:o + 512], in0=psb(bc), in1=xcs[t], op=ALU.mult),
                    reads=[("ps", bc), ("xcs", t)], writes=[("ub", cc % 2, st)])
                P.add("act", lambda e, t=t, ub=ub, o=o, w2c=w2c, cbc=cbc: e.activation(
                    out=at[t], in_=ub[:, o:o + 512], func=AF.Identity, scale=w2c, bias=cbc),
                    reads=[("ub", cc % 2, st), "cst"], writes=[("at", t)])
                P.add("dve", lambda e, t=t, ub=ub, o=o, w1c=w1c: e.scalar_tensor_tensor(
                    out=at[t], in0=ub[:, o - 1:o + 511], scalar=w1c, in1=at[t], op0=ALU.mult, op1=ALU.add),
                    reads=[("ub", cc % 2, st), ("ub", cc % 2, st - 1), ("at", t)], writes=[("at", t)])
                P.add("dve", lambda e, t=t, ub=ub, o=o, w0c=w0c: e.scalar_tensor_tensor(
                    out=at[t], in0=ub[:, o - 2:o + 510], scalar=w0c, in1=at[t], op0=ALU.mult, op1=ALU.add),
                    reads=[("ub", cc % 2, st), ("ub", cc % 2, st - 1), ("at", t)], writes=[("at", t)])
                P.add("dve", lambda e, t=t, bbk=bbk, cc=cc, st=st: e.tensor_tensor(
                    out=ycT[:, cc, st * 512:(st + 1) * 512], in0=psb(bbk), in1=at[t], op=ALU.mult),
                    reads=[("ps", bbk), ("at", t)], writes=[("ycT", st)])
            if cc + 2 < 8:
                load_qkv_like(cc % 2, 3, cc + 2)
        if DEBUG_TAPS:
            P.add("sp", lambda e: e.dma_start(out=taps["t_yc"][:, :], in_=ycT.rearrange("p k t -> p (k t)")),
                  reads=[("ycT", st) for st in range(4)], slot="tap")
        if STOP_AFTER == "p2":
            finish()
            return nc

        def load_p3(i, oc):
            wv = wslot(i, 4)
            cs = slice(oc * 128, (oc + 1) * 128)
            P.add("pool", lambda e: e.dma_start(out=wv[:, 0, :, :], in_=w_ap_v[:, :, cs]), writes=[("w", i)],
                  slot=f"w{i}")
            P.add("pool", lambda e: e.dma_start(out=wv[:, 1, :, :], in_=w_cp_v[:, :, cs]), writes=[("w", i, 1)],
                  slot=f"w{i}")
            P.add("pool", lambda e: e.dma_start(out=wv[:, 2, :, :], in_=w_gate_v[:, 0, :, cs]), writes=[("w", i, 2)],
                  slot=f"w{i}")
            P.add("pool", lambda e: e.dma_start(out=wv[:, 3, :, :], in_=w_gate_v[:, 1, :, cs]), writes=[("w", i, 3)],
                  slot=f"w{i}")
        load_p3(0, 0)
        load_p3(1, 1)
        sga = [view(WB + 8 * KB + i * 16 * KB, [128, 512], F32) for i in range(2)]
        sgc = [view(WB + 10 * KB + i * 16 * KB, [128, 512], F32) for i in range(2)]
        m1 = [view(WB + 12 * KB + i * 16 * KB, [128, 512], F32) for i in range(2)]
        m2 = [view(WB + 14 * KB + i * 16 * KB, [128, 512], F32) for i in range(2)]

        k3 = [0]
        for oc in range(8):
            ws = oc % 2
            wv = wslot(ws, 4)
            for st in range(4):
                loc = st + 1
                b1, b2, b3, b4 = nb("all"), nb("all"), nb("all"), nb("all")

                for which, bb, src, c0, rd in ((2, b1, hT, loc * 512, [("w", ws, 2), ("hT", loc)]),
                                               (3, b2, hT, loc * 512, [("w", ws, 3), ("hT", loc)]),
                                               (0, b3, yaT, st * 512, [("w", ws)] + ya_res),
                                               (1, b4, ycT, st * 512, [("w", ws, 1), ("ycT", st)])):
                    def mm1(e, wv=wv, which=which, bb=bb, src=src, c0=c0):
                        ins = None
                        for kt in range(8):
                            ins = e.matmul(psb(bb), wv[:, which, kt, :], src[:, kt, c0:c0 + 512],
                                           start=(kt == 0), stop=(kt == 7))
                        return ins
                    P.add("pe", mm1, reads=rd, writes=[("ps", bb)])
                t = k3[0] % 2
                k3[0] += 1
                P.add("act", lambda e, t=t, b1=b1, oc=oc: e.activation(out=sga[t], in_=psb(b1), func=AF.Sigmoid,
                                                                       bias=bgt[:, oc:oc + 1]),
                      reads=[("ps", b1), "cst"], writes=[("sga", t)])
                P.add("act", lambda e, t=t, b2=b2, oc=oc: e.activation(out=sgc[t], in_=psb(b2), func=AF.Sigmoid,
                                                                       bias=bgt[:, 8 + oc:9 + oc]),
                      reads=[("ps", b2), "cst"], writes=[("sgc", t)])
                P.add("dve", lambda e, t=t, b3=b3: e.tensor_tensor(out=m1[t], in0=psb(b3), in1=sga[t], op=ALU.mult),
                      reads=[("ps", b3), ("sga", t)], writes=[("m1", t)])
                P.add("dve", lambda e, t=t, b4=b4: e.tensor_tensor(out=m2[t], in0=psb(b4), in1=sgc[t], op=ALU.mult),
                      reads=[("ps", b4), ("sgc", t)], writes=[("m2", t)])
                P.add("pool", lambda e, t=t, oc=oc, st=st: e.tensor_tensor(
                    out=mT[:, oc, st * 512:(st + 1) * 512], in0=m1[t], in1=m2[t], op=ALU.add),
                    reads=[("m1", t), ("m2", t)], writes=[("mT", st)])
            if oc + 2 < 8:
                load_p3(oc % 2, oc + 2)
        if DEBUG_TAPS:
            P.add("sp", lambda e: e.dma_start(out=taps["t_m"][:, :], in_=mT.rearrange("p k t -> p (k t)")),
                  reads=[("mT", st) for st in range(4)], slot="tap")
        if STOP_AFTER == "p3a":
            finish()
            return nc

        WoA = view(WB, [128, 4, D], BF16)
        WoB = view(PH + 152 * KB, [128, 4, D], BF16)
        P.add("pool", lambda e: e.dma_start(out=WoA, in_=w_out_v[:, 0:4, :]),
              writes=[("w", 0), ("w", 0, 1), ("w", 0, 2), ("w", 0, 3)], slot="w0")
        P.add("pool", lambda e: e.dma_start(out=WoB, in_=w_out_v[:, 4:8, :]),
              writes=["wo2"] + p2_tokens, slot="wd0")
        P.barrier()
        P.add("sp", lambda e: e.dma_start(out=gbb, in_=g2b_d[:, :]), writes=["gbb"], slot="const")
        def wu_slot(i):
            return view(WB + i * 16 * KB, [128, 8, 512], BF16)

        def wd_slot(i):
            return view(WB + i * 16 * KB + 8 * KB, [128, 4, D], BF16)

        def load_wu(i, grp):
            wu = wu_slot(i)
            P.add("pool", lambda e: e.dma_start(out=wu, in_=w_up_v[:, :, grp * 512:(grp + 1) * 512]),
                  writes=[("w", i)], slot=f"w{i}")

        def load_wd(i, grp):
            wd = wd_slot(i)
            P.add("pool", lambda e: e.dma_start(out=wd, in_=w_down_v[:, grp * 4:(grp + 1) * 4, :]),
                  writes=[("w", i, 1)], slot=f"wd{i}")

        def load_p4(i, grp):
            load_wu(i, grp)
            load_wd(i, grp)
        load_p4(1, 0)
        p3b = Bump(PH + 96 * KB, PH + 104 * KB)
        xr = [p3b([128, D], F32) for _ in range(2)]
        p3c = Bump(PH + 136 * KB, ARENA)
        h2b = [p3c([128, D], BF16) for _ in range(2)]
        junk2 = p3c([128, D], BF16)
        def p3b_A(tt):
            s = tt % 2
            P.add("sp", lambda e: e.dma_start(out=xr[s], in_=xh[HALO + tt * 128:HALO + (tt + 1) * 128, :]),
                  writes=[("xr", s)], slot=f"xt{s}")
            for half in range(2):
                b = nb("all")

                def mmo(e, half=half, b=b):
                    ins = None
                    for kt in range(8):
                        wsrc = WoA[:, kt, :] if kt < 4 else WoB[:, kt - 4, :]
                        ins = e.matmul(psb(b), mT[:, kt, tt * 128:(tt + 1) * 128], wsrc[:, half * 512:(half + 1) * 512],
                                       start=(kt == 0), stop=(kt == 7))
                    return ins
                P.add("pe", mmo, reads=[("w", 0), "wo2", ("mT", tt // 4)], writes=[("ps", b)])
                P.add("dve", lambda e, half=half, b=b: e.tensor_tensor(
                    out=x1[:, tt, half * 512:(half + 1) * 512], in0=psb(b), in1=xr[s][:, half * 512:(half + 1) * 512],
                    op=ALU.add), reads=[("ps", b), ("xr", s)], writes=[("x1", tt, half)])
            P.add("act", lambda e: e.activation(out=junk2, in_=x1[:, tt, :], func=AF.Square, scale=1.0 / 32.0,
                                                accum_out=ms2[:, tt:tt + 1]),
                  reads=[("x1", tt, 0), ("x1", tt, 1), "ms_init", "junk2"], writes=[("ms2", tt), "junk2"])

        def p3b_A2(tt):
            rsqrt_act(rstd2[:, tt:tt + 1], ms2[:, tt:tt + 1], lnt[:, tt:tt + 1], [("ms2", tt)], [("rstd2", tt)],
                      ("lnt2", tt))

        trb = {}

        def p3b_B1(tt):
            s = tt % 2
            P.add("dve", lambda e: e.scalar_tensor_tensor(
                out=h2b[s], in0=x1[:, tt, :], scalar=rstd2[:, tt:tt + 1], in1=gbb, op0=ALU.mult, op1=ALU.mult),
                reads=[("x1", tt, 0), ("x1", tt, 1), ("rstd2", tt), "gbb"], writes=[("h2b", s)])

        def p3b_B2(tt):
            s = tt % 2
            b = nb("all")
            trb[tt] = b

            def tr2(e):
                pbf = psb(b).bitcast(BF16)
                ins = None
                for kt in range(8):
                    ins = e.transpose(out=pbf[:, kt * 128:(kt + 1) * 128], in_=h2b[s][:, kt * 128:(kt + 1) * 128],
                                      identity=ident)
                return ins
            P.add("pe", tr2, reads=[("h2b", s), "cmat"], writes=[("ps", b)])

        def p3b_C(tt):
            b = trb[tt]
            P.add("dve", lambda e: e.tensor_copy(
                out=h2T[:, :, tt * 128:(tt + 1) * 128],
                in_=psb(b).bitcast(BF16).rearrange("p (k t) -> p k t", k=8)),
                reads=[("ps", b)], writes=[("h2T", tt // 4)])
        for i in range(16 + 4):
            if 0 <= i - 3 < 16:
                p3b_B1(i - 3)
            if i < 16:
                p3b_A(i)
            if 0 <= i - 1 < 16:
                p3b_A2(i - 1)
            if 0 <= i - 3 < 16:
                p3b_B2(i - 3)
            if 0 <= i - 4 < 16:
                p3b_C(i - 4)
        load_p4(0, 1)
        x1_res = [("x1", tt, half) for tt in range(16) for half in range(2)]
        if DEBUG_TAPS:
            P.add("sp", lambda e: e.dma_start(out=taps["t_x1"][:, :], in_=x1.rearrange("p k t -> p (k t)")),
                  reads=x1_res, slot="tap")
            P.add("sp", lambda e: e.dma_start(out=taps["t_h2"][:, :], in_=h2T.rearrange("p k t -> p (k t)")),
                  reads=[("h2T", st) for st in range(4)], slot="tap")
        if STOP_AFTER == "p3b":
            finish()
            return nc

        P.barrier()
        p4 = Bump(PH + 104 * KB, ARENA)
        uT = [p4([128, 4, NTOK], BF16) for _ in range(2)]
        rl = [p4([128, 512], F32) for _ in range(3)]

        k4 = [0]

        def up_phase(grp):
            ws = grp % 2
            wl = (grp + 1) % 2
            wu = wu_slot(wl)
            for fc in range(4):
                for st in range(4):
                    b = nb("all")

                    def mmu(e, wu=wu, fc=fc, st=st, b=b):
                        ins = None
                        for kt in range(8):
                            ins = e.matmul(psb(b), wu[:, kt, fc * 128:(fc + 1) * 128], h2T[:, kt, st * 512:(st + 1) * 512],
                                           start=(kt == 0), stop=(kt == 7))
                        return ins
                    P.add("pe", mmu, reads=[("w", wl), ("h2T", st)], writes=[("ps", b)])
                    t = k4[0] % 3
                    k4[0] += 1
                    P.add("act", lambda e, t=t, b=b: e.activation(out=rl[t], in_=psb(b), func=AF.Relu),
                          reads=[("ps", b)], writes=[("rl", t)])
                    P.add("pool", lambda e, t=t, ws=ws, fc=fc, st=st: e.tensor_tensor(
                        out=uT[ws][:, fc, st * 512:(st + 1) * 512], in0=rl[t], in1=rl[t], op=ALU.mult),
                        reads=[("rl", t)], writes=[("uT", ws, st)])

        def down_phase(grp):
            ws = grp % 2
            wl = (grp + 1) % 2
            wd = wd_slot(wl)
            for tt in range(16):
                for half in range(2):
                    b = nb("all")

                    def mmd(e, wd=wd, ws=ws, tt=tt, half=half, b=b):
                        ins = None
                        for fc in range(4):
                            ins = e.matmul(psb(b), uT[ws][:, fc, tt * 128:(tt + 1) * 128],
                                           wd[:, fc, half * 512:(half + 1) * 512], start=(fc == 0), stop=(fc == 3))
                        return ins
                    P.add("pe", mmd, reads=[("w", wl, 1), ("uT", ws, tt // 4)], writes=[("ps", b)])
                    P.add("dve", lambda e, tt=tt, half=half, b=b: e.tensor_tensor(
                        out=x1[:, tt, half * 512:(half + 1) * 512], in0=psb(b), in1=x1[:, tt, half * 512:(half + 1) * 512],
                        op=ALU.add), reads=[("ps", b), ("x1", tt, half)], writes=[("x1", tt, half)])
                if grp == 7:
                    P.add("sp", lambda e, tt=tt: e.dma_start(out=y[tt * 128:(tt + 1) * 128, :], in_=x1[:, tt, :]),
                          reads=[("x1", tt, 0), ("x1", tt, 1)], slot="out")
            if grp + 2 < 8:
                load_wd(wl, grp + 2)

        up_phase(0)
        for grp in range(8):
            if grp + 1 < 8:
                up_phase(grp + 1)
            if grp + 2 < 8:
                load_wu((grp + 1) % 2, grp + 2)
            down_phase(grp)
        finish()
    return nc


_CACHE = {}


def _host_consts(q_norm_g, k_norm_g, rel_bias, conv_w, conv_b, b_gate, norm1_g, norm2_g):
    cst = np.zeros((8, 128, 72), np.float32)
    p = np.arange(128)
    cst[:, :, 0] = q_norm_g[p % 64]
    cst[:, :, 1] = k_norm_g[p % 64]
    cwv = conv_w.reshape(3, 8, 128)
    cst[:, :, 2:26] = np.transpose(cwv, (2, 1, 0)).reshape(128, 24)
    cst[:, :, 26:34] = conv_b.reshape(8, 128).T
    cst[:, :, 34:50] = b_gate.reshape(16, 128).T
    cst[:, :, 50:70] = 1.0
    for c in range(8):
        if c % 2 == 0:
            cst[c, :, 50:54] = 0.0
    cmat = np.zeros((128, 256), np.float32)
    cmat[:, 0:128] = np.eye(128, dtype=np.float32)
    cmat[:, 128:256] = (p[:, None] // 64 == p[None, :] // 64).astype(np.float32) / 64.0
    kk = np.arange(128)[:, None, None]
    j = np.arange(5)[None, :, None]
    qq = np.arange(128)[None, None, :]
    dist = 512 - 128 * j + qq - kk
    idx = np.clip(dist, -256, 256) + 256
    bias = rel_bias[:, idx]
    cdiff = (8 - 2 * j + qq // 64) - (kk // 64)
    valid = (cdiff >= 0) & (cdiff <= 8)
    bias = np.where(valid[None], bias, np.float32(-1e30)).astype(np.float32)
    biasT = np.ascontiguousarray(np.transpose(bias, (1, 0, 2, 3))).reshape(128, 16 * 640)
    g1b = np.ascontiguousarray(np.broadcast_to(norm1_g[None, :], (128, D))).astype(np.float32)
    g2b = np.ascontiguousarray(np.broadcast_to(norm2_g[None, :], (128, D))).astype(np.float32)
    return cst, cmat, biasT, g1b, g2b


def kernel(x, norm1_g, w_in, q_norm_g, k_norm_g, rel_bias, conv_w, conv_b, w_attn_proj, w_conv_proj,
           w_gate, b_gate, w_out, norm2_g, w_up, w_down):
    f = lambda a: np.ascontiguousarray(np.asarray(a, dtype=np.float32))
    x = f(x)
    B, S, _ = x.shape
    cst, cmat, biasT, g1b, g2b = _host_consts(f(q_norm_g), f(k_norm_g), f(rel_bias), f(conv_w), f(conv_b),
                                              f(b_gate), f(norm1_g), f(norm2_g))
    if "nc" not in _CACHE:
        _CACHE["nc"] = build_program()
    nc = _CACHE["nc"]
    shared = {"w_in": f(w_in), "w_gate": f(w_gate), "w_ap": f(w_attn_proj), "w_cp": f(w_conv_proj),
              "w_out": f(w_out), "w_up": f(w_up), "w_down": f(w_down), "g1b": g1b, "g2b": g2b, "cmat": cmat,
              "biasT": biasT}
    in_maps = []
    for c in range(N_RUN):
        b, half = c // 2, c % 2
        xh = np.zeros((NLOC, D), np.float32)
        if half == 0:
            xh[HALO:] = x[b, 0:NTOK]
        else:
            xh[:] = x[b, NTOK - HALO:2 * NTOK]
        m = dict(shared)
        m["xh"] = xh
        m["cst"] = cst[c]
        in_maps.append(m)
    res = run_bass_kernel_spmd(nc, in_maps, core_ids=list(range(N_RUN)))
    _CACHE["last"] = res
    out = np.zeros((B, S, D), np.float32)
    for c in range(N_RUN):
        b, half = c // 2, c % 2
        out[b, half * NTOK:(half + 1) * NTOK] = res.results[c]["y"]
    return out
```

```python
import numpy as np
import concourse.bass as bass
import concourse.mybir as mybir
from concourse.bass_utils import run_bass_kernel_spmd

F32 = mybir.dt.float32
BF16 = mybir.dt.bfloat16
U8 = mybir.dt.uint8
AF = mybir.ActivationFunctionType
ALU = mybir.AluOpType

D = 1024
NTOK = 2048
HALO = 512
NLOC = NTOK + HALO
NTT = NLOC // 128
EPS = 1e-6
KB = 1024

DEBUG_TAPS = False
N_RUN = 8
STOP_AFTER = None


class _Op:
    __slots__ = ("eng", "fn", "raw", "oth", "idx", "sig", "sigidx", "slot", "dmawait", "barrier")


class Prog:
    ENGS = ("pe", "act", "dve", "pool", "sp")

    def __init__(self):
        self.ops = []
        self.last_writer = {}
        self.readers = {}
        self.slot_count = {}
        self.last_on_eng = {}

    def add(self, eng, fn, reads=(), writes=(), slot=None):
        op = _Op()
        op.eng, op.fn, op.idx, op.sig, op.slot, op.barrier = eng, fn, len(self.ops), False, slot, False
        op.raw, op.oth, op.dmawait = set(), set(), {}
        for r in reads:
            w = self.last_writer.get(r)
            if w is not None:
                op.raw.add(w)
        for r in writes:
            w = self.last_writer.get(r)
            if w is not None:
                op.oth.add(w)
            for rd in self.readers.get(r, ()):
                op.oth.add(rd)
        for r in reads:
            self.readers.setdefault(r, []).append(op.idx)
        for r in writes:
            self.last_writer[r] = op.idx
            self.readers[r] = []
        for d in list(op.raw | op.oth):
            dop = self.ops[d]
            if dop.slot is not None:
                op.dmawait[dop.slot] = self.slot_count[dop.slot]
        if slot is not None:
            self.slot_count[slot] = self.slot_count.get(slot, 0) + 1
        self.ops.append(op)
        self.last_on_eng[eng] = op.idx
        return op

    def barrier(self):
        lasts = dict(self.last_on_eng)
        slots = dict(self.slot_count)
        for eng in self.ENGS:
            op = _Op()
            op.eng, op.fn, op.idx, op.sig, op.slot, op.barrier = eng, None, len(self.ops), False, None, True
            op.raw = set(v for k, v in lasts.items())
            op.oth = set()
            op.dmawait = dict(slots)
            self.ops.append(op)
        self.last_writer = {}
        self.readers = {}

    def emit(self, nc, engobjs, sems, slot_sems):
        ops = self.ops
        for op in ops:
            for d in op.raw | op.oth:
                dop = ops[d]
                if dop.slot is not None:
                    continue
                if dop.eng == op.eng and not op.barrier:
                    if op.eng == "pe" or op.eng == "sp":
                        continue
                    if d not in op.raw:
                        continue
                dop.sig = True
        cnt = {e: 0 for e in self.ENGS}
        for op in ops:
            if op.sig:
                cnt[op.eng] += 1
                op.sigidx = cnt[op.eng]
        per_eng = {e: [] for e in self.ENGS}
        for op in ops:
            per_eng[op.eng].append(op)

        def run(eng):
            e = engobjs[eng]
            waited = {}
            for op in per_eng[eng]:
                need = {}
                for d in op.raw | op.oth:
                    dop = ops[d]
                    if dop.slot is not None or not dop.sig:
                        continue
                    if dop.eng == eng and not op.barrier:
                        if eng == "pe" or eng == "sp" or d not in op.raw:
                            continue
                    if dop.eng == eng and op.barrier:
                        continue
                    key = ("e", dop.eng)
                    need[key] = max(need.get(key, 0), dop.sigidx)
                for s, c in op.dmawait.items():
                    key = ("s", s)
                    need[key] = max(need.get(key, 0), 16 * c)
                for key, v in need.items():
                    if waited.get(key, 0) >= v:
                        continue
                    waited[key] = v
                    sem = sems[key[1]] if key[0] == "e" else slot_sems[key[1]]
                    e.wait_ge(sem, v)
                if op.fn is None:
                    continue
                ins = op.fn(e)
                if op.slot is not None:
                    ins.then_inc(slot_sems[op.slot], 16)
                elif op.sig:
                    ins.then_inc(sems[eng], 1)
        return run


def build_program():
    nc = bass.Bass("TRN2", target_bir_lowering=False)

    def din(name, shape):
        return nc.dram_tensor(name, list(shape), F32, kind="ExternalInput").ap()

    xh = din("xh", [NLOC, D])
    w_in = din("w_in", [D, 6 * D])
    w_gate = din("w_gate", [D, 2 * D])
    w_ap = din("w_ap", [D, D])
    w_cp = din("w_cp", [D, D])
    w_out = din("w_out", [D, D])
    w_up = din("w_up", [D, 4 * D])
    w_down = din("w_down", [4 * D, D])
    g1b_d = din("g1b", [128, D])
    g2b_d = din("g2b", [128, D])
    cst_d = din("cst", [128, 72])
    cmat_d = din("cmat", [128, 256])
    bias_d = din("biasT", [128, 16 * 640])
    y = nc.dram_tensor("y", [NTOK, D], F32, kind="ExternalOutput").ap()
    taps = {}
    if DEBUG_TAPS:
        def dout(name, shape, dt):
            taps[name] = nc.dram_tensor(name, list(shape), dt, kind="ExternalOutput").ap()
        dout("t_hT", [128, 8 * NLOC], BF16)
        dout("t_ya", [128, 8 * NTOK], BF16)
        dout("t_yc", [128, 8 * NTOK], BF16)
        dout("t_m", [128, 8 * NTOK], BF16)
        dout("t_x1", [128, 16 * D], F32)
        dout("t_h2", [128, 8 * NTOK], BF16)

    w_in_v = w_in.rearrange("(kt p) (s n) -> p s kt n", p=128, s=6)
    w_gate_v = w_gate.rearrange("(kt p) (s n) -> p s kt n", p=128, s=2)
    w_ap_v = w_ap.rearrange("(kt p) n -> p kt n", p=128)
    w_cp_v = w_cp.rearrange("(kt p) n -> p kt n", p=128)
    w_out_v = w_out.rearrange("(kt p) n -> p kt n", p=128)
    w_up_v = w_up.rearrange("(kt p) n -> p kt n", p=128)
    w_down_v = w_down.rearrange("(fc p) n -> p fc n", p=128)

    ARENA = 207 * KB
    P = Prog()

    from contextlib import ExitStack
    with ExitStack() as es:
        arena = es.enter_context(nc.sbuf_tensor("arena", [128, ARENA], U8))
        banks = [es.enter_context(nc.psum_tensor(f"bank{i}", [128, 512], F32)) for i in range(8)]
        sems = {e: es.enter_context(nc.semaphore(f"sem_{e}")) for e in Prog.ENGS}
        slot_names = ["const", "constp", "xt0", "xt1", "xt2", "xt3", "xt4", "xt5", "w0", "w1", "wd0", "wd1", "bias0", "bias1", "out", "tap"]
        slot_sems = {s: es.enter_context(nc.semaphore(f"slot_{s}")) for s in slot_names}
        block = es.enter_context(nc.Block())

        def view(off, shape, dt):
            esz = 4 if dt == F32 else 2
            n = int(np.prod(shape[1:]))
            assert off % 4 == 0 and off + n * esz <= ARENA, (off, shape)
            v = arena[:, off:off + n * esz].bitcast(dt)
            if len(shape) == 3:
                v = v.rearrange("p (a b) -> p a b", a=shape[1])
            elif len(shape) == 4:
                v = v.rearrange("p (a b c) -> p a b c", a=shape[1], b=shape[2])
            return v

        class Bump:
            def __init__(self, base, limit):
                self.o, self.limit = base, limit

            def __call__(self, shape, dt):
                esz = 4 if dt == F32 else 2
                n = int(np.prod(shape[1:])) * esz
                n4 = (n + 63) // 64 * 64
                off = self.o
                self.o += n4
                assert self.o <= self.limit, ("sbuf overflow", self.o, self.limit)
                return view(off, shape, dt)

        def psb(b):
            return banks[b][:, :]

        gb = Bump(0, 12 * KB)
        cst = gb([128, 72], F32)
        cmat = gb([128, 256], BF16)
        gbb = gb([128, D], F32)
        ms = gb([128, NTT], F32)
        rstd = gb([128, NTT], F32)
        ms2 = gb([128, 16], F32)
        rstd2 = gb([128, 16], F32)
        lnt = gb([128, NTT], F32)
        epsc = gb([128, 1], F32)
        ginv = gb([128, 2], F32)
        gsgn = gb([128, 2], F32)
        glog = gb([128, 2], F32)
        ident = cmat[:, 0:128]
        blk = cmat[:, 128:256]
        gq = cst[:, 0:1]
        gk = cst[:, 1:2]
        cw = cst[:, 2:26]
        cb = cst[:, 26:34]
        bgt = cst[:, 34:50]
        vmask = cst[:, 50:70]
        WB = 12 * KB
        PH = 44 * KB
        hT = view(PH, [128, 8, NLOC], BF16)
        yaT = view(PH + 40 * KB, [128, 8, NTOK], BF16)
        ycT = view(PH + 72 * KB, [128, 8, NTOK], BF16)
        mT = view(PH + 104 * KB, [128, 8, NTOK], BF16)
        x1 = view(PH, [128, 16, D], F32)
        h2T = view(PH + 64 * KB, [128, 8, NTOK], BF16)

        P.add("sp", lambda e: e.dma_start(out=cst, in_=cst_d[:, :]), writes=["cst"], slot="const")
        P.add("sp", lambda e: e.dma_start(out=gbb, in_=g1b_d[:, :]), writes=["gbb"], slot="const")
        P.add("pool", lambda e: e.dma_start(out=cmat, in_=cmat_d[:, :]), writes=["cmat"], slot="constp")
        P.add("dve", lambda e: e.memset(ms, 0.0), writes=["ms_init"])
        P.add("dve", lambda e: e.memset(ms2, 0.0), writes=["ms_init"])

        P.add("dve", lambda e: e.memset(epsc, EPS), writes=["epsc"])
        P.add("act", lambda e: e.activation(out=gsgn, in_=cst[:, 0:2], func=AF.Sign), reads=["cst"], writes=["gsgn"])
        P.add("dve", lambda e: e.tensor_tensor(out=ginv, in0=cst[:, 0:2], in1=gsgn, op=ALU.mult),
              reads=["cst", "gsgn"], writes=["gabs"])
        P.add("dve", lambda e: e.tensor_scalar_max(out=ginv, in0=ginv, scalar1=1e-30), reads=["gabs"], writes=["gabs2"])
        P.add("act", lambda e: e.activation(out=glog, in_=ginv, func=AF.Ln), reads=["gabs2"], writes=["glog"])

        def rsqrt_act(out, in_, tmp, reads, writes, tmptok):
            P.add("act", lambda e: e.activation(out=tmp, in_=in_, func=AF.Ln, bias=epsc[0:tmp.shape[0], :]),
                  reads=list(reads) + ["epsc"], writes=[tmptok])
            P.add("act", lambda e: e.activation(out=out, in_=tmp, func=AF.Exp, scale=-0.5),
                  reads=[tmptok], writes=list(writes))

        def wslot(i, S):
            return view(WB + i * 16 * KB, [128, S, 8, 128], BF16)

        def wtok(i, k):
            return ("w", i) if k == 0 else ("w", i, k)

        def load_qkv_like(i, sbase, cchunk):
            wv = wslot(i, 3)
            for k in range(3):
                src = w_in_v[:, sbase + k, :, cchunk * 128:(cchunk + 1) * 128]
                P.add("pool", lambda e, k=k, src=src: e.dma_start(out=wv[:, k, :, :], in_=src),
                      writes=[wtok(i, k)], slot=f"w{i}")

        p1 = Bump(PH + 72 * KB, ARENA)
        Etab = p1([128, 16, 5, 128], BF16)
        PT = [p1([128, 2, 5, 128], BF16) for _ in range(3)]
        rcp = [p1([128, 2, 128], F32) for _ in range(2)]
        lnd = [p1([128, 2, 128], F32) for _ in range(2)]
        qA, qB, kT, Vb = [None, None], [None, None], [None, None], [None, None]
        qA[1] = p1([128, NTOK], BF16)
        qB[1] = p1([128, NTOK], BF16)
        kT[1] = p1([128, NLOC], BF16)
        Vb[1] = p1([128, NTT, 192], BF16)
        assert p1.o <= PH + 136 * KB
        qA[0] = p1([128, NTOK], BF16)
        qB[0] = p1([128, NTOK], BF16)
        kT[0] = p1([128, NLOC], BF16)
        Vb[0] = p1([128, NTT, 192], BF16)
        assert PH + 124 * KB <= PH + 136 * KB and p1.o >= PH + 136 * KB
        qsb = [p1([128, 512], F32) for _ in range(3)]
        sqb = [p1([128, 512], BF16) for _ in range(3)]
        rsb = [p1([128, 512], F32) for _ in range(3)]
        obank = [None]
        bstage = [view(WB + 8 * KB + i * 16 * KB, [128, 640], F32) for i in range(2)]

        pb = Bump(PH + 40 * KB, PH + 72 * KB)
        xt = [pb([128, D], F32) for _ in range(6)]
        xsb = [pb([128, D], BF16) for _ in range(2)]
        junk = pb([128, D], BF16)
        load_qkv_like(0, 0, 0)
        load_qkv_like(1, 0, 1)
        def pro_A(tt):
            s = tt % 6
            P.add("sp", lambda e: e.dma_start(out=xt[s], in_=xh[tt * 128:(tt + 1) * 128, :]),
                  writes=[("xt", s)], slot=f"xt{s}")
            P.add("act", lambda e: e.activation(out=junk, in_=xt[s], func=AF.Square, scale=1.0 / 32.0,
                                                accum_out=ms[:, tt:tt + 1]),
                  reads=[("xt", s), "ms_init", "junk"], writes=[("ms", tt), "junk"])

        def pro_A2(tt):
            rsqrt_act(rstd[:, tt:tt + 1], ms[:, tt:tt + 1], lnt[:, tt:tt + 1], [("ms", tt)], [("rstd", tt)], ("lnt", tt))

        def pro_B(tt):
            s, s2, b = tt % 6, tt % 2, tt % 2
            P.add("dve", lambda e: e.scalar_tensor_tensor(
                out=xsb[s2], in0=xt[s], scalar=rstd[:, tt:tt + 1], in1=gbb, op0=ALU.mult, op1=ALU.mult),
                reads=[("xt", s), ("rstd", tt), "gbb"], writes=[("xsb", s2)])

            def tr(e):
                pbf = psb(b).bitcast(BF16)
                ins = None
                for kt in range(8):
                    ins = e.transpose(out=pbf[:, kt * 128:(kt + 1) * 128], in_=xsb[s2][:, kt * 128:(kt + 1) * 128],
                                      identity=ident)
                return ins
            P.add("pe", tr, reads=[("xsb", s2), "cmat"], writes=[("ps", b)])

        def pro_C(tt):
            b = tt % 2
            P.add("dve", lambda e: e.tensor_copy(
                out=hT[:, :, tt * 128:(tt + 1) * 128],
                in_=psb(b).bitcast(BF16).rearrange("p (k t) -> p k t", k=8)),
                reads=[("ps", b)], writes=[("hT", tt // 4)])
        def etab_head(h):
            s = h % 2
            P.add("sp", lambda e: e.dma_start(out=bstage[s], in_=bias_d[:, h * 640:(h + 1) * 640]),
                  writes=[("bst", s)], slot=f"bias{s}")
            P.add("act", lambda e: e.activation(
                out=Etab[:, h, :, :], in_=bstage[s].rearrange("p (j q) -> p j q", j=5), func=AF.Exp),
                reads=[("bst", s)], writes=["Etab"])
        for i in range(NTT + 3):
            if i < NTT:
                pro_A(i)
            if 0 <= i - 1 < NTT:
                pro_A2(i - 1)
            if 0 <= i - 2 < NTT:
                pro_B(i - 2)
            if 0 <= i - 3 < NTT:
                pro_C(i - 3)
        if DEBUG_TAPS:
            P.add("sp", lambda e: e.dma_start(out=taps["t_hT"][:, :], in_=hT.rearrange("p k t -> p (k t)")),
                  reads=[("hT", i) for i in range(5)], slot="tap")

        def finish():
            P.barrier()
            engobjs = {}
            run = P.emit(nc, engobjs, sems, slot_sems)

            def mk(eng):
                def f(e):
                    engobjs[eng] = e
                    run(eng)
                return f
            block.tensor(mk("pe"))
            block.scalar(mk("act"))
            block.vector(mk("dve"))
            block.gpsimd(mk("pool"))
            block.sync(mk("sp"))

        if STOP_AFTER == "pro":
            finish()
            return nc

        for s in range(2):
            P.add("pool", lambda e, s=s: e.memset(qA[s][64:128, :], 0.0), writes=[("qA", s)])
            P.add("pool", lambda e, s=s: e.memset(qB[s][0:64, :], 0.0), writes=[("qB", s)])
            P.add("dve", lambda e, s=s: e.tensor_copy(
                out=Vb[s][:, :, 64:128], in_=vmask.unsqueeze(2).to_broadcast([128, NTT, 64])),
                reads=["cst"], writes=[("Vones", s)])

        ring = {"proj": [0, 1, 2], "S": [3, 4, 5], "O": [6, 7]}
        rpos = {"proj": 0, "S": 0, "O": 0}

        def nb(kind):
            b = ring[kind][rpos[kind] % len(ring[kind])]
            rpos[kind] += 1
            return b

        tcnt = [0]

        def proj_units(hp):
            ws = hp % 2
            wv = wslot(ws, 3)
            units = []

            def qk_unit(which, st):
                st_ = {}

                def part1a():
                    b = nb("proj")
                    st_["b"] = b
                    loc = (st + 1) if which == 0 else st

                    def mm(e):
                        ins = None
                        for kt in range(8):
                            ins = e.matmul(psb(b), wv[:, which, kt, :], hT[:, kt, loc * 512:(loc + 1) * 512],
                                           start=(kt == 0), stop=(kt == 7))
                        return ins
                    P.add("pe", mm, reads=[wtok(ws, which), ("hT", loc)], writes=[("ps", b)])
                    t = tcnt[0] % 3
                    tcnt[0] += 1
                    st_["t"] = t
                    P.add("dve", lambda e: e.tensor_scalar_mul(out=qsb[t], in0=psb(b), scalar1=gsgn[:, which:which + 1]),
                          reads=[("ps", b), "gsgn"], writes=[("qsb", t)])
                    P.add("dve", lambda e: e.tensor_tensor(out=sqb[t], in0=qsb[t], in1=qsb[t], op=ALU.mult),
                          reads=[("qsb", t)], writes=[("sqb", t)])

                def part1b():
                    pass

                def part2():
                    t = st_["t"]
                    b2 = nb("proj")
                    P.add("pe", lambda e: e.matmul(psb(b2), blk, sqb[t], start=True, stop=True),
                          reads=[("sqb", t), "cmat"], writes=[("ps", b2)])
                    P.add("act", lambda e: e.activation(out=rsb[t], in_=psb(b2), func=AF.Ln, bias=epsc),
                          reads=[("ps", b2), "epsc"], writes=[("rsb", t)])
                    P.add("act", lambda e: e.activation(out=rsb[t], in_=rsb[t], func=AF.Exp, scale=-0.5,
                                                        bias=glog[:, which:which + 1]),
                          reads=[("rsb", t), "glog"], writes=[("rsb", t)])
                    c0 = st * 512
                    if which == 0:
                        P.add("pool", lambda e: e.tensor_tensor(
                            out=qA[ws][0:64, c0:c0 + 512], in0=qsb[t][0:64, :], in1=rsb[t][0:64, :], op=ALU.mult),
                            reads=[("qsb", t), ("rsb", t)], writes=[("qA", ws)])
                        P.add("pool", lambda e: e.tensor_tensor(
                            out=qB[ws][64:128, c0:c0 + 512], in0=qsb[t][64:128, :], in1=rsb[t][64:128, :], op=ALU.mult),
                            reads=[("qsb", t), ("rsb", t)], writes=[("qB", ws)])
                    else:
                        P.add("pool", lambda e: e.tensor_tensor(
                            out=kT[ws][:, c0:c0 + 512], in0=qsb[t], in1=rsb[t], op=ALU.mult),
                            reads=[("qsb", t), ("rsb", t)], writes=[("kT", ws)])
                return part1a, part1b, part2

            def v_unit(t4):
                def part1():
                    b = nb("proj")

                    def mm(e):
                        ins = None
                        for q in range(4):
                            tt = t4 * 4 + q
                            for kt in range(8):
                                ins = e.matmul(psb(b)[:, q * 128:(q + 1) * 128], hT[:, kt, tt * 128:(tt + 1) * 128],
                                               wv[:, 2, kt, :], start=(kt == 0), stop=(kt == 7))
                        return ins
                    P.add("pe", mm, reads=[wtok(ws, 2), ("hT", t4)], writes=[("ps", b)])
                    src = psb(b).rearrange("p (q c) -> p q c", q=4)
                    P.add("dve", lambda e: e.tensor_copy(out=Vb[ws][:, t4 * 4:(t4 + 1) * 4, 0:64], in_=src[:, :, 0:64]),
                          reads=[("ps", b)], writes=[("V", ws)])
                    P.add("dve", lambda e: e.tensor_copy(out=Vb[ws][:, t4 * 4:(t4 + 1) * 4, 128:192],
                                                         in_=src[:, :, 64:128]),
                          reads=[("ps", b)], writes=[("V", ws)])
                return part1, None, None
            for st in range(5):
                units.append(qk_unit(1, st))
            for st in range(4):
                units.append(qk_unit(0, st))
            for t4 in range(5):
                units.append(v_unit(t4))
            return units

        q1b, q2 = [], []

        def drain_stages():
            if q2 and q2[0][0] <= 0:
                q2.pop(0)[1]()
            if q1b and q1b[0][0] <= 0:
                f1b, f2 = q1b.pop(0)[1:]
                f1b()
                q2.append([1, f2])
            for q in (q1b, q2):
                for ent in q:
                    ent[0] -= 1

        def push_unit(u3):
            p1a, p1b, p2 = u3
            p1a()
            if p1b is not None:
                q1b.append([0, p1b, p2])
        for n_, u3 in enumerate(proj_units(0)):
            drain_stages()
            push_unit(u3)
            for h in range(16):
                if h * 14 // 16 == n_:
                    etab_head(h)
        while q1b or q2:
            drain_stages()

        def attn_step(hp, g):
            ws = hp % 2
            pt = (hp * 16 + g) % 3
            st_ = {}

            def e_qk():
                bA, bB, bZ = nb("S"), nb("S"), nb("S")
                st_["b"] = (bA, bB, bZ)
                for hd, qq, bb in ((0, qA[ws], bA), (1, qB[ws], bB)):
                    def qk4(e, qq=qq, bb=bb):
                        ins = None
                        for j in range(4):
                            ins = e.matmul(psb(bb)[:, j * 128:(j + 1) * 128], kT[ws][:, (g + j) * 128:(g + j + 1) * 128],
                                           qq[:, g * 128:(g + 1) * 128], start=True, stop=True)
                        return ins
                    P.add("pe", qk4, reads=[("kT", ws), ("qA", ws), ("qB", ws)], writes=[("ps", bb)])

                def qkz(e):
                    ins = None
                    for hd, qq in ((0, qA[ws]), (1, qB[ws])):
                        ins = e.matmul(psb(bZ)[:, hd * 128:(hd + 1) * 128], kT[ws][:, (g + 4) * 128:(g + 5) * 128],
                                       qq[:, g * 128:(g + 1) * 128], start=True, stop=True)
                    return ins
                P.add("pe", qkz, reads=[("kT", ws), ("qA", ws), ("qB", ws)], writes=[("ps", bZ)])

            def e_softmax():
                bA, bB, bZ = st_["b"]
                P.add("act", lambda e: e.activation(
                    out=PT[pt][:, 0, 0:4, :], in_=psb(bA).rearrange("p (j q) -> p j q", j=4), func=AF.Exp, scale=0.125),
                    reads=[("ps", bA)], writes=[("PT", pt, 0)])
                P.add("act", lambda e: e.activation(
                    out=PT[pt][:, 1, 0:4, :], in_=psb(bB).rearrange("p (j q) -> p j q", j=4), func=AF.Exp, scale=0.125),
                    reads=[("ps", bB)], writes=[("PT", pt, 1)])
                P.add("act", lambda e: e.activation(
                    out=PT[pt][:, :, 4, :], in_=psb(bZ)[:, 0:256].rearrange("p (h q) -> p h q", h=2), func=AF.Exp,
                    scale=0.125),
                    reads=[("ps", bZ)], writes=[("PT", pt, 2)])
                P.add("dve", lambda e: e.tensor_tensor(
                    out=PT[pt], in0=PT[pt], in1=Etab[:, 2 * hp:2 * hp + 2, :, :], op=ALU.mult),
                    reads=[("PT", pt, 0), ("PT", pt, 1), ("PT", pt, 2), "Etab"], writes=[("PTm", pt)])

            def e_pv():
                if g % 2 == 0:
                    obank[0] = nb("O")
                bO = obank[0]
                st_["bO"] = bO
                c0 = (g % 2) * 256

                def pv(e):
                    ins = None
                    for hd in range(2):
                        for j in range(5):
                            ins = e.matmul(psb(bO)[:, c0 + hd * 128:c0 + (hd + 1) * 128],
                                           Vb[ws][:, g + j, hd * 64:hd * 64 + 128], PT[pt][:, hd, j, :],
                                           start=(j == 0), stop=(j == 4))
                    return ins
                P.add("pe", pv, reads=[("PTm", pt), ("V", ws), ("Vones", ws)],
                      writes=[("ps", bO), ("PT", pt, 0), ("PT", pt, 1), ("PT", pt, 2)])

            def e_norm():
                if g % 2 == 0:
                    return
                bO = st_["bO"]
                rc = (hp * 8 + g // 2) % 2
                O4 = psb(bO).rearrange("p (g h q) -> p g h q", g=2, h=2)
                g0 = g - 1
                ydst = yaT[:, hp, g0 * 128:(g0 + 2) * 128].rearrange("p (g q) -> p g q", g=2)
                P.add("act", lambda e: e.activation(out=lnd[rc][0:64, :, :], in_=O4[64:128, :, 0, :], func=AF.Ln),
                      reads=[("ps", bO)], writes=[("lndA", rc)])
                P.add("act", lambda e: e.activation(out=lnd[rc][64:128, :, :], in_=O4[0:64, :, 1, :], func=AF.Ln),
                      reads=[("ps", bO)], writes=[("lndB", rc)])
                P.add("act", lambda e: e.activation(out=rcp[rc], in_=lnd[rc], func=AF.Exp, scale=-1.0),
                      reads=[("lndA", rc), ("lndB", rc)], writes=[("rcp", rc)])
                P.add("dve", lambda e: e.tensor_tensor(
                    out=ydst[0:64], in0=O4[0:64, :, 0, :], in1=rcp[rc][0:64, :, :], op=ALU.mult),
                    reads=[("ps", bO), ("rcp", rc)], writes=[("yaT", g // 4, hp, 0)])
                P.add("dve", lambda e: e.tensor_tensor(
                    out=ydst[64:128], in0=O4[64:128, :, 1, :], in1=rcp[rc][64:128, :, :], op=ALU.mult),
                    reads=[("ps", bO), ("rcp", rc)], writes=[("yaT", g // 4, hp, 1)])
            return e_qk, e_softmax, e_pv, e_norm

        steps = [attn_step(hp, g) for hp in range(8) for g in range(16)]
        NS = len(steps)
        for i in range(NS + 3):
            hp, g = divmod(i, 16)
            if i < NS:
                if g == 0:
                    nxt = proj_units(hp + 1) if hp + 1 < 8 else []
                    if hp + 2 < 8:
                        load_qkv_like(hp % 2, 0, hp + 2)
                steps[i][0]()
                steps[i][1]()
            if 0 <= i - 2 < NS:
                steps[i - 2][3]()
            drain_stages()
            if i < NS and nxt:
                push_unit(nxt.pop(0))
            if 0 <= i - 1 < NS:
                steps[i - 1][2]()
        while q1b or q2:
            drain_stages()
        ya_res = [("yaT", st, hp, hd) for st in range(4) for hp in range(8) for hd in range(2)]
        if DEBUG_TAPS:
            P.add("sp", lambda e: e.dma_start(out=taps["t_ya"][:, :], in_=yaT.rearrange("p k t -> p (k t)")),
                  reads=ya_res, slot="tap")
        if STOP_AFTER == "p1":
            finish()
            return nc

        load_qkv_like(0, 3, 0)
        load_qkv_like(1, 3, 1)
        p2 = Bump(PH + 136 * KB, ARENA)
        ubuf = [p2([128, 2 + NTOK], F32) for _ in range(2)]
        xcs = [p2([128, 512], F32) for _ in range(2)]
        at = [p2([128, 512], F32) for _ in range(2)]
        xch = p2([128, 64], F32)
        ring = {"all": list(range(8))}
        rpos = {"all": 0}
        k2 = [0]
        p2_tokens = ["xch"] + [("xcs", t) for t in range(2)] + [("at", t) for t in range(2)] + \
                    [("ub", p, st) for p in range(2) for st in range(-1, 4)]
        for cc in range(8):
            ws = cc % 2
            wv = wslot(ws, 3)
            ub = ubuf[cc % 2]
            w0c, w1c, w2c = (cw[:, cc * 3 + j:cc * 3 + j + 1] for j in range(3))
            cbc = cb[:, cc:cc + 1]
            b = nb("all")

            def mmh(e, wv=wv, b=b):
                ins = None
                for which, c0 in ((2, 0), (1, 64)):
                    for kt in range(8):
                        ins = e.matmul(psb(b)[:, c0:c0 + 64], wv[:, which, kt, :], hT[:, kt, 448:512],
                                       start=(kt == 0), stop=(kt == 7))
                return ins
            P.add("pe", mmh, reads=[wtok(ws, 1), wtok(ws, 2), ("hT", 0)], writes=[("ps", b)])
            P.add("act", lambda e, b=b: e.activation(out=xch, in_=psb(b)[:, 0:64], func=AF.Copy),
                  reads=[("ps", b)], writes=["xch"])
            P.add("dve", lambda e, b=b, ub=ub: e.tensor_tensor(out=ub[:, 0:2], in0=psb(b)[:, 126:128],
                                                              in1=xch[:, 62:64], op=ALU.mult),
                  reads=[("ps", b), "xch"], writes=[("ub", cc % 2, -1)])
            for st in range(4):
                loc = st + 1
                bx, bc, bbk = nb("all"), nb("all"), nb("all")

                def mm3(e, wv=wv, loc=loc, bx=bx, bc=bc, bbk=bbk):
                    ins = None
                    for which, bb in ((2, bx), (1, bc), (0, bbk)):
                        for kt in range(8):
                            ins = e.matmul(psb(bb), wv[:, which, kt, :], hT[:, kt, loc * 512:(loc + 1) * 512],
                                           start=(kt == 0), stop=(kt == 7))
                    return ins
                P.add("pe", mm3, reads=[wtok(ws, 0), wtok(ws, 1), wtok(ws, 2), ("hT", loc)],
                      writes=[("ps", bx), ("ps", bc), ("ps", bbk)])
                t = k2[0] % 2
                k2[0] += 1
                o = 2 + st * 512
                P.add("act", lambda e, t=t, bx=bx: e.activation(out=xcs[t], in_=psb(bx), func=AF.Copy),
                      reads=[("ps", bx)], writes=[("xcs", t)])
                P.add("dve", lambda e, t=t, bc=bc, ub=ub, o=o: e.tensor_tensor(
                    out=ub[:, o:o + 512], in0=psb(bc), in1=xcs[t], op=ALU.mult),
                    reads=[("ps", bc), ("xcs", t)], writes=[("ub", cc % 2, st)])
                P.add("act", lambda e, t=t, ub=ub, o=o, w2c=w2c, cbc=cbc: e.activation(
                    out=at[t], in_=ub[:, o:o + 512], func=AF.Identity, scale=w2c, bias=cbc),
                    reads=[("ub", cc % 2, st), "cst"], writes=[("at", t)])
                P.add("dve", lambda e, t=t, ub=ub, o=o, w1c=w1c: e.scalar_tensor_tensor(
                    out=at[t], in0=ub[:, o - 1:o + 511], scalar=w1c, in1=at[t], op0=ALU.mult, op1=ALU.add),
                    reads=[("ub", cc % 2, st), ("ub", cc % 2, st - 1), ("at", t)], writes=[("at", t)])
                P.add("dve", lambda e, t=t, ub=ub, o=o, w0c=w0c: e.scalar_tensor_tensor(
                    out=at[t], in0=ub[:, o - 2:o + 510], scalar=w0c, in1=at[t], op0=ALU.mult, op1=ALU.add),
                    reads=[("ub", cc % 2, st), ("ub", cc % 2, st - 1), ("at", t)], writes=[("at", t)])
                P.add("dve", lambda e, t=t, bbk=bbk, cc=cc, st=st: e.tensor_tensor(
                    out=ycT[:, cc, st * 512:(st + 1) * 512], in0=psb(bbk), in1=at[t], op=ALU.mult),
                    reads=[("ps", bbk), ("at", t)], writes=[("ycT", st)])
            if cc + 2 < 8:
                load_qkv_like(cc % 2, 3, cc + 2)
        if DEBUG_TAPS:
            P.add("sp", lambda e: e.dma_start(out=taps["t_yc"][:, :], in_=ycT.rearrange("p k t -> p (k t)")),
                  reads=[("ycT", st) for st in range(4)], slot="tap")
        if STOP_AFTER == "p2":
            finish()
            return nc

        def load_p3(i, oc):
            wv = wslot(i, 4)
            cs = slice(oc * 128, (oc + 1) * 128)
            P.add("pool", lambda e: e.dma_start(out=wv[:, 0, :, :], in_=w_ap_v[:, :, cs]), writes=[("w", i)],
                  slot=f"w{i}")
            P.add("pool", lambda e: e.dma_start(out=wv[:, 1, :, :], in_=w_cp_v[:, :, cs]), writes=[("w", i, 1)],
                  slot=f"w{i}")
            P.add("pool", lambda e: e.dma_start(out=wv[:, 2, :, :], in_=w_gate_v[:, 0, :, cs]), writes=[("w", i, 2)],
                  slot=f"w{i}")
            P.add("pool", lambda e: e.dma_start(out=wv[:, 3, :, :], in_=w_gate_v[:, 1, :, cs]), writes=[("w", i, 3)],
                  slot=f"w{i}")
        load_p3(0, 0)
        load_p3(1, 1)
        sga = [view(WB + 8 * KB + i * 16 * KB, [128, 512], F32) for i in range(2)]
        sgc = [view(WB + 10 * KB + i * 16 * KB, [128, 512], F32) for i in range(2)]
        m1 = [view(WB + 12 * KB + i * 16 * KB, [128, 512], F32) for i in range(2)]
        m2 = [view(WB + 14 * KB + i * 16 * KB, [128, 512], F32) for i in range(2)]

        k3 = [0]
        WoA = view(WB, [128, 4, D], BF16)
        WoB = view(PH + 152 * KB, [128, 4, D], BF16)
        for oc in range(8):
            ws = oc % 2
            wv = wslot(ws, 4)
            for st in range(4):
                loc = st + 1
                b1, b2, b3, b4 = nb("all"), nb("all"), nb("all"), nb("all")

                for which, bb, src, c0, rd in ((2, b1, hT, loc * 512, [("w", ws, 2), ("hT", loc)]),
                                               (3, b2, hT, loc * 512, [("w", ws, 3), ("hT", loc)]),
                                               (0, b3, yaT, st * 512, [("w", ws)] + ya_res),
                                               (1, b4, ycT, st * 512, [("w", ws, 1), ("ycT", st)])):
                    def mm1(e, wv=wv, which=which, bb=bb, src=src, c0=c0):
                        ins = None
                        for kt in range(8):
                            ins = e.matmul(psb(bb), wv[:, which, kt, :], src[:, kt, c0:c0 + 512],
                                           start=(kt == 0), stop=(kt == 7))
                        return ins
                    P.add("pe", mm1, reads=rd, writes=[("ps", bb)])
                t = k3[0] % 2
                k3[0] += 1
                P.add("act", lambda e, t=t, b1=b1, oc=oc: e.activation(out=sga[t], in_=psb(b1), func=AF.Sigmoid,
                                                                       bias=bgt[:, oc:oc + 1]),
                      reads=[("ps", b1), "cst"], writes=[("sga", t)])
                P.add("act", lambda e, t=t, b2=b2, oc=oc: e.activation(out=sgc[t], in_=psb(b2), func=AF.Sigmoid,
                                                                       bias=bgt[:, 8 + oc:9 + oc]),
                      reads=[("ps", b2), "cst"], writes=[("sgc", t)])
                P.add("dve", lambda e, t=t, b3=b3: e.tensor_tensor(out=m1[t], in0=psb(b3), in1=sga[t], op=ALU.mult),
                      reads=[("ps", b3), ("sga", t)], writes=[("m1", t)])
                P.add("dve", lambda e, t=t, b4=b4: e.tensor_tensor(out=m2[t], in0=psb(b4), in1=sgc[t], op=ALU.mult),
                      reads=[("ps", b4), ("sgc", t)], writes=[("m2", t)])
                P.add("pool", lambda e, t=t, oc=oc, st=st: e.tensor_tensor(
                    out=mT[:, oc, st * 512:(st + 1) * 512], in0=m1[t], in1=m2[t], op=ALU.add),
                    reads=[("m1", t), ("m2", t)], writes=[("mT", st)])
            if oc + 2 < 8:
                load_p3(oc % 2, oc + 2)
            elif oc == 6:
                P.add("pool", lambda e: e.dma_start(out=WoA, in_=w_out_v[:, 0:4, :]),
                      writes=[("w", 0), ("w", 0, 1), ("w", 0, 2), ("w", 0, 3)], slot="w0")
                P.add("pool", lambda e: e.dma_start(out=WoB, in_=w_out_v[:, 4:8, :]),
                      writes=["wo2"] + p2_tokens, slot="wd0")
        if DEBUG_TAPS:
            P.add("sp", lambda e: e.dma_start(out=taps["t_m"][:, :], in_=mT.rearrange("p k t -> p (k t)")),
                  reads=[("mT", st) for st in range(4)], slot="tap")
        if STOP_AFTER == "p3a":
            finish()
            return nc

        P.barrier()
        P.add("sp", lambda e: e.dma_start(out=gbb, in_=g2b_d[:, :]), writes=["gbb"], slot="const")
        def wu_slot(i):
            return view(WB + i * 16 * KB, [128, 8, 512], BF16)

        def wd_slot(i):
            return view(WB + i * 16 * KB + 8 * KB, [128, 4, D], BF16)

        def load_wu(i, grp):
            wu = wu_slot(i)
            P.add("pool", lambda e: e.dma_start(out=wu, in_=w_up_v[:, :, grp * 512:(grp + 1) * 512]),
                  writes=[("w", i)], slot=f"w{i}")

        def load_wd(i, grp):
            wd = wd_slot(i)
            P.add("pool", lambda e: e.dma_start(out=wd, in_=w_down_v[:, grp * 4:(grp + 1) * 4, :]),
                  writes=[("w", i, 1)], slot=f"wd{i}")

        def load_p4(i, grp):
            load_wu(i, grp)
            load_wd(i, grp)
        load_p4(1, 0)
        p3b = Bump(PH + 96 * KB, PH + 104 * KB)
        xr = [p3b([128, D], F32) for _ in range(2)]
        p3c = Bump(PH + 136 * KB, ARENA)
        h2b = [p3c([128, D], BF16) for _ in range(2)]
        junk2 = p3c([128, D], BF16)
        def p3b_A(tt):
            s = tt % 2
            P.add("sp", lambda e: e.dma_start(out=xr[s], in_=xh[HALO + tt * 128:HALO + (tt + 1) * 128, :]),
                  writes=[("xr", s)], slot=f"xt{s}")
            for half in range(2):
                b = nb("all")

                def mmo(e, half=half, b=b):
                    ins = None
                    for kt in range(8):
                        wsrc = WoA[:, kt, :] if kt < 4 else WoB[:, kt - 4, :]
                        ins = e.matmul(psb(b), mT[:, kt, tt * 128:(tt + 1) * 128], wsrc[:, half * 512:(half + 1) * 512],
                                       start=(kt == 0), stop=(kt == 7))
                    return ins
                P.add("pe", mmo, reads=[("w", 0), "wo2", ("mT", tt // 4)], writes=[("ps", b)])
                P.add("dve", lambda e, half=half, b=b: e.tensor_tensor(
                    out=x1[:, tt, half * 512:(half + 1) * 512], in0=psb(b), in1=xr[s][:, half * 512:(half + 1) * 512],
                    op=ALU.add), reads=[("ps", b), ("xr", s)], writes=[("x1", tt, half)])
            P.add("act", lambda e: e.activation(out=junk2, in_=x1[:, tt, :], func=AF.Square, scale=1.0 / 32.0,
                                                accum_out=ms2[:, tt:tt + 1]),
                  reads=[("x1", tt, 0), ("x1", tt, 1), "ms_init", "junk2"], writes=[("ms2", tt), "junk2"])

        def p3b_A2(tt):
            rsqrt_act(rstd2[:, tt:tt + 1], ms2[:, tt:tt + 1], lnt[:, tt:tt + 1], [("ms2", tt)], [("rstd2", tt)],
                      ("lnt2", tt))

        trb = {}

        def p3b_B1(tt):
            s = tt % 2
            P.add("dve", lambda e: e.scalar_tensor_tensor(
                out=h2b[s], in0=x1[:, tt, :], scalar=rstd2[:, tt:tt + 1], in1=gbb, op0=ALU.mult, op1=ALU.mult),
                reads=[("x1", tt, 0), ("x1", tt, 1), ("rstd2", tt), "gbb"], writes=[("h2b", s)])

        def p3b_B2(tt):
            s = tt % 2
            b = nb("all")
            trb[tt] = b

            def tr2(e):
                pbf = psb(b).bitcast(BF16)
                ins = None
                for kt in range(8):
                    ins = e.transpose(out=pbf[:, kt * 128:(kt + 1) * 128], in_=h2b[s][:, kt * 128:(kt + 1) * 128],
                                      identity=ident)
                return ins
            P.add("pe", tr2, reads=[("h2b", s), "cmat"], writes=[("ps", b)])

        def p3b_C(tt):
            b = trb[tt]
            P.add("dve", lambda e: e.tensor_copy(
                out=h2T[:, :, tt * 128:(tt + 1) * 128],
                in_=psb(b).bitcast(BF16).rearrange("p (k t) -> p k t", k=8)),
                reads=[("ps", b)], writes=[("h2T", tt // 4)])
        for i in range(16 + 4):
            if 0 <= i - 3 < 16:
                p3b_B1(i - 3)
            if i < 16:
                p3b_A(i)
            if 0 <= i - 1 < 16:
                p3b_A2(i - 1)
            if 0 <= i - 3 < 16:
                p3b_B2(i - 3)
            if 0 <= i - 4 < 16:
                p3b_C(i - 4)
        load_p4(0, 1)
        x1_res = [("x1", tt, half) for tt in range(16) for half in range(2)]
        if DEBUG_TAPS:
            P.add("sp", lambda e: e.dma_start(out=taps["t_x1"][:, :], in_=x1.rearrange("p k t -> p (k t)")),
                  reads=x1_res, slot="tap")
            P.add("sp", lambda e: e.dma_start(out=taps["t_h2"][:, :], in_=h2T.rearrange("p k t -> p (k t)")),
                  reads=[("h2T", st) for st in range(4)], slot="tap")
        if STOP_AFTER == "p3b":
            finish()
            return nc

        P.barrier()
        p4 = Bump(PH + 104 * KB, ARENA)
        uT = [p4([128, 4, NTOK], BF16) for _ in range(2)]
        rl = [p4([128, 512], F32) for _ in range(3)]

        k4 = [0]

        def up_phase(grp):
            ws = grp % 2
            wl = (grp + 1) % 2
            wu = wu_slot(wl)
            for fc in range(4):
                for st in range(4):
                    b = nb("all")

                    def mmu(e, wu=wu, fc=fc, st=st, b=b):
                        ins = None
                        for kt in range(8):
                            ins = e.matmul(psb(b), wu[:, kt, fc * 128:(fc + 1) * 128], h2T[:, kt, st * 512:(st + 1) * 512],
                                           start=(kt == 0), stop=(kt == 7))
                        return ins
                    P.add("pe", mmu, reads=[("w", wl), ("h2T", st)], writes=[("ps", b)])
                    t = k4[0] % 3
                    k4[0] += 1
                    P.add("act", lambda e, t=t, b=b: e.activation(out=rl[t], in_=psb(b), func=AF.Relu),
                          reads=[("ps", b)], writes=[("rl", t)])
                    P.add("pool", lambda e, t=t, ws=ws, fc=fc, st=st: e.tensor_tensor(
                        out=uT[ws][:, fc, st * 512:(st + 1) * 512], in0=rl[t], in1=rl[t], op=ALU.mult),
                        reads=[("rl", t)], writes=[("uT", ws, st)])

        def down_phase(grp):
            ws = grp % 2
            wl = (grp + 1) % 2
            wd = wd_slot(wl)
            for tt in range(16):
                for half in range(2):
                    b = nb("all")

                    def mmd(e, wd=wd, ws=ws, tt=tt, half=half, b=b):
                        ins = None
                        for fc in range(4):
                            ins = e.matmul(psb(b), uT[ws][:, fc, tt * 128:(tt + 1) * 128],
                                           wd[:, fc, half * 512:(half + 1) * 512], start=(fc == 0), stop=(fc == 3))
                        return ins
                    P.add("pe", mmd, reads=[("w", wl, 1), ("uT", ws, tt // 4)], writes=[("ps", b)])
                    P.add("dve", lambda e, tt=tt, half=half, b=b: e.tensor_tensor(
                        out=x1[:, tt, half * 512:(half + 1) * 512], in0=psb(b), in1=x1[:, tt, half * 512:(half + 1) * 512],
                        op=ALU.add), reads=[("ps", b), ("x1", tt, half)], writes=[("x1", tt, half)])
                if grp == 7:
                    P.add("sp", lambda e, tt=tt: e.dma_start(out=y[tt * 128:(tt + 1) * 128, :], in_=x1[:, tt, :]),
                          reads=[("x1", tt, 0), ("x1", tt, 1)], slot="out")
            if grp + 2 < 8:
                load_wd(wl, grp + 2)

        up_phase(0)
        for grp in range(8):
            if grp + 1 < 8:
                up_phase(grp + 1)
            if grp + 2 < 8:
                load_wu((grp + 1) % 2, grp + 2)
            down_phase(grp)
        finish()
    return nc


_CACHE = {}


def _host_consts(q_norm_g, k_norm_g, rel_bias, conv_w, conv_b, b_gate, norm1_g, norm2_g):
    cst = np.zeros((8, 128, 72), np.float32)
    p = np.arange(128)
    cst[:, :, 0] = q_norm_g[p % 64]
    cst[:, :, 1] = k_norm_g[p % 64]
    cwv = conv_w.reshape(3, 8, 128)
    cst[:, :, 2:26] = np.transpose(cwv, (2, 1, 0)).reshape(128, 24)
    cst[:, :, 26:34] = conv_b.reshape(8, 128).T
    cst[:, :, 34:50] = b_gate.reshape(16, 128).T
    cst[:, :, 50:70] = 1.0
    for c in range(8):
        if c % 2 == 0:
            cst[c, :, 50:54] = 0.0
    cmat = np.zeros((128, 256), np.float32)
    cmat[:, 0:128] = np.eye(128, dtype=np.float32)
    cmat[:, 128:256] = (p[:, None] // 64 == p[None, :] // 64).astype(np.float32) / 64.0
    kk = np.arange(128)[:, None, None]
    j = np.arange(5)[None, :, None]
    qq = np.arange(128)[None, None, :]
    dist = 512 - 128 * j + qq - kk
    idx = np.clip(dist, -256, 256) + 256
    bias = rel_bias[:, idx]
    cdiff = (8 - 2 * j + qq // 64) - (kk // 64)
    valid = (cdiff >= 0) & (cdiff <= 8)
    bias = np.where(valid[None], bias, np.float32(-1e30)).astype(np.float32)
    biasT = np.ascontiguousarray(np.transpose(bias, (1, 0, 2, 3))).reshape(128, 16 * 640)
    g1b = np.ascontiguousarray(np.broadcast_to(norm1_g[None, :], (128, D))).astype(np.float32)
    g2b = np.ascontiguousarray(np.broadcast_to(norm2_g[None, :], (128, D))).astype(np.float32)
    return cst, cmat, biasT, g1b, g2b


def kernel(x, norm1_g, w_in, q_norm_g, k_norm_g, rel_bias, conv_w, conv_b, w_attn_proj, w_conv_proj,
           w_gate, b_gate, w_out, norm2_g, w_up, w_down):
    f = lambda a: np.ascontiguousarray(np.asarray(a, dtype=np.float32))
    x = f(x)
    B, S, _ = x.shape
    cst, cmat, biasT, g1b, g2b = _host_consts(f(q_norm_g), f(k_norm_g), f(rel_bias), f(conv_w), f(conv_b),
                                              f(b_gate), f(norm1_g), f(norm2_g))
    if "nc" not in _CACHE:
        _CACHE["nc"] = build_program()
    nc = _CACHE["nc"]
    shared = {"w_in": f(w_in), "w_gate": f(w_gate), "w_ap": f(w_attn_proj), "w_cp": f(w_conv_proj),
              "w_out": f(w_out), "w_up": f(w_up), "w_down": f(w_down), "g1b": g1b, "g2b": g2b, "cmat": cmat,
              "biasT": biasT}
    in_maps = []
    for c in range(N_RUN):
        b, half = c // 2, c % 2
        xh = np.zeros((NLOC, D), np.float32)
        if half == 0:
            xh[HALO:] = x[b, 0:NTOK]
        else:
            xh[:] = x[b, NTOK - HALO:2 * NTOK]
        m = dict(shared)
        m["xh"] = xh
        m["cst"] = cst[c]
        in_maps.append(m)
    res = run_bass_kernel_spmd(nc, in_maps, core_ids=list(range(N_RUN)))
    _CACHE["last"] = res
    out = np.zeros((B, S, D), np.float32)
    for c in range(N_RUN):
        b, half = c // 2, c % 2
        out[b, half * NTOK:(half + 1) * NTOK] = res.results[c]["y"]
    return out
```
